# Optimizing a Trainium2 kernel written in Bass

```python
import math
import jax, jax.numpy as jnp
from jax import lax
import numpy as np

D_MODEL = 2048
BATCH = 2
SEQ = 4096
DEPTH = 2

CTX_LEN = 256
GRID_W = 64
MIX_WIDTH = D_MODEL
N_MIXERS = 4
GROUP_W = MIX_WIDTH // N_MIXERS
S5_CH = 16
S5_GROUPS = GROUP_W // S5_CH
S5_STATE = 64
S5_DT_MIN = 1e-3
S5_DT_MAX = 1e-1
SG_HEADS = 4
SG_HEAD_DIM = GROUP_W // SG_HEADS
SG_CHUNK = 128
POOL_WINDOWS = (2, 4, 8, 16)
POOL_DIM = GROUP_W // len(POOL_WINDOWS)
M2_HEAD_DIM = 64
M2_HEADS = GROUP_W // M2_HEAD_DIM
M2_STATE = 128
M2_GROUPS = 2
M2_CONV = 4
M2_PAD = (M2_CONV // 2, M2_CONV - 1 - M2_CONV // 2)
M2_CHUNK = 128
M2_XBC = GROUP_W + 2 * M2_GROUPS * M2_STATE
M2_DT_MIN = 1e-3
M2_DT_MAX = 1e-1
IN_SIZES = (GROUP_W, GROUP_W, GROUP_W, GROUP_W, GROUP_W, M2_XBC, 2 * M2_HEADS)
IN_WIDTH = 5 * GROUP_W + M2_XBC + 2 * M2_HEADS
FFN_HIDDEN = ((8 * D_MODEL // 3 + 255) // 256) * 256
FFN_CONV = 3
DEEPNORM_ALPHA = (2 * DEPTH) ** 0.25
DEEPNORM_BETA = (8 * DEPTH) ** -0.25
LN_EPS = 1e-5
RMS_EPS = 1e-5

kernel_name = "hybrid_parallel_s5_gmlp_pool_ssd_dit"


def layer_norm(x, g, b):
    xf = x.astype(jnp.float32)
    mu = jnp.mean(xf, -1, keepdims=True)
    var = jnp.mean(jnp.square(xf - mu), -1, keepdims=True)
    return ((xf - mu) * lax.rsqrt(var + LN_EPS) * g + b).astype(x.dtype)


def modulate(x, shift, scale):
    return x * (1 + scale) + shift


def split_in(p):
    offs, o = [], 0
    for s in IN_SIZES[:-1]:
        o += s
        offs.append(o)
    return jnp.split(p, offs, axis=-1)


def dw_conv1d(x, w, b, pad):
    y = lax.conv_general_dilated(x, w[:, None, :].astype(x.dtype), window_strides=(1,), padding=[pad],
                                 dimension_numbers=('NWC', 'WIO', 'NWC'), feature_group_count=x.shape[-1])
    return y + b.astype(x.dtype)


def _lin_combine(e1, e2):
    a1, b1 = e1
    a2, b2 = e2
    return a2 * a1, a2 * b1 + b2


def s5_states(u, a_bar, b_bar, h0, reverse):
    bu = jnp.einsum('blgh,gph->blgp', u.astype(jnp.complex64), b_bar)
    a = jnp.broadcast_to(a_bar, (1, u.shape[1]) + a_bar.shape)
    a_cum, h = lax.associative_scan(_lin_combine, (a, bu), reverse=reverse, axis=1)
    return h + a_cum * h0[:, None]


def s5_mixer(u_ctx, u_lat, a_re, a_im, b_re, b_im, c_re, c_im, log_step, d, glu_w, glu_b, need_ctx):
    f32 = jnp.float32

    def groups(u):
        return u.astype(f32).reshape(u.shape[0], u.shape[1], S5_GROUPS, S5_CH)

    uc, ul = groups(u_ctx), groups(u_lat)
    ys_ctx, ys_lat = [], []
    for direction, reverse in enumerate((False, True)):
        lam = lax.complex(a_re[direction].astype(f32), a_im[direction].astype(f32))
        step = jnp.exp(log_step[direction].astype(f32))[:, None]
        a_bar = jnp.exp(lam * step)
        b_bar = ((a_bar - 1.0) / lam)[..., None] * lax.complex(b_re[direction].astype(f32),
                                                               b_im[direction].astype(f32))
        c_mat = lax.complex(c_re[direction].astype(f32), c_im[direction].astype(f32))
        h0 = jnp.zeros((uc.shape[0], S5_GROUPS, S5_STATE), jnp.complex64)
        hc = s5_states(uc, a_bar, b_bar, h0, reverse)
        hc_final = hc[:, 0] if reverse else hc[:, -1]
        hl = s5_states(ul, a_bar, b_bar, hc_final, reverse)
        ys_lat.append(jnp.einsum('blgp,ghp->blgh', hl, c_mat).real)
        if need_ctx:
            ys_ctx.append(jnp.einsum('blgp,ghp->blgh', hc, c_mat).real)

    def finish(ys, u, u_orig):
        y = (ys[0] + ys[1] + d.astype(f32).reshape(S5_GROUPS, S5_CH) * u).reshape(u_orig.shape)
        z = jax.nn.gelu(y)
        return (z * jax.nn.sigmoid(z @ glu_w.astype(f32) + glu_b.astype(f32))).astype(u_orig.dtype)

    y_lat = finish(ys_lat, ul, u_lat)
    y_ctx = finish(ys_ctx, uc, u_ctx) if need_ctx else None
    return y_ctx, y_lat


def chunk_gating(u, v, ln_g, ln_b, w_s, b_s):
    bsz, l, _ = u.shape
    u = jax.nn.gelu(u)
    v = jax.nn.gelu(v).reshape(bsz, l // SG_CHUNK, SG_CHUNK, SG_HEADS, SG_HEAD_DIM)
    v = layer_norm(v, ln_g.reshape(SG_HEADS, SG_HEAD_DIM), ln_b.reshape(SG_HEADS, SG_HEAD_DIM))
    s = jnp.einsum('hij,bcjhd->bcihd', w_s, v) + b_s.T[None, None, :, :, None]
    return u * s.reshape(bsz, l, GROUP_W)


def pool_mixer(p, w, bias, scale):
    bsz, l, _ = p.shape
    pf = p.astype(jnp.float32)
    csum = jnp.pad(jnp.cumsum(pf, axis=1), ((0, 0), (1, 0), (0, 0)))
    t = jnp.arange(l)
    outs = []
    for g, win in enumerate(POOL_WINDOWS):
        lo = jnp.clip(t - win // 2, 0, l - 1)
        hi = jnp.clip(t + win // 2 - 1, 0, l - 1)
        sl = slice(g * POOL_DIM, (g + 1) * POOL_DIM)
        cs = csum[..., sl]
        total = jnp.take(cs, hi + 1, axis=1) - jnp.take(cs, lo, axis=1)
        mean = total / (hi - lo + 1).astype(jnp.float32)[None, :, None]
        outs.append(jnp.einsum('blc,cd->bld', mean - pf[..., sl], w[g]))
    y = jnp.concatenate(outs, axis=-1) + bias
    return (y * scale).astype(p.dtype)


def m2_prepare(xbc, dt_raw, conv_w, conv_b, dt_bias):
    f32 = jnp.float32
    xbc = jax.nn.silu(dw_conv1d(xbc, conv_w, conv_b, M2_PAD)).astype(f32)
    bsz, l, _ = xbc.shape
    n_bc = M2_GROUPS * M2_STATE
    rep = M2_HEADS // M2_GROUPS
    xs = xbc[..., :GROUP_W].reshape(bsz, l, M2_HEADS, M2_HEAD_DIM)
    bm = jnp.repeat(xbc[..., GROUP_W:GROUP_W + n_bc].reshape(bsz, l, M2_GROUPS, M2_STATE), rep, axis=2)
    cm = jnp.repeat(xbc[..., GROUP_W + n_bc:].reshape(bsz, l, M2_GROUPS, M2_STATE), rep, axis=2)
    dt = jax.nn.softplus(dt_raw.astype(f32).reshape(bsz, l, 2, M2_HEADS) + dt_bias.astype(f32))
    return xs, bm, cm, dt


def ssd_scan(xs, dt, a, bm, cm, h0, need_y):
    bsz, l, nh, hp = xs.shape
    nc = l // M2_CHUNK

    def chunks(t):
        return t.reshape((bsz, nc, M2_CHUNK) + t.shape[2:])

    xd = chunks(xs * dt[..., None])
    bc, cc = chunks(bm), chunks(cm)
    a_cum = jnp.cumsum(chunks(dt * a), axis=2)
    a_tot = a_cum[:, :, -1]
    decay_end = jnp.exp(a_tot[:, :, None] - a_cum)
    chunk_states = jnp.einsum('bcqhn,bcqh,bcqhp->bchpn', bc, decay_end, xd)

    def step(s, inp):
        tot, st = inp
        return jnp.exp(tot)[..., None, None] * s + st, s

    h_final, h_prev = lax.scan(step, h0, (jnp.moveaxis(a_tot, 1, 0), jnp.moveaxis(chunk_states, 1, 0)))
    if not need_y:
        return None, h_final
    h_prev = jnp.moveaxis(h_prev, 0, 1)
    seg = a_cum[:, :, :, None, :] - a_cum[:, :, None, :, :]
    lower = jnp.tril(jnp.ones((M2_CHUNK, M2_CHUNK), bool))[None, None, :, :, None]
    decay = jnp.exp(jnp.where(lower, seg, -jnp.inf))
    scores = jnp.einsum('bcihn,bcjhn->bcijh', cc, bc) * decay
    y = (jnp.einsum('bcijh,bcjhp->bcihp', scores, xd)
         + jnp.einsum('bcihn,bchpn->bcihp', cc, h_prev) * jnp.exp(a_cum)[..., None])
    return y.reshape(bsz, l, nh, hp), h_final


def ssd_direction(inputs, direction, a, h0, need_y):
    xs, bm, cm, dt = inputs
    dt = dt[:, :, direction]
    if direction == 1:
        xs, bm, cm, dt = (jnp.flip(t, 1) for t in (xs, bm, cm, dt))
    y, h_final = ssd_scan(xs, dt, a, bm, cm, h0, need_y)
    if direction == 1 and y is not None:
        y = jnp.flip(y, 1)
    return y, h_final


def gated_rmsnorm(y, z, w):
    bsz, l, _ = z.shape
    g = (y * jax.nn.silu(z.astype(jnp.float32))).reshape(bsz, l, M2_GROUPS, GROUP_W // M2_GROUPS)
    g = g * lax.rsqrt(jnp.mean(jnp.square(g), -1, keepdims=True) + RMS_EPS)
    return (g.reshape(bsz, l, GROUP_W) * w.astype(jnp.float32)).astype(z.dtype)


def mamba2_mixer(z_ctx, xbc_ctx, dt_ctx, z_lat, xbc_lat, dt_lat, conv_w, conv_b, dt_bias, a_log, d,
                 norm_w, need_ctx):
    ctx_in = m2_prepare(xbc_ctx, dt_ctx, conv_w, conv_b, dt_bias)
    lat_in = m2_prepare(xbc_lat, dt_lat, conv_w, conv_b, dt_bias)
    a = -jnp.exp(a_log.astype(jnp.float32))
    ys_ctx, ys_lat = [], []
    for direction in range(2):
        h0 = jnp.zeros((ctx_in[0].shape[0], M2_HEADS, M2_HEAD_DIM, M2_STATE), jnp.float32)
        y_c, h_c = ssd_direction(ctx_in, direction, a[direction], h0, need_ctx)
        y_l, _ = ssd_direction(lat_in, direction, a[direction], h_c, True)
        ys_lat.append(y_l)
        if need_ctx:
            ys_ctx.append(y_c)
    d_h = d.astype(jnp.float32)[None, None, :, None]

    def finish(ys, xs, z):
        y = (ys[0] + ys[1] + d_h * xs).reshape(z.shape[0], z.shape[1], GROUP_W)
        return gated_rmsnorm(y, z, norm_w)

    y_lat = finish(ys_lat, lat_in[0], z_lat)
    y_ctx = finish(ys_ctx, ctx_in[0], z_ctx) if need_ctx else None
    return y_ctx, y_lat


def conv_ffn(h, w_up, conv_w, conv_b, w_down, rows):
    gate, val = jnp.split(h @ w_up, 2, axis=-1)
    bsz, l, f = gate.shape
    if rows is None:
        gate = dw_conv1d(gate, conv_w[FFN_CONV // 2], conv_b, (FFN_CONV // 2, FFN_CONV // 2))
    else:
        g2 = gate.reshape(bsz, rows, GRID_W, f)
        g2 = lax.conv_general_dilated(g2, conv_w[:, :, None, :].astype(g2.dtype), (1, 1), 'SAME',
                                      dimension_numbers=('NHWC', 'HWIO', 'NHWC'), feature_group_count=f)
        gate = g2.reshape(bsz, l, f) + conv_b.astype(gate.dtype)
    return (jax.nn.gelu(gate) * val) @ w_down


def setup_inputs(seed: int = 0) -> dict:
    key = jax.random.key(seed)
    ks = iter(jax.random.split(key, 64))
    f32 = jnp.float32
    L, D, F = DEPTH, D_MODEL, FFN_HIDDEN

    def nrm(shape, s):
        return s * jax.random.normal(next(ks), shape, f32)

    def unif(shape, lo, hi):
        return jax.random.uniform(next(ks), shape, f32, lo, hi)

    n_idx = jnp.arange(S5_STATE, dtype=f32)
    m2_dt = jnp.exp(unif((L, 2, M2_HEADS), math.log(M2_DT_MIN), math.log(M2_DT_MAX)))
    return {
        'x': nrm((BATCH, SEQ, D), 1.0),
        'c': nrm((BATCH, D), 1.0),
        'ctx': nrm((BATCH, CTX_LEN, D), 1.0),
        'c_ctx': nrm((D,), 1.0),
        'w_ada': nrm((L, D, 6 * D), 0.5 * D ** -0.5),
        'b_ada': nrm((L, 6 * D), 0.01),
        'w_in': nrm((L, D, IN_WIDTH), D ** -0.5),
        'w_out': nrm((L, MIX_WIDTH, D), DEEPNORM_BETA * MIX_WIDTH ** -0.5),
        'ln1_g': 1.0 + nrm((L, D), 0.02),
        'ln1_b': nrm((L, D), 0.02),
        'ln2_g': 1.0 + nrm((L, D), 0.02),
        'ln2_b': nrm((L, D), 0.02),
        's5_a_re': -0.5 + nrm((L, 2, S5_GROUPS, S5_STATE), 0.01),
        's5_a_im': math.pi * n_idx + nrm((L, 2, S5_GROUPS, S5_STATE), 0.01),
        's5_b_re': nrm((L, 2, S5_GROUPS, S5_STATE, S5_CH), (2 * S5_CH) ** -0.5),
        's5_b_im': nrm((L, 2, S5_GROUPS, S5_STATE, S5_CH), (2 * S5_CH) ** -0.5),
        's5_c_re': nrm((L, 2, S5_GROUPS, S5_CH, S5_STATE), 0.5),
        's5_c_im': nrm((L, 2, S5_GROUPS, S5_CH, S5_STATE), 0.5),
        's5_log_step': unif((L, 2, S5_GROUPS), math.log(S5_DT_MIN), math.log(S5_DT_MAX)),
        's5_d': nrm((L, GROUP_W), 1.0),
        's5_glu_w': nrm((L, GROUP_W, GROUP_W), GROUP_W ** -0.5),
        's5_glu_b': nrm((L, GROUP_W), 0.01),
        'sg_ln_g': 1.0 + nrm((L, GROUP_W), 0.02),
        'sg_ln_b': nrm((L, GROUP_W), 0.02),
        'sg_w': nrm((L, SG_HEADS, SG_CHUNK, SG_CHUNK), SG_CHUNK ** -0.5),
        'sg_b': 1.0 + nrm((L, SG_HEADS, SG_CHUNK), 0.01),
        'pool_w': nrm((L, len(POOL_WINDOWS), POOL_DIM, POOL_DIM), POOL_DIM ** -0.5),
        'pool_b': nrm((L, GROUP_W), 0.01),
        'pool_scale': 1.0 + nrm((L, GROUP_W), 0.02),
        'm2_conv_w': nrm((L, M2_CONV, M2_XBC), M2_CONV ** -0.5),
        'm2_conv_b': nrm((L, M2_XBC), 0.01),
        'm2_dt_bias': m2_dt + jnp.log(-jnp.expm1(-m2_dt)),
        'm2_a_log': jnp.log(unif((L, 2, M2_HEADS), 1.0, 16.0)),
        'm2_d': 1.0 + nrm((L, M2_HEADS), 0.01),
        'm2_norm_w': 1.0 + nrm((L, GROUP_W), 0.02),
        'ffn_w_up': nrm((L, D, 2 * F), D ** -0.5),
        'ffn_conv_w': nrm((L, FFN_CONV, FFN_CONV, F), 1.0 / FFN_CONV),
        'ffn_conv_b': nrm((L, F), 0.01),
        'ffn_w_down': nrm((L, F, D), DEEPNORM_BETA * F ** -0.5),
    }


def reference(x, c, ctx, c_ctx, w_ada, b_ada, w_in, w_out, ln1_g, ln1_b, ln2_g, ln2_b,
              s5_a_re, s5_a_im, s5_b_re, s5_b_im, s5_c_re, s5_c_im, s5_log_step, s5_d, s5_glu_w, s5_glu_b,
              sg_ln_g, sg_ln_b, sg_w, sg_b, pool_w, pool_b, pool_scale,
              m2_conv_w, m2_conv_b, m2_dt_bias, m2_a_log, m2_d, m2_norm_w,
              ffn_w_up, ffn_conv_w, ffn_conv_b, ffn_w_down):
    h_lat, h_ctx = x, ctx
    rows = h_lat.shape[1] // GRID_W
    silu_c = jax.nn.silu(c)[:, None, :]
    silu_cc = jax.nn.silu(c_ctx)[None, None, :]
    for i in range(DEPTH):
        need_ctx = i < DEPTH - 1
        m_lat = jnp.split(silu_c @ w_ada[i] + b_ada[i], 6, axis=-1)
        m_ctx = jnp.split(silu_cc @ w_ada[i] + b_ada[i], 6, axis=-1)

        pl = split_in(modulate(h_lat, m_lat[0], m_lat[1]) @ w_in[i])
        pc = split_in(modulate(h_ctx, m_ctx[0], m_ctx[1]) @ w_in[i])
        ya_ctx, ya_lat = s5_mixer(pc[0], pl[0], s5_a_re[i], s5_a_im[i], s5_b_re[i], s5_b_im[i],
                                  s5_c_re[i], s5_c_im[i], s5_log_step[i], s5_d[i], s5_glu_w[i],
                                  s5_glu_b[i], need_ctx)
        yd_ctx, yd_lat = mamba2_mixer(pc[4], pc[5], pc[6], pl[4], pl[5], pl[6], m2_conv_w[i], m2_conv_b[i],
                                      m2_dt_bias[i], m2_a_log[i], m2_d[i], m2_norm_w[i], need_ctx)
        mix_lat = jnp.concatenate([
            ya_lat,
            chunk_gating(pl[1], pl[2], sg_ln_g[i], sg_ln_b[i], sg_w[i], sg_b[i]),
            pool_mixer(pl[3], pool_w[i], pool_b[i], pool_scale[i]),
            yd_lat], axis=-1) @ w_out[i]
        h_lat = layer_norm(DEEPNORM_ALPHA * h_lat + m_lat[2] * mix_lat, ln1_g[i], ln1_b[i])
        if need_ctx:
            mix_ctx = jnp.concatenate([
                ya_ctx,
                chunk_gating(pc[1], pc[2], sg_ln_g[i], sg_ln_b[i], sg_w[i], sg_b[i]),
                pool_mixer(pc[3], pool_w[i], pool_b[i], pool_scale[i]),
                yd_ctx], axis=-1) @ w_out[i]
            h_ctx = layer_norm(DEEPNORM_ALPHA * h_ctx + m_ctx[2] * mix_ctx, ln1_g[i], ln1_b[i])

        f_lat = conv_ffn(modulate(h_lat, m_lat[3], m_lat[4]), ffn_w_up[i], ffn_conv_w[i], ffn_conv_b[i],
                         ffn_w_down[i], rows)
        h_lat = layer_norm(DEEPNORM_ALPHA * h_lat + m_lat[5] * f_lat, ln2_g[i], ln2_b[i])
        if need_ctx:
            f_ctx = conv_ffn(modulate(h_ctx, m_ctx[3], m_ctx[4]), ffn_w_up[i], ffn_conv_w[i], ffn_conv_b[i],
                             ffn_w_down[i], None)
            h_ctx = layer_norm(DEEPNORM_ALPHA * h_ctx + m_ctx[5] * f_ctx, ln2_g[i], ln2_b[i])
    return h_lat
```

```python
import math
from contextlib import ExitStack
import numpy as np
import concourse.bass as bass
import concourse.mybir as mybir
from concourse.bass_utils import run_bass_kernel_spmd

D = 2048
KT = 16
CTX = 256
LAT = 1024
HAL = 8
NT = CTX + 2 * HAL + LAT
C_CTX = (0, 256)
C_HL = (256, 264)
C_HR = (264, 272)
C_LAT = (272, 1296)
NR = CTX + LAT
INW = 3600
NMT = 29
FH = 5632
NFT = 44
ALPHA = 4 ** 0.25
LN_EPS = 1e-5
NL = 2
TWO_PI = 2.0 * math.pi
BIGW = ("w_ada", "w_in", "w_out", "ffn_w_up", "ffn_w_down")


def host_prep(inp):
    f = np.float32
    L = NL
    com = {}
    for k in ("s5_glu_w", "pool_w"):
        com[k] = np.ascontiguousarray(inp[k], dtype=f)
    com["b_ada_l"] = np.ascontiguousarray(inp["b_ada"].reshape(L, 96, 128).transpose(0, 2, 1))
    lnp = np.stack([inp[k].reshape(L, 16, 128).transpose(0, 2, 1) for k in ("ln1_g", "ln1_b", "ln2_g", "ln2_b")], 2)
    com["ln_p"] = np.ascontiguousarray(lnp)
    s5p = np.zeros((L, 2, 128, 3, 16), f)
    for l in range(L):
        for d in range(2):
            s5p[l, d, :, 0, :] = inp["s5_a_re"][l, d].reshape(16, 128).T
            s5p[l, d, :, 1, :] = inp["s5_a_im"][l, d].reshape(16, 128).T
            s5p[l, d, :, 2, :] = np.repeat(inp["s5_log_step"][l, d], 64).reshape(16, 128).T
    com["s5p"] = s5p
    BT = np.zeros((L, 2, 2, 128, 4, 2, 128), f)
    CT = np.zeros((L, 2, 2, 128, 16, 64), f)
    for l in range(L):
        for d in range(2):
            for ri, (bk, ck) in enumerate((("s5_b_re", "s5_c_re"), ("s5_b_im", "s5_c_im"))):
                Bm = inp[bk][l, d]
                Cm = inp[ck][l, d]
                for t in range(16):
                    q, r = divmod(t, 4)
                    rb, e = divmod(r, 2)
                    for gi in range(2):
                        g = 2 * t + gi
                        p0 = 64 * rb + 32 * e + gi * 16
                        BT[l, d, ri, p0:p0 + 16, q, e, gi * 64:(gi + 1) * 64] = Bm[g].T
                        CT[l, d, ri, gi * 64:(gi + 1) * 64, t, 32 * e + gi * 16:32 * e + gi * 16 + 16] = Cm[g].T
    com["s5BT"] = BT
    com["s5CT"] = CT
    com["s5d"] = np.ascontiguousarray(inp["s5_d"].reshape(L, 4, 128).transpose(0, 2, 1))
    com["s5glub"] = np.ascontiguousarray(inp["s5_glu_b"].reshape(L, 4, 128).transpose(0, 2, 1))
    com["sglnG"] = np.ascontiguousarray(np.broadcast_to(inp["sg_ln_g"].reshape(L, 1, 4, 128), (L, 128, 4, 128)))
    com["sglnB"] = np.ascontiguousarray(np.broadcast_to(inp["sg_ln_b"].reshape(L, 1, 4, 128), (L, 128, 4, 128)))
    com["sgwT"] = np.ascontiguousarray(inp["sg_w"].transpose(0, 3, 1, 2))
    com["sgb"] = np.ascontiguousarray(np.broadcast_to(inp["sg_b"].reshape(L, 1, 4, 128), (L, 128, 4, 128)))
    com["poolb"] = np.ascontiguousarray(inp["pool_b"].reshape(L, 4, 128).transpose(0, 2, 1))
    com["pools"] = np.ascontiguousarray(inp["pool_scale"].reshape(L, 4, 128).transpose(0, 2, 1))
    com["m2cw"] = np.ascontiguousarray(inp["m2_conv_w"].reshape(L, 4, 8, 128).transpose(0, 3, 2, 1))
    com["m2cb"] = np.ascontiguousarray(inp["m2_conv_b"].reshape(L, 8, 128).transpose(0, 2, 1))
    dtb = np.zeros((L, 40, 1), f)
    alog = np.zeros((L, 40, 1), f)
    for l in range(L):
        dtb[l, 0:8, 0] = inp["m2_dt_bias"][l, 0]
        dtb[l, 32:40, 0] = inp["m2_dt_bias"][l, 1]
        alog[l, 0:8, 0] = inp["m2_a_log"][l, 0]
        alog[l, 32:40, 0] = inp["m2_a_log"][l, 1]
    com["m2dtb"] = dtb
    com["m2alog"] = alog
    com["m2d"] = np.ascontiguousarray(np.repeat(inp["m2_d"], 64, axis=1).reshape(L, 4, 128).transpose(0, 2, 1))
    com["m2nw"] = np.ascontiguousarray(inp["m2_norm_w"].reshape(L, 4, 128).transpose(0, 2, 1))
    com["fcw"] = np.ascontiguousarray(inp["ffn_conv_w"].reshape(L, 9, NFT, 128).transpose(0, 3, 2, 1))
    com["fcb"] = np.ascontiguousarray(inp["ffn_conv_b"].reshape(L, NFT, 128).transpose(0, 2, 1))
    com["ident"] = np.eye(128, dtype=f)
    com["iota1"] = np.ascontiguousarray(np.broadcast_to(np.arange(1, 257, dtype=f)[None, :], (128, 256)))
    jj = np.arange(128)[:, None]
    ii = np.arange(128)[None, :]
    com["maskF"] = (jj <= ii).astype(f)
    com["maskB"] = (jj >= ii).astype(f)
    cm = np.ones((40, 2, NR), f)
    cm[:, 0, 0::128] = 0.0
    cm[:, 1, 127::128] = 0.0
    com["cmask"] = cm
    selh = np.zeros((40, 16, 128), f)
    for d in range(2):
        for h in range(8):
            selh[32 * d + h, d * 8 + h, :] = 1.0
    com["selh"] = selh
    per = []
    for c in range(8):
        b, k = divmod(c, 4)
        pc = {}
        xs = inp["x"][b, k * LAT:(k + 1) * LAT, :]
        pc["x_fm"] = np.ascontiguousarray(xs.reshape(LAT, 16, 128).transpose(2, 1, 0))
        pc["ctx_fm"] = np.ascontiguousarray(inp["ctx"][b].reshape(CTX, 16, 128).transpose(2, 1, 0))
        cv = np.stack([inp["c"][b].reshape(16, 128).T, inp["c_ctx"].reshape(16, 128).T], 2)
        pc["cvec"] = np.ascontiguousarray(cv, dtype=f)
        fl = np.zeros((128, 36), f)
        if k > 0:
            fl[:, c - 1] = 1.0
            fl[:, 32] = 1.0
        if k < 3:
            fl[:, 8 + c + 1] = 1.0
            fl[:, 33] = 1.0
        for j in range(8):
            if j // 4 == b and j < c:
                fl[:, 16 + j] = 1.0
            if j // 4 == b and j > c:
                fl[:, 24 + j] = 1.0
        pc["flags"] = fl
        for wk in BIGW:
            w = inp[wk]
            rows = w.shape[1] // 8
            pc[wk] = np.ascontiguousarray(w[:, c * rows:(c + 1) * rows, :], dtype=f)
        per.append(pc)
    return com, per

ENGS = ("pe", "dve", "act", "pool", "sp")
NDMA_SEM = 6


class Prog:
    def __init__(self, nc, same_engine_sync=True):
        self.nc = nc
        self.same = same_engine_sync
        self.ops = {e: [] for e in ENGS}
        self.sems = {}
        self.semval = {}
        self.seen = {e: {} for e in ENGS}
        self.last_w = {}
        self.readers = {}
        self.dma_rr = {e: 0 for e in ENGS}
        self._ctx = []
        for e in ENGS:
            self._mksem("c_" + e)
            for i in range(NDMA_SEM):
                self._mksem("d_%s_%d" % (e, i))
        self._mksem("cc")
        self._blk_cm = nc.Block()
        self.block = self._blk_cm.__enter__()
        self.n = {e: 0 for e in ENGS}

    def _mksem(self, name):
        cm = self.nc.semaphore(name)
        s = cm.__enter__()
        self._ctx.append(cm)
        self.sems[name] = s
        self.semval[name] = 0

    def _collect(self, eng, reads, writes, is_pe_acc=False):
        need = {}

        def add(ev):
            if ev is None:
                return
            s, v, src = ev
            if src == eng and (not self.same or (eng == "pe")):
                return
            if self.seen[eng].get(s, 0) >= v:
                return
            if need.get(s, 0) < v:
                need[s] = v

        for k in reads:
            add(self.last_w.get(k))
        for k in writes:
            add(self.last_w.get(k))
            for r in self.readers.get(k, ()):
                add(r)
        for s, v in need.items():
            self.seen[eng][s] = v
        return list(need.items())

    def _commit(self, ev, reads, writes):
        for k in reads:
            self.readers.setdefault(k, []).append(ev)
        for k in writes:
            self.last_w[k] = ev
            self.readers[k] = []

    def op(self, eng, fn, reads=(), writes=()):
        waits = self._collect(eng, reads, writes)
        s = "c_" + eng
        self.semval[s] += 1
        v = self.semval[s]
        self._emit(eng, waits, fn, s, 1)
        self._commit((s, v, eng), reads, writes)

    def dma(self, q, fn, reads=(), writes=()):
        i = self.dma_rr[q]
        self.dma_rr[q] = (i + 1) % NDMA_SEM
        s = "d_%s_%d" % (q, i)
        waits = self._collect(q, reads, writes)
        pv = self.semval[s]
        if pv > 0 and self.seen[q].get(s, 0) < pv:
            waits.append((s, pv))
            self.seen[q][s] = pv
        self.semval[s] += 16
        v = self.semval[s]
        self._emit(q, waits, fn, s, 16)
        self._commit((s, v, "dma"), reads, writes)

    def collective(self, fn, reads=(), writes=()):
        q = "pool"
        waits = self._collect(q, reads, writes)
        s = "cc"
        self.semval[s] += 1
        v = self.semval[s]
        self._emit(q, waits, fn, s, 1)
        self._commit((s, v, "cc"), reads, writes)

    ENGMAP = {"pe": "tensor", "dve": "vector", "act": "scalar", "pool": "gpsimd", "sp": "sync"}

    def _emit(self, e, waits, fn, s=None, inc=0):
        self.n[e] += 1

        def body(engobj):
            for ws, wv in waits:
                engobj.wait_ge(self.sems[ws], wv)
            if fn is not None:
                inst = fn(engobj)
                inst.then_inc(self.sems[s], inc)

        getattr(self.block, self.ENGMAP[e])(body)

    def barrier(self):
        for e in ENGS:
            waits = []
            for s, v in self.semval.items():
                if v > 0 and self.seen[e].get(s, 0) < v:
                    waits.append((s, v))
                    self.seen[e][s] = v
            if waits:
                self._emit(e, waits, None)
        self.last_w.clear()
        self.readers.clear()

    def emit(self, final_keys=()):
        self.barrier()
        self._blk_cm.__exit__(None, None, None)
        for cm in reversed(self._ctx):
            cm.__exit__(None, None, None)
        self._ctx = []

    def ninstr(self):
        return dict(self.n)
F32 = mybir.dt.float32
BF16 = mybir.dt.bfloat16
I32 = mybir.dt.int32
ALU = mybir.AluOpType
AF = mybir.ActivationFunctionType
TOKB = [(0, 256), (256, 272), (272, 784), (784, 1296)]
RB = [(0, 256), (256, 768), (768, 1280)]


class KB:
    def __init__(self, com, per0, nlayers=NL, dbg=()):
        self.nc = nc = bass.Bass("TRN2", target_bir_lowering=False)
        self.P = Prog(nc)
        self.cms = []
        self.I = {}
        for k, v in list(com.items()) + list(per0.items()):
            self.I[k] = nc.dram_tensor(k, list(v.shape), F32, kind="ExternalInput").ap()
        self.out = nc.dram_tensor("out", [128, 16, LAT], F32, kind="ExternalOutput").ap()
        self.dbg = {}
        self.dbg_want = dbg
        self.nlayers = nlayers
        dr = lambda n, s, dt=F32: nc.dram_tensor(n, s, dt).ap()
        self.hbuf = dr("hbuf", [128, 16, NT])
        self.proj = dr("proj", [NMT * 128, NT])
        self.h1buf = dr("h1buf", [128, 16, NR + 128])
        self.actbuf = dr("actbuf", [NFT, 128, NR], BF16)
        self.r2buf = dr("r2buf", [128, 16, NR])
        self.cin0 = dr("cin0", [256, 128]); self.cout0 = dr("cout0", [2048, 128])
        self.cin2 = dr("cin2", [256, 1024]); self.cout2 = dr("cout2", [2048, 1024])
        self.cinA = dr("cinA", [128, 64]); self.coutA = dr("coutA", [1024, 64])
        self.cinD = dr("cinD", [128, 1040]); self.coutD = dr("coutD", [1024, 1040])
        self.uid = 0

    def sb(self, st, name, shape, dt=F32):
        self.uid += 1
        return st.enter_context(self.nc.sbuf_tensor("%s_%d" % (name, self.uid), shape, dt))

    def ps(self, st, name, shape, dt=F32):
        self.uid += 1
        return st.enter_context(self.nc.psum_tensor("%s_%d" % (name, self.uid), shape, dt))

    def dump(self, name, src_ap, shape, key):
        if name not in self.dbg_want:
            return
        t = self.nc.dram_tensor("dbg_" + name, shape, F32, kind="ExternalOutput").ap()
        self.dbg[name] = t
        n0 = shape[0]
        step = 128 if n0 > 128 else (n0 if len(shape) < 3 else 16)
        for a in range(0, n0, step):
            b = min(n0, a + step)
            self.P.dma("sp", lambda e, a=a, b=b: e.dma_start(out=t[a:b], in_=src_ap[a:b]), reads=[key], writes=["dbg_" + name])

    def mm(self, out, lhsT, rhs, start, stop, r, w):
        self.P.op("pe", lambda e: e.matmul(out, lhsT=lhsT, rhs=rhs, start=start, stop=stop), reads=r, writes=w)

    def tr(self, out, in_, ident, r, w):
        self.P.op("pe", lambda e: e.transpose(out, in_=in_, identity=ident), reads=r, writes=w)

    def tt(self, out, a, b, op, r, w, eng="dve"):
        self.P.op(eng, lambda e: e.tensor_tensor(out=out, in0=a, in1=b, op=op), reads=r, writes=w)

    def ts(self, out, a, s1, s2, op0, op1, r, w, eng="dve"):
        if op1 is None:
            self.P.op(eng, lambda e: e.tensor_scalar(out=out, in0=a, scalar1=s1, scalar2=None, op0=op0), reads=r, writes=w)
        else:
            self.P.op(eng, lambda e: e.tensor_scalar(out=out, in0=a, scalar1=s1, scalar2=s2, op0=op0, op1=op1), reads=r, writes=w)

    def stt(self, out, a, s, b, op0, op1, r, w):
        self.P.op("dve", lambda e: e.scalar_tensor_tensor(out=out, in0=a, scalar=s, in1=b, op0=op0, op1=op1), reads=r, writes=w)

    def act(self, out, a, func, r, w, bias=0.0, scale=1.0):
        self.P.op("act", lambda e: e.activation(out=out, in_=a, func=func, bias=bias, scale=scale), reads=r, writes=w)

    def cp(self, out, a, r, w, eng="dve"):
        if eng == "act":
            self.P.op("act", lambda e: e.activation(out=out, in_=a, func=AF.Copy), reads=r, writes=w)
        else:
            self.P.op(eng, lambda e: e.tensor_copy(out=out, in_=a), reads=r, writes=w)

    def ld(self, out, src, r, w, q="sp", slow=False):
        if slow:
            self.P.dma(q, lambda e: e.dma_start(out=out, in_=src, allow_slow_non_contiguous=True), reads=r, writes=w)
        else:
            self.P.dma(q, lambda e: e.dma_start(out=out, in_=src), reads=r, writes=w)

    def gelu(self, out, x, t, r, w, tmpkey):
        self.tt(t, x, x, ALU.mult, r, [tmpkey])
        self.ts(t, t, 0.044715, 1.0, ALU.mult, ALU.add, [tmpkey], [tmpkey])
        self.tt(t, t, x, ALU.mult, r + [tmpkey], [tmpkey])
        self.act(t, t, AF.Sigmoid, [tmpkey], [tmpkey], scale=1.5957691216057308)
        self.tt(out, t, x, ALU.mult, r + [tmpkey], w)

    def vop(self, name, r, w, *args, **kw):
        self.P.op("dve", lambda e: getattr(e, name)(*args, **kw), reads=r, writes=w)

    def exchange(self, st0, first_ap, last_ap, hl_ap, hr_ap, n, cin, cout, rkeys, wkeys, tag):
        P = self.P
        w = 16 * n
        ci = cin.rearrange("(s p) (k j) -> s p k j", s=2, k=16)
        self.ld(ci[0], first_ap, rkeys, [tag + "cin"], q="sp")
        self.ld(ci[1], last_ap, rkeys, [tag + "cin"], q="sp")
        P.collective(lambda e: e.collective_compute("AllGather", ALU.bypass, replica_groups=[list(range(8))],
                                                    ins=[cin.opt()], outs=[cout.opt()]),
                     reads=[tag + "cin"], writes=[tag + "cout"])
        co = cout.rearrange("(r s p) c -> s p r c", r=8, s=2)
        with ExitStack() as st:
            g = self.sb(st, "exg", [128, 8, w])
            acc = self.sb(st, "exa", [128, w])
            fl = self.flags
            for side, (src, dst, so) in enumerate(((co[1], hl_ap, 0), (co[0], hr_ap, 8))):
                gk, ak = tag + "g", tag + "a"
                self.ld(g[:], src, [tag + "cout"], [gk])
                self.ts(acc[:], g[:, 0, :], fl[:, so:so + 1], None, ALU.mult, None, [gk], [ak])
                for j in range(1, 8):
                    self.stt(acc[:], g[:, j, :], fl[:, so + j:so + j + 1], acc[:], ALU.mult, ALU.add, [gk, ak], [ak])
                self.ld(dst, acc[:].rearrange("p (k j) -> p k j", k=16), [ak], wkeys)
            P.barrier()
def phase_setup(self, st):
    P = self.P
    I = self.I
    self.flags_t = self.sb(st, "flags", [128, 36])
    self.flags = self.flags_t
    self.ld(self.flags_t[:], I["flags"], [], ["flags"])
    self.ident = self.sb(st, "ident", [128, 128])
    self.ld(self.ident[:], I["ident"], [], ["ident"])
    self.epsc = self.sb(st, "epsc", [128, 2])
    self.P.op("dve", lambda e: e.memset(self.epsc[:, 0:1], LN_EPS), writes=["epsc"])
    self.P.op("dve", lambda e: e.memset(self.epsc[:, 1:2], 1.0), writes=["epsc"])
    self.mod = self.sb(st, "mod", [128, 96, 2])
    self.mod1p = self.sb(st, "mod1p", [128, 96, 2])
    self.cv = self.sb(st, "cv", [128, 16, 2])
    self.cvb = self.sb(st, "cvb", [128, 16, 2], BF16)
    self.ld(self.cv[:], I["cvec"], [], ["cv"])
    self.act(self.cvb[:], self.cv[:], AF.Silu, ["cv"], ["cvb"])
    self.ld(self.hbuf[:, :, C_LAT[0]:C_LAT[1]], I["x_fm"], [], ["hbuf"])
    self.ld(self.hbuf[:, :, 0:CTX], I["ctx_fm"], [], ["hbuf"])
    self.W = {}
    for wk in BIGW:
        self.W[wk] = []
        for l in range(self.nlayers):
            src = I[wk][l]
            rows, cols = src.shape
            part = self.nc.dram_tensor("wp_%s_%d" % (wk, l), [rows, cols], BF16).ap()
            full = self.nc.dram_tensor("wf_%s_%d" % (wk, l), [rows * 8, cols], BF16).ap()
            nd = (cols + 1023) // 1024
            rstep = max(1, 768 // nd)
            for a in range(0, rows, rstep):
                b = min(rows, a + rstep)
                P.dma("pool", lambda e, a=a, b=b, part=part, src=src: e.dma_start(out=part[a:b], in_=src[a:b], max_dma_last_dim=4096),
                      reads=[], writes=["wp_%s_%d" % (wk, l)])
            P.collective(lambda e, part=part, full=full: e.collective_compute(
                "AllGather", ALU.bypass, replica_groups=[list(range(8))], ins=[part.opt()], outs=[full.opt()]),
                reads=["wp_%s_%d" % (wk, l)], writes=["wf_%s_%d" % (wk, l)])
            self.W[wk].append(full)
    P.barrier()


def phase_e0(self):
    hb = self.hbuf
    with ExitStack() as st:
        self.exchange(st, hb[:, :, C_LAT[0]:C_LAT[0] + 8], hb[:, :, C_LAT[1] - 8:C_LAT[1]],
                      hb[:, :, C_HL[0]:C_HL[1]], hb[:, :, C_HR[0]:C_HR[1]], 8, self.cin0, self.cout0,
                      ["hbuf"], ["hbuf"], "e0")


def phase_ada(self, l):
    P = self.P
    I = self.I
    with ExitStack() as st:
        bada = self.sb(st, "bada", [128, 96])
        self.ld(bada[:], I["b_ada_l"][l], [], ["bada"])
        pp = self.ps(st, "adaps", [128, 96, 2])
        wv = self.W["w_ada"][l].rearrange("(k p) n -> p k n", p=128)
        slabs = [self.sb(st, "adaw", [128, 16, 512], BF16) for _ in range(2)]
        for cb in range(24):
            sl = slabs[cb % 2]
            sk = "adaw%d" % (cb % 2)
            self.ld(sl[:], wv[:, :, cb * 512:(cb + 1) * 512], [], [sk], q="sp")
            for mi in range(4):
                j = cb * 4 + mi
                for k in range(16):
                    self.mm(pp[:, j, :], sl[:, k, mi * 128:(mi + 1) * 128], self.cvb[:, k, :], k == 0, k == 15,
                            [sk, "cvb"], ["adaps"])
        self.tt(self.mod[:], pp[:], bada[:].unsqueeze(2).to_broadcast([128, 96, 2]), ALU.add, ["adaps", "bada"], ["mod"])
        self.ts(self.mod1p[:], self.mod[:], 1.0, None, ALU.add, None, ["mod"], ["mod1p"])
        self.dump("mod%d" % l, self.mod[:], [128, 96, 2], "mod")
        P.barrier()


def phase_inproj(self, l):
    P = self.P
    I = self.I
    with ExitStack() as st:
        hm = self.sb(st, "hm", [128, 16, NT], BF16)
        hst = [self.sb(st, "hst", [128, NT]) for _ in range(2)]
        for k in range(16):
            hs = hst[k % 2]
            hk = "hst%d" % (k % 2)
            self.ld(hs[:], self.hbuf[:, k, :], ["hbuf"], [hk])
            self.ts(hm[:, k, 0:CTX], hs[:, 0:CTX], self.mod1p[:, 16 + k, 1:2], self.mod[:, k, 1:2], ALU.mult, ALU.add,
                    [hk, "mod", "mod1p"], ["hm"])
            self.ts(hm[:, k, CTX:NT], hs[:, CTX:NT], self.mod1p[:, 16 + k, 0:1], self.mod[:, k, 0:1], ALU.mult, ALU.add,
                    [hk, "mod", "mod1p"], ["hm"])
        wv = self.W["w_in"][l].rearrange("(k p) n -> p k n", p=128)
        slabs = [self.sb(st, "inw", [128, 16, 512], BF16) for _ in range(2)]
        stg = [self.sb(st, "pstg", [128, NT]) for _ in range(2)]
        pss = [self.ps(st, "pjps", [128, 512]) for _ in range(4)]
        pi = 0
        for cb in range(8):
            c0 = cb * 512
            cw = min(512, INW - c0)
            sl = slabs[cb % 2]
            sk = "inw%d" % (cb % 2)
            self.ld(sl[:, :, 0:cw], wv[:, :, c0:c0 + cw], [], [sk], q="sp")
            for mi in range((cw + 127) // 128):
                m = cb * 4 + mi
                mw = min(128, cw - mi * 128)
                sg = stg[m % 2]
                gk = "pstg%d" % (m % 2)
                for (a, b) in TOKB:
                    pt = pss[pi % 4]
                    pk = "pjps%d" % (pi % 4)
                    pi += 1
                    for k in range(16):
                        self.mm(pt[0:mw, 0:b - a], sl[:, k, mi * 128:mi * 128 + mw], hm[:, k, a:b], k == 0, k == 15,
                                [sk, "hm"], [pk])
                    self.cp(sg[0:mw, a:b], pt[0:mw, 0:b - a], [pk], [gk], eng="act")
                self.ld(self.proj[m * 128:m * 128 + mw, :], sg[0:mw, :], [gk], ["proj"])
        P.barrier()


KB.phase_setup = phase_setup
KB.phase_e0 = phase_e0
KB.phase_ada = phase_ada
KB.phase_inproj = phase_inproj
def winchain(self, E, A, Bt, Lext, level, rk, keyA, keyB):
    self.tt(A[:, 1:Lext], E[:, 0:Lext - 1], E[:, 1:Lext], ALU.add, rk, [keyA])
    if level == 1:
        return A, keyA
    self.tt(Bt[:, 2:Lext - 1], A[:, 1:Lext - 2], A[:, 3:Lext], ALU.add, [keyA], [keyB])
    if level == 2:
        return Bt, keyB
    self.tt(A[:, 4:Lext - 3], Bt[:, 2:Lext - 5], Bt[:, 6:Lext - 1], ALU.add, [keyB], [keyA])
    if level == 3:
        return A, keyA
    self.tt(Bt[:, 8:Lext - 7], A[:, 4:Lext - 11], A[:, 12:Lext - 3], ALU.add, [keyA], [keyB])
    return Bt, keyB


def mix_gating(self, l, need_ctx):
    I = self.I
    P = self.P
    with ExitStack() as st:
        lnG = self.sb(st, "lnG", [128, 4, 128]); lnB = self.sb(st, "lnB", [128, 4, 128])
        wsf = self.sb(st, "wsf", [128, 4, 128]); wsb = self.sb(st, "wsb", [128, 4, 128], BF16)
        bsR = self.sb(st, "bsR", [128, 4, 128])
        self.ld(lnG[:], I["sglnG"][l], [], ["lnG"]); self.ld(lnB[:], I["sglnB"][l], [], ["lnB"])
        self.ld(wsf[:], I["sgwT"][l], [], ["wsf"]); self.ld(bsR[:], I["sgb"][l], [], ["bsR"])
        self.cp(wsb[:], wsf[:], ["wsf"], ["wsb"])
        U = self.sb(st, "gU", [128, NR]); V = self.sb(st, "gV", [128, NR]); T = self.sb(st, "gT", [128, NR])
        tps = [self.ps(st, "gtp", [128, 128]) for _ in range(2)]
        sps = [self.ps(st, "gsp", [128, 128]) for _ in range(2)]
        st6 = self.sb(st, "st6", [128, 6]); mv = self.sb(st, "mv", [128, 2]); rs = self.sb(st, "rs", [128, 1])
        vn = self.sb(st, "vn", [128, 128]); vnb = self.sb(st, "vnb", [128, 128], BF16); sv = self.sb(st, "sv", [128, 128])
        c0 = 0 if need_ctx else 2
        a0 = c0 * 128
        n = 0
        for hd in range(4):
            ru = 512 + hd * 128
            rv = 1024 + hd * 128
            for (tile, key, r0) in ((U, "gU", ru), (V, "gV", rv)):
                if need_ctx:
                    self.ld(tile[:, 0:CTX], self.proj[r0:r0 + 128, 0:CTX], ["proj"], [key])
                self.ld(tile[:, CTX:NR], self.proj[r0:r0 + 128, C_LAT[0]:C_LAT[1]], ["proj"], [key])
                self.gelu(tile[:, a0:NR], tile[:, a0:NR], T[:, a0:NR], [key], [key], "gT")
            for c in range(c0, 10):
                cs = c * 128
                tp = tps[n % 2]; tk = "gtp%d" % (n % 2)
                sp = sps[n % 2]; sk = "gsp%d" % (n % 2)
                n += 1
                self.tr(tp[:], V[:, cs:cs + 128], self.ident[:], ["gV", "ident"], [tk])
                self.vop("bn_stats", [tk], ["st6"], out=st6[:], in_=tp[:])
                self.vop("bn_aggr", ["st6"], ["mv"], out=mv[:], in_=st6[:])
                self.act(rs[:], mv[:, 1:2], AF.Sqrt, ["mv"], ["rs"], bias=self.epsc[:, 0:1])
                self.vop("reciprocal", ["rs"], ["rs"], out=rs[:], in_=rs[:])
                self.ts(vn[:], tp[:], mv[:, 0:1], rs[:, 0:1], ALU.subtract, ALU.mult, [tk, "mv", "rs"], ["vn"])
                self.tt(vn[:], vn[:], lnG[:, hd, :], ALU.mult, ["vn", "lnG"], ["vn"])
                self.tt(vnb[:], vn[:], lnB[:, hd, :], ALU.add, ["vn", "lnB"], ["vnb"])
                self.mm(sp[:], vnb[:], wsb[:, hd, :], True, True, ["vnb", "wsb"], [sk])
                self.tt(sv[:], sp[:], bsR[:, hd, :], ALU.add, [sk, "bsR"], ["sv"])
                self.tt(self.mix[:, 4 + hd, cs:cs + 128], sv[:], U[:, cs:cs + 128], ALU.mult, ["sv", "gU"], ["mix"])
        P.barrier()


def mix_pool(self, l, need_ctx):
    I = self.I
    P = self.P
    fl = self.flags
    with ExitStack() as st:
        pwf = self.sb(st, "pwf", [128, 4, 128]); pwb = self.sb(st, "pwb", [128, 4, 128], BF16)
        pb = self.sb(st, "pb", [128, 4]); psc = self.sb(st, "psc", [128, 4])
        self.ld(pwf[:], I["pool_w"][l].rearrange("g c d -> c g d"), [], ["pwf"])
        self.cp(pwb[:], pwf[:], ["pwf"], ["pwb"])
        self.ld(pb[:], I["poolb"][l], [], ["pb"]); self.ld(psc[:], I["pools"][l], [], ["psc"])
        LX = LAT + 16
        E = self.sb(st, "pE", [128, LX]); A = self.sb(st, "pA", [128, LX]); Bt = self.sb(st, "pB", [128, LX])
        rc = self.sb(st, "prc", [128, 4, LAT]); df = self.sb(st, "pdf", [128, LAT], BF16)
        mn = self.sb(st, "pmn", [128, LAT])
        pss = [self.ps(st, "pps", [128, 512]) for _ in range(2)]
        segs = [("lat", LAT, CTX)]
        if need_ctx:
            segs.append(("ctx", CTX, 0))
        n = 0
        for (sname, Lr, mo) in segs:
            Lext = Lr + 16
            self.vop("memset", [], ["pE"], E[:, 0:Lext], 1.0)
            if sname == "lat":
                self.cp(E[:, 0:8], fl[:, 32:33].to_broadcast([128, 8]), ["flags"], ["pE"])
                self.cp(E[:, 8 + Lr:Lext], fl[:, 33:34].to_broadcast([128, 8]), ["flags"], ["pE"])
            else:
                self.vop("memset", [], ["pE"], E[:, 0:8], 0.0)
                self.vop("memset", [], ["pE"], E[:, 8 + Lr:Lext], 0.0)
            for g in range(4):
                W, wk = self.winchain(E, A, Bt, Lext, g + 1, ["pE"], "pA", "pB")
                self.vop("reciprocal", [wk], ["prc"], out=rc[:, g, 0:Lr], in_=W[:, 8:8 + Lr])
            for g in range(4):
                r0 = 1536 + g * 128
                if sname == "lat":
                    self.ld(E[:, 0:8], self.proj[r0:r0 + 128, C_HL[0]:C_HL[1]], ["proj"], ["pE"])
                    self.ld(E[:, 8:8 + Lr], self.proj[r0:r0 + 128, C_LAT[0]:C_LAT[1]], ["proj"], ["pE"])
                    self.ld(E[:, 8 + Lr:Lext], self.proj[r0:r0 + 128, C_HR[0]:C_HR[1]], ["proj"], ["pE"])
                    self.ts(E[:, 0:8], E[:, 0:8], fl[:, 32:33], None, ALU.mult, None, ["pE", "flags"], ["pE"])
                    self.ts(E[:, 8 + Lr:Lext], E[:, 8 + Lr:Lext], fl[:, 33:34], None, ALU.mult, None, ["pE", "flags"], ["pE"])
                else:
                    self.vop("memset", [], ["pE"], E[:, 0:8], 0.0)
                    self.vop("memset", [], ["pE"], E[:, 8 + Lr:Lext], 0.0)
                    self.ld(E[:, 8:8 + Lr], self.proj[r0:r0 + 128, 0:CTX], ["proj"], ["pE"])
                W, wk = self.winchain(E, A, Bt, Lext, g + 1, ["pE"], "pA", "pB")
                self.tt(mn[:, 0:Lr], W[:, 8:8 + Lr], rc[:, g, 0:Lr], ALU.mult, [wk, "prc"], ["pmn"])
                self.tt(df[:, 0:Lr], mn[:, 0:Lr], E[:, 8:8 + Lr], ALU.subtract, ["pmn", "pE"], ["pdf"])
                for a in range(0, Lr, 512):
                    b = min(Lr, a + 512)
                    pt = pss[n % 2]; pk = "pps%d" % (n % 2)
                    n += 1
                    self.mm(pt[:, 0:b - a], pwb[:, g, :], df[:, a:b], True, True, ["pwb", "pdf"], [pk])
                    self.ts(self.mix[:, 8 + g, mo + a:mo + b], pt[:, 0:b - a], pb[:, g:g + 1], psc[:, g:g + 1], ALU.add, ALU.mult,
                            [pk, "pb", "psc"], ["mix"])
        P.barrier()


def phase_mixers(self, l):
    need_ctx = l < NL - 1
    if not any(k in self.dbg_want for k in ("mixA", "mixD", "mixAD")):
        self.mix_gating(l, need_ctx)
        self.mix_pool(l, need_ctx)
    if "mixBC" in self.dbg_want:
        with ExitStack() as st:
            t = self.sb(st, "dmix", [128, 8, NR])
            self.cp(t[:], self.mix[:, 4:12, :], ["mix"], ["dmix"])
            self.dump("mixBC", t[:], [128, 8, NR], "dmix")
            self.P.barrier()
        return
    if "mixA" in self.dbg_want or "mixAD" in self.dbg_want:
        self.mix_s5(l, need_ctx)
    if "mixD" in self.dbg_want or "mixAD" in self.dbg_want:
        self.mix_ssd(l, need_ctx)
    if "mixAD" in self.dbg_want or "mixA" in self.dbg_want or "mixD" in self.dbg_want:
        nm = [k for k in ("mixA", "mixD", "mixAD") if k in self.dbg_want][0]
        with ExitStack() as st:
            t = self.sb(st, "dmix", [128, 8, NR])
            self.vop("memset", [], ["dmix"], t[:], 0.0)
            if nm != "mixD":
                self.cp(t[:, 0:4], self.mix[:, 0:4, :], ["mix"], ["dmix"])
            if nm != "mixA":
                self.cp(t[:, 4:8], self.mix[:, 12:16, :], ["mix"], ["dmix"])
            self.dump(nm, t[:], [128, 8, NR], "dmix")
            self.P.barrier()
        return
    self.mix_s5(l, need_ctx)
    self.mix_ssd(l, need_ctx)


KB.winchain = winchain
KB.mix_gating = mix_gating
KB.mix_pool = mix_pool
KB.phase_mixers = phase_mixers
def sincos(self, ang, sn, cs, ki, red, m, n, rk, tag, tk="s"):
    for (dst, off) in ((sn, 0.0), (cs, math.pi / 2)):
        kk, kr, km = tk + "ki", tk + "red", tk + "m"
        if off == 0.0:
            src = ang
            self.ts(ki, src, 1.0 / TWO_PI, None, ALU.mult, None, rk, [kk])
        else:
            self.ts(red, ang, off, None, ALU.add, None, rk, [kr])
            src = red
            self.ts(ki, src, 1.0 / TWO_PI, None, ALU.mult, None, [kr], [kk])
        self.stt(red, ki, -TWO_PI, src, ALU.mult, ALU.add, [kk, kr] + rk, [kr])
        self.ts(m, red, math.pi, -TWO_PI, ALU.is_gt, ALU.mult, [kr], [km])
        self.tt(red, red, m, ALU.add, [kr, km], [kr])
        self.ts(m, red, -math.pi, TWO_PI, ALU.is_lt, ALU.mult, [kr], [km])
        self.tt(red, red, m, ALU.add, [kr, km], [kr])
        self.act(dst, red, AF.Sin, [kr], [tag + "out"])


def cmul(self, ore, oim, are, aim, bre, bim, t1, t2, rk, wk, k1, k2):
    self.tt(t1, are, bre, ALU.mult, rk, [k1])
    self.tt(t2, aim, bim, ALU.mult, rk, [k2])
    self.tt(t1, t1, t2, ALU.subtract, [k1, k2], [k1])
    self.tt(t2, are, bim, ALU.mult, rk, [k2])
    self.tt(oim, aim, bre, ALU.mult, rk, wk)
    self.tt(oim, oim, t2, ALU.add, wk + [k2], wk)
    self.cp(ore, t1, [k1], wk)


def mix_s5(self, l, need_ctx):
    I = self.I
    P = self.P
    fl = self.flags
    NB = 256
    with ExitStack() as st:
        sb = lambda n, s, dt=F32: self.sb(st, n, s, dt)
        prm = sb("s5prm", [128, 2, 3, 16])
        self.ld(prm[:], I["s5p"][l].rearrange("d p k t -> p d k t"), [], ["prm"])
        lre, lim, lst = prm[:, :, 0, :], prm[:, :, 1, :], prm[:, :, 2, :]
        stp = sb("stp", [128, 2, 16]); zre = sb("zre", [128, 2, 16]); th = sb("th", [128, 2, 16]); rr = sb("rr", [128, 2, 16])
        self.act(stp[:], lst, AF.Exp, ["prm"], ["stp"])
        self.tt(zre[:], lre, stp[:], ALU.mult, ["prm", "stp"], ["zre"])
        self.tt(th[:], lim, stp[:], ALU.mult, ["prm", "stp"], ["th"])
        self.act(rr[:], zre[:], AF.Exp, ["zre"], ["rr"])
        ki = sb("ski", [128, 2, 16], I32); red = sb("sred", [128, 2, 16]); mm_ = sb("smm", [128, 2, 16])
        ang = sb("sang", [128, 2, 16])
        sn1 = sb("sn1", [128, 2, 16]); cs1 = sb("cs1", [128, 2, 16])
        self.sincos(th[:], sn1[:], cs1[:], ki[:], red[:], mm_[:], 32, ["th"], "sc1")
        pw = {}
        for (nm, mult) in (("256", 256.0), ("1024", 1024.0)):
            s_ = sb("sn" + nm, [128, 2, 16]); c_ = sb("cs" + nm, [128, 2, 16]); rp = sb("rp" + nm, [128, 2, 16])
            self.ts(ang[:], th[:], mult, None, ALU.mult, None, ["th"], ["ang"])
            self.sincos(ang[:], s_[:], c_[:], ki[:], red[:], mm_[:], 32, ["ang"], "sc" + nm)
            self.act(rp[:], zre[:], AF.Exp, ["zre"], ["rp" + nm], scale=mult)
            pre = sb("pre" + nm, [128, 2, 16]); pim = sb("pim" + nm, [128, 2, 16])
            self.tt(pre[:], rp[:], c_[:], ALU.mult, ["rp" + nm, "sc" + nm + "out"], ["p" + nm])
            self.tt(pim[:], rp[:], s_[:], ALU.mult, ["rp" + nm, "sc" + nm + "out"], ["p" + nm])
            pw[nm] = (pre, pim)
        t1 = sb("t1", [128, 2, 16]); t2 = sb("t2", [128, 2, 16]); t3 = sb("t3", [128, 2, 16])
        nre = sb("nre", [128, 2, 16]); nim = sb("nim", [128, 2, 16]); den = sb("den", [128, 2, 16])
        bsr = sb("bsr", [128, 2, 16]); bsi = sb("bsi", [128, 2, 16])
        self.tt(nre[:], rr[:], cs1[:], ALU.mult, ["rr", "sc1out"], ["nre"])
        self.ts(nre[:], nre[:], -1.0, None, ALU.add, None, ["nre"], ["nre"])
        self.tt(nim[:], rr[:], sn1[:], ALU.mult, ["rr", "sc1out"], ["nim"])
        self.tt(den[:], lre, lre, ALU.mult, ["prm"], ["den"])
        self.tt(t1[:], lim, lim, ALU.mult, ["prm"], ["t1"])
        self.tt(den[:], den[:], t1[:], ALU.add, ["den", "t1"], ["den"])
        self.vop("reciprocal", ["den"], ["den"], out=den[:], in_=den[:])
        self.tt(t1[:], nre[:], lre, ALU.mult, ["nre", "prm"], ["t1"])
        self.tt(t2[:], nim[:], lim, ALU.mult, ["nim", "prm"], ["t2"])
        self.tt(t1[:], t1[:], t2[:], ALU.add, ["t1", "t2"], ["t1"])
        self.tt(bsr[:], t1[:], den[:], ALU.mult, ["t1", "den"], ["bsr"])
        self.tt(t1[:], nim[:], lre, ALU.mult, ["nim", "prm"], ["t1"])
        self.tt(t2[:], nre[:], lim, ALU.mult, ["nre", "prm"], ["t2"])
        self.tt(t1[:], t1[:], t2[:], ALU.subtract, ["t1", "t2"], ["t1"])
        self.tt(bsi[:], t1[:], den[:], ALU.mult, ["t1", "den"], ["bsi"])
        CeR = sb("CeR", [128, 2, 16, 64], BF16); nCeI = sb("nCeI", [128, 2, 16, 64], BF16)
        BTb = sb("BTb", [128, 2, 2, 4, 2, 128], BF16)
        with ExitStack() as st2:
            ctf = self.sb(st2, "ctf", [128, 2, 2, 16, 64])
            self.ld(ctf[:], I["s5CT"][l].rearrange("d r p t c -> p d r t c"), [], ["ctf"])
            x1 = self.sb(st2, "cx1", [128, 16, 64]); x2 = self.sb(st2, "cx2", [128, 16, 64])
            for d in range(2):
                br = bsr[:, d, :].unsqueeze(2).to_broadcast([128, 16, 64])
                bi = bsi[:, d, :].unsqueeze(2).to_broadcast([128, 16, 64])
                self.tt(x1[:], ctf[:, d, 0], br, ALU.mult, ["ctf", "bsr"], ["cx1"])
                self.tt(x2[:], ctf[:, d, 1], bi, ALU.mult, ["ctf", "bsi"], ["cx2"])
                self.tt(CeR[:, d], x1[:], x2[:], ALU.subtract, ["cx1", "cx2"], ["CeR"])
                self.tt(x1[:], ctf[:, d, 0], bi, ALU.mult, ["ctf", "bsi"], ["cx1"])
                self.tt(x2[:], ctf[:, d, 1], br, ALU.mult, ["ctf", "bsr"], ["cx2"])
                self.tt(x1[:], x1[:], x2[:], ALU.add, ["cx1", "cx2"], ["cx1"])
                self.ts(nCeI[:, d], x1[:], -1.0, None, ALU.mult, None, ["cx1"], ["nCeI"])
            btf = self.sb(st2, "btf", [128, 2, 2, 4, 2, 128])
            self.ld(btf[:], I["s5BT"][l].rearrange("d r p q e n -> p d r q e n"), [], ["btf"])
            self.cp(BTb[:], btf[:], ["btf"], ["BTb"])
            P.barrier()
        tab = sb("tab", [128, 2, 16, 2, NB])
        with ExitStack() as st2:
            io = self.sb(st2, "io", [128, NB])
            self.ld(io[:], I["iota1"], [], ["io"])
            ta = self.sb(st2, "ta", [128, NB]); tki = self.sb(st2, "tki", [128, NB], I32)
            tred = self.sb(st2, "tred", [128, NB]); tm = self.sb(st2, "tm", [128, NB])
            for d in range(2):
                for t in range(16):
                    self.ts(ta[:], io[:], th[:, d, t:t + 1], None, ALU.mult, None, ["io", "th"], ["ta"])
                    self.sincos(ta[:], tab[:, d, t, 1, :], tab[:, d, t, 0, :], tki[:], tred[:], tm[:], NB, ["ta"], "tb", tk="t")
            P.barrier()
        ub = sb("ub", [128, 4, NR], BF16)
        sd = sb("s5d", [128, 4]); gb = sb("glub", [128, 4])
        self.ld(sd[:], I["s5d"][l], [], ["sd"]); self.ld(gb[:], I["s5glub"][l], [], ["gb"])
        gwb = sb("gwb", [128, 4, 512], BF16)
        with ExitStack() as st2:
            uf = self.sb(st2, "uf", [128, NR])
            for q in range(4):
                self.ld(uf[:, 0:CTX], self.proj[q * 128:(q + 1) * 128, 0:CTX], ["proj"], ["uf"])
                self.ld(uf[:, CTX:NR], self.proj[q * 128:(q + 1) * 128, C_LAT[0]:C_LAT[1]], ["proj"], ["uf"])
                self.cp(ub[:, q, :], uf[:], ["uf"], ["ub"])
            gwf = self.sb(st2, "gwf", [128, 4, 512])
            self.ld(gwf[:], I["s5_glu_w"][l].rearrange("(q p) n -> p q n", p=128), [], ["gwf"])
            self.cp(gwb[:], gwf[:], ["gwf"], ["gwb"])
            P.barrier()
        EZ = sb("EZ", [128, 2, 2, 16, 5])
        ET = sb("ET", [128, 2, 2, 16, 4])
        HC = sb("HC", [128, 2, 2, 16])
        ZERO = sb("zero", [128, 1])
        self.vop("memset", [], ["zero"], ZERO[:], 0.0)
        self.vop("memset", [], ["EZ"], EZ[:], 0.0)
        vps = [self.ps(st, "vps", [128, 2, NB]) for _ in range(2)]
        yps = [self.ps(st, "yps", [128, NB]) for _ in range(2)]
        gps = [self.ps(st, "gps", [128, NB]) for _ in range(2)]
        wre = sb("wre", [128, NB]); wim = sb("wim", [128, NB]); m1 = sb("m1", [128, NB]); m2 = sb("m2", [128, NB])
        gre = sb("gre", [128, NB]); gim = sb("gim", [128, NB])
        hre = sb("hre", [128, NB], BF16); him = sb("him", [128, NB], BF16)
        e1 = sb("e1", [128, 1])
        cnt = [0]

        def block(d, t, col0, init_re, init_im, exit_re, exit_im, full, ypt, ypk, first, ikeys, last=False):
            q, r = divmod(t, 4)
            vp = vps[cnt[0] % 2]; vk = "vps%d" % (cnt[0] % 2)
            cnt[0] += 1
            rv = (lambda ap: ap[:, ::-1]) if d == 1 else (lambda ap: ap)
            rb, e_ = divmod(r, 2)
            usl = ub[64 * rb:64 * rb + 64, q, col0:col0 + NB]
            self.mm(vp[:, 0, :], BTb[64 * rb:64 * rb + 64, d, 0, q, e_, :], usl, True, True, ["BTb", "ub"], [vk])
            self.mm(vp[:, 1, :], BTb[64 * rb:64 * rb + 64, d, 1, q, e_, :], usl, True, True, ["BTb", "ub"], [vk])
            c = rv(tab[:, d, t, 0, :]); s = rv(tab[:, d, t, 1, :])
            self.tt(m1[:], vp[:, 0, :], c, ALU.mult, [vk, "tab"], ["m1"])
            self.tt(m2[:], vp[:, 1, :], s, ALU.mult, [vk, "tab"], ["m2"])
            self.tt(wre[:], m1[:], m2[:], ALU.add, ["m1", "m2"], ["wre"])
            self.tt(m1[:], vp[:, 1, :], c, ALU.mult, [vk, "tab"], ["m1"])
            self.tt(m2[:], vp[:, 0, :], s, ALU.mult, [vk, "tab"], ["m2"])
            self.tt(wim[:], m1[:], m2[:], ALU.subtract, ["m1", "m2"], ["wim"])
            rbc = rr[:, d, t:t + 1].to_broadcast([128, NB])
            self.P.op("dve", lambda e: e.tensor_tensor_scan(out=rv(gre[:]), data0=rbc, data1=rv(wre[:]), initial=init_re,
                                                            op0=ALU.mult, op1=ALU.add), reads=["rr", "wre"] + ikeys, writes=["gre"])
            self.P.op("dve", lambda e: e.tensor_tensor_scan(out=rv(gim[:]), data0=rbc, data1=rv(wim[:]), initial=init_im,
                                                            op0=ALU.mult, op1=ALU.add), reads=["rr", "wim"] + ikeys, writes=["gim"])
            if exit_re is not None:
                lc = 0 if d == 1 else NB - 1
                cL = tab[:, d, t, 0, NB - 1:NB]; sL = tab[:, d, t, 1, NB - 1:NB]
                gr = gre[:, lc:lc + 1]; gi = gim[:, lc:lc + 1]
                self.tt(e1[:], gi, sL, ALU.mult, ["gim", "tab"], ["e1"])
                self.stt(exit_re, gr, cL, e1[:], ALU.mult, ALU.subtract, ["gre", "tab", "e1"], ["EZ"])
                self.tt(e1[:], gi, cL, ALU.mult, ["gim", "tab"], ["e1"])
                self.stt(exit_im, gr, sL, e1[:], ALU.mult, ALU.add, ["gre", "tab", "e1"], ["EZ"])
            if full:
                self.tt(m1[:], gre[:], c, ALU.mult, ["gre", "tab"], ["m1"])
                self.tt(m2[:], gim[:], s, ALU.mult, ["gim", "tab"], ["m2"])
                self.tt(hre[:], m1[:], m2[:], ALU.subtract, ["m1", "m2"], ["hre"])
                self.tt(m1[:], gre[:], s, ALU.mult, ["gre", "tab"], ["m1"])
                self.tt(m2[:], gim[:], c, ALU.mult, ["gim", "tab"], ["m2"])
                self.tt(him[:], m1[:], m2[:], ALU.add, ["m1", "m2"], ["him"])
                self.mm(ypt[64 * rb:64 * rb + 64, :], CeR[:, d, t, :], hre[:], first, False, ["CeR", "hre"], [ypk])
                self.mm(ypt[64 * rb:64 * rb + 64, :], nCeI[:, d, t, :], him[:], False, last, ["nCeI", "him"], [ypk])

        zb = sb("zb", [128, 4, NB], BF16); yv = sb("yv", [128, NB]); gt = sb("gt", [128, NB]); uf2 = sb("uf2", [128, NB])
        sg = sb("sg", [128, NB])
        ycnt = [0]

        def full_block(col0, pcol0, inits):
            for q in range(4):
                ypt = yps[ycnt[0] % 2]; ypk = "yps%d" % (ycnt[0] % 2)
                ycnt[0] += 1
                for r in range(4):
                    t = 4 * q + r
                    for d in range(2):
                        ire, iim, ik = inits(d, t)
                        block(d, t, col0, ire, iim, None, None, True, ypt, ypk, (d == 0 and r % 2 == 0), ik, last=(d == 1 and r % 2 == 1))
                self.ld(uf2[:], self.proj[q * 128:(q + 1) * 128, pcol0:pcol0 + NB], ["proj"], ["uf2"])
                self.stt(yv[:], uf2[:], sd[:, q:q + 1], ypt[:], ALU.mult, ALU.add, ["uf2", "sd", ypk], ["yv"])
                self.gelu(yv[:], yv[:], gt[:], ["yv"], ["yv"], "gt")
                self.cp(zb[:, q, :], yv[:], ["yv"], ["zb"])
                if q == 3:
                    zf_keep = None
            for m in range(4):
                gp = gps[m % 2]; gk = "gps%d" % (m % 2)
                for q in range(4):
                    self.mm(gp[:], gwb[:, q, m * 128:(m + 1) * 128], zb[:, q, :], q == 0, q == 3, ["gwb", "zb"], [gk])
                self.act(sg[:], gp[:], AF.Sigmoid, [gk, "gb"], ["sg"], bias=gb[:, m:m + 1])
                self.tt(self.mix[:, m, col0:col0 + NB], sg[:], zb[:, m, :], ALU.mult, ["sg", "zb"], ["mix"])

        if need_ctx:
            full_block(0, 0, lambda d, t: (0.0, 0.0, []))
        for d in range(2):
            for t in range(16):
                block(d, t, 0, 0.0, 0.0, HC[:, d, 0, t:t + 1], HC[:, d, 1, t:t + 1], False, None, None, False, [])
        for d in range(2):
            for t in range(16):
                for k in range(4):
                    b = k if d == 0 else 3 - k
                    col0 = CTX + b * NB
                    block(d, t, col0, EZ[:, d, 0, t, k:k + 1], EZ[:, d, 1, t, k:k + 1],
                          EZ[:, d, 0, t, k + 1:k + 2], EZ[:, d, 1, t, k + 1:k + 2], False, None, None, False, ["EZ"])
        stg = sb("s5stg", [128, 64])
        self.cp(stg[:].rearrange("p (a b t) -> p a b t", a=2, b=2), EZ[:, :, :, :, 4], ["EZ"], ["s5stg"])
        self.ld(self.cinA, stg[:], ["s5stg"], ["cinA"])
        self.gather(self.cinA, self.coutA, "cinA", "coutA")
        s5_finish(self, EZ, ET, HC, pw, full_block, sb, need_ctx)
        P.barrier()


def gather(self, cin, cout, rk, wk):
    self.P.collective(lambda e: e.collective_compute("AllGather", ALU.bypass, replica_groups=[list(range(8))],
                                                     ins=[cin.opt()], outs=[cout.opt()]),
                      reads=[rk], writes=[wk])


def s5_finish(self, EZ, ET, HC, pw, full_block, sb, need_ctx):
    fl = self.flags
    NB = 256
    G = sb("s5G", [128, 8, 64])
    self.ld(G[:], self.coutA.rearrange("(r p) c -> p r c", p=128), ["coutA"], ["s5G"])
    Gv = G[:].rearrange("p r (a b t) -> p r a b t", a=2, b=2)
    H = sb("s5H", [128, 2, 2, 16]); N1 = sb("s5N1", [128, 2, 16])
    ca1 = sb("ca1", [128, 16]); ca2 = sb("ca2", [128, 16])
    p1024 = pw["1024"]
    p256 = pw["256"]
    self.cp(H[:], HC[:], ["EZ"], ["s5H"])
    for d in range(2):
        order = range(8) if d == 0 else range(7, -1, -1)
        mo = 16 if d == 0 else 24
        hr, hi = H[:, d, 0, :], H[:, d, 1, :]
        pr, pi = p1024[0][:, d, :], p1024[1][:, d, :]
        n1, n2 = N1[:, 0, :], N1[:, 1, :]
        for j in order:
            self.cmul(n1, n2, pr, pi, hr, hi, ca1[:], ca2[:], ["s5H", "p1024"], ["s5N1"], "ca1", "ca2")
            self.tt(n1, n1, Gv[:, j, d, 0, :], ALU.add, ["s5N1", "s5G"], ["s5N1"])
            self.tt(n2, n2, Gv[:, j, d, 1, :], ALU.add, ["s5N1", "s5G"], ["s5N1"])
            self.tt(n1, n1, hr, ALU.subtract, ["s5N1", "s5H"], ["s5N1"])
            self.tt(n2, n2, hi, ALU.subtract, ["s5N1", "s5H"], ["s5N1"])
            mj = fl[:, mo + j:mo + j + 1]
            self.stt(hr, n1, mj, hr, ALU.mult, ALU.add, ["s5N1", "flags", "s5H"], ["s5H"])
            self.stt(hi, n2, mj, hi, ALU.mult, ALU.add, ["s5N1", "flags", "s5H"], ["s5H"])
    X = sb("s5X", [128, 2, 2, 16]); Y = sb("s5Y", [128, 2, 2, 16])
    self.cp(X[:], H[:], ["s5H"], ["s5X"])
    for k in range(4):
        self.tt(ET[:, :, :, :, k], EZ[:, :, :, :, k], X[:], ALU.add, ["EZ", "s5X"], ["ET"])
        if k < 3:
            for d in range(2):
                self.cmul(Y[:, d, 0, :], Y[:, d, 1, :], p256[0][:, d, :], p256[1][:, d, :], X[:, d, 0, :], X[:, d, 1, :],
                          ca1[:], ca2[:], ["s5X", "p256"], ["s5Y"], "ca1", "ca2")
            self.cp(X[:], Y[:], ["s5Y"], ["s5X"])
    for b in range(4):
        def inits(d, t, b=b):
            k = b if d == 0 else 3 - b
            return ET[:, d, 0, t, k:k + 1], ET[:, d, 1, t, k:k + 1], ["ET"]
        full_block(CTX + b * NB, C_LAT[0] + b * NB, inits)


KB.sincos = sincos
KB.cmul = cmul
KB.mix_s5 = mix_s5
KB.gather = gather
def mix_ssd(self, l, need_ctx):
    I = self.I
    P = self.P
    fl = self.flags
    NCH = 10
    with ExitStack() as st:
        sb = lambda n, s, dt=F32: self.sb(st, n, s, dt)
        XS = sb("XS", [128, 4, NR]); BM = sb("BM", [128, 2, NR], BF16); CMb = sb("CMb", [128, 2, NR], BF16)
        cw = sb("m2cw", [128, 8, 4]); cb = sb("m2cb", [128, 8])
        self.ld(cw[:], I["m2cw"][l], [], ["cw"]); self.ld(cb[:], I["m2cb"][l], [], ["cb"])
        identb = sb("identb", [128, 128], BF16)
        self.cp(identb[:], self.ident[:], ["ident"], ["identb"])
        with ExitStack() as st2:
            E = self.sb(st2, "cE", [128, LAT + 3]); acc = self.sb(st2, "cacc", [128, LAT])
            for j in range(8):
                r0 = 2560 + j * 128
                for (sname, Lr, mo) in (("lat", LAT, CTX), ("ctx", CTX, 0)):
                    if sname == "lat":
                        self.ld(E[:, 0:2], self.proj[r0:r0 + 128, C_HL[1] - 2:C_HL[1]], ["proj"], ["cE"])
                        self.ld(E[:, 2:2 + Lr], self.proj[r0:r0 + 128, C_LAT[0]:C_LAT[1]], ["proj"], ["cE"])
                        self.ld(E[:, 2 + Lr:3 + Lr], self.proj[r0:r0 + 128, C_HR[0]:C_HR[0] + 1], ["proj"], ["cE"], slow=True)
                        self.ts(E[:, 0:2], E[:, 0:2], fl[:, 32:33], None, ALU.mult, None, ["cE", "flags"], ["cE"])
                        self.ts(E[:, 2 + Lr:3 + Lr], E[:, 2 + Lr:3 + Lr], fl[:, 33:34], None, ALU.mult, None, ["cE", "flags"], ["cE"])
                    else:
                        self.vop("memset", [], ["cE"], E[:, 0:2], 0.0)
                        self.vop("memset", [], ["cE"], E[:, 2 + Lr:3 + Lr], 0.0)
                        self.ld(E[:, 2:2 + Lr], self.proj[r0:r0 + 128, 0:CTX], ["proj"], ["cE"])
                    self.ts(acc[:, 0:Lr], E[:, 0:Lr], cw[:, j, 0:1], None, ALU.mult, None, ["cE", "cw"], ["cacc"])
                    for k in range(1, 4):
                        self.stt(acc[:, 0:Lr], E[:, k:k + Lr], cw[:, j, k:k + 1], acc[:, 0:Lr], ALU.mult, ALU.add, ["cE", "cw", "cacc"], ["cacc"])
                    if j < 4:
                        dst, dk = XS[:, j, mo:mo + Lr], "XS"
                    elif j < 6:
                        dst, dk = BM[:, j - 4, mo:mo + Lr], "BM"
                    else:
                        dst, dk = CMb[:, j - 6, mo:mo + Lr], "CMb"
                    self.act(dst, acc[:, 0:Lr], AF.Silu, ["cacc", "cb"], [dk], bias=cb[:, j:j + 1])
            P.barrier()
        DT = sb("DT", [40, NR]); AC = sb("AC", [40, NR])
        TM = sb("TM", [128, NCH, 2, 40])
        ATB = sb("ATB", [128, NCH, 16]); EAT = sb("EAT", [128, NCH, 16])
        selh = sb("selh", [40, 16, 128])
        self.ld(selh[:], I["selh"], [], ["selh"])
        ones40 = sb("ones40", [40, 128])
        self.vop("memset", [], ["ones40"], ones40[:], 1.0)
        with ExitStack() as st2:
            dtb = self.sb(st2, "dtb", [40, 1]); alg = self.sb(st2, "alg", [40, 1]); av = self.sb(st2, "av", [40, 1])
            self.ld(dtb[:], I["m2dtb"][l], [], ["dtb"]); self.ld(alg[:], I["m2alog"][l], [], ["alg"])
            cmk = self.sb(st2, "cmk", [40, 2, NR])
            self.ld(cmk[:], I["cmask"], [], ["cmk"])
            self.vop("memset", [], ["DT"], DT[:], 0.0)
            for (p0, r0) in ((0, 3584), (32, 3592)):
                self.ld(DT[p0:p0 + 8, 0:CTX], self.proj[r0:r0 + 8, 0:CTX], ["proj", "DT"], ["DT"])
                self.ld(DT[p0:p0 + 8, CTX:NR], self.proj[r0:r0 + 8, C_LAT[0]:C_LAT[1]], ["proj", "DT"], ["DT"])
            self.act(DT[:], DT[:], AF.Exp, ["DT", "dtb"], ["DT"], bias=dtb[:, 0:1])
            self.act(DT[:], DT[:], AF.Ln, ["DT"], ["DT"], bias=self.epsc[0:40, 1:2])
            self.act(av[:], alg[:], AF.Exp, ["alg"], ["av"])
            self.ts(av[:], av[:], -1.0, None, ALU.mult, None, ["av"], ["av"])
            DTA = self.sb(st2, "DTA", [40, NR]); AC1 = self.sb(st2, "AC1", [40, NR])
            self.ts(DTA[:], DT[:], av[:, 0:1], None, ALU.mult, None, ["DT", "av"], ["DTA"])
            self.P.op("dve", lambda e: e.tensor_tensor_scan(out=AC[:], data0=cmk[:, 0, :], data1=DTA[:], initial=0.0,
                                                            op0=ALU.mult, op1=ALU.add), reads=["cmk", "DTA"], writes=["AC"])
            self.P.op("dve", lambda e: e.tensor_tensor_scan(out=AC1[:, ::-1], data0=cmk[:, 1, ::-1], data1=DTA[:, ::-1], initial=0.0,
                                                            op0=ALU.mult, op1=ALU.add), reads=["cmk", "DTA"], writes=["AC1"])
            self.cp(AC[32:40, :], AC1[32:40, :], ["AC1", "AC"], ["AC"])
            ATC = self.sb(st2, "ATC", [40, NCH]); RH = self.sb(st2, "RH", [40, NCH, 16])
            self.vop("memset", [], ["ATC"], ATC[:], 0.0)
            self.cp(ATC[0:8, :], AC[0:8, 127:NR:128], ["AC", "ATC"], ["ATC"])
            self.cp(ATC[32:40, :], AC[32:40, 0:NR:128], ["AC", "ATC"], ["ATC"])
            self.tt(RH[:], selh[:, :, 0:1].rearrange("p m o -> p o m").to_broadcast([40, NCH, 16]),
                    ATC[:].unsqueeze(2).to_broadcast([40, NCH, 16]), ALU.mult, ["selh", "ATC"], ["RH"])
            pa = self.ps(st2, "pa", [128, NCH * 16])
            self.mm(pa[:], ones40[:], RH[:].rearrange("p c m -> p (c m)"), True, True, ["ones40", "RH"], ["pa"])
            self.cp(ATB[:].rearrange("p c m -> p (c m)"), pa[:], ["pa"], ["ATB"])
            self.act(EAT[:], ATB[:], AF.Exp, ["ATB"], ["EAT"])
            pt = self.ps(st2, "ptm", [128, 2, 40])
            for c in range(NCH):
                cs = c * 128
                self.tr(pt[:, 0, :], DT[0:40, cs:cs + 128], self.ident[0:40, 0:40], ["DT", "ident"], ["ptm"])
                self.tr(pt[:, 1, :], AC[0:40, cs:cs + 128], self.ident[0:40, 0:40], ["AC", "ident"], ["ptm"])
                self.cp(TM[:, c], pt[:], ["ptm"], ["TM"])
            P.barrier()
        XD = sb("XD", [128, NCH, 2, 512], BF16)
        CS = sb("CS", [128, NCH, 2, 512], BF16)
        c0 = 0
        with ExitStack() as st2:
            pxs = [self.ps(st2, "pxs", [128, 512]) for _ in range(2)]
            pbc = self.ps(st2, "pbc", [128, 2, 128], BF16)
            pcs = [self.ps(st2, "pcs", [128, 512]) for _ in range(2)]
            bct = self.sb(st2, "bct", [128, 2, 128], BF16)
            de = self.sb(st2, "de", [128, 16]); cf = self.sb(st2, "cf", [128, 16])
            xdd = self.sb(st2, "xdd", [128, 2, 512], BF16)
            for c in range(NCH):
                cs = c * 128
                px = pxs[c % 2]; pxk = "pxs%d" % (c % 2)
                for j in range(4):
                    self.tr(px[:, j * 128:(j + 1) * 128], XS[:, j, cs:cs + 128], self.ident[:], ["XS", "ident"], [pxk])
                for g in range(2):
                    self.tr(pbc[:, g, :], BM[:, g, cs:cs + 128], identb[:], ["BM", "identb"], ["pbc"])
                self.cp(bct[:], pbc[:], ["pbc"], ["bct"])
                dt16 = TM[:, c, 0, :].rearrange("p (a b) -> p a b", b=8)[:, 0:5:4, :]
                ac16 = TM[:, c, 1, :].rearrange("p (a b) -> p a b", b=8)[:, 0:5:4, :]
                self.tt(de[:].rearrange("p (a b) -> p a b", b=8), ATB[:, c, :].rearrange("p (a b) -> p a b", b=8), ac16,
                        ALU.subtract, ["ATB", "TM"], ["de"])
                self.act(de[:], de[:], AF.Exp, ["de"], ["de"])
                self.tt(cf[:].rearrange("p (a b) -> p a b", b=8), de[:].rearrange("p (a b) -> p a b", b=8), dt16, ALU.mult,
                        ["de", "TM"], ["cf"])
                pxv = px[:].rearrange("p (h e) -> p h e", e=64)
                for d in range(2):
                    self.tt(XD[:, c, d, :].rearrange("p (h e) -> p h e", e=64), pxv,
                            dt16[:, d, :].unsqueeze(2).to_broadcast([128, 8, 64]), ALU.mult, [pxk, "TM"], ["XD"])
                    self.tt(xdd[:, d, :].rearrange("p (h e) -> p h e", e=64), pxv,
                            cf[:, d * 8:(d + 1) * 8].unsqueeze(2).to_broadcast([128, 8, 64]), ALU.mult, [pxk, "cf"], ["xdd"])
                for d in range(2):
                    pc = pcs[d]; pck = "pcs%d" % d
                    for g in range(2):
                        self.mm(pc[:, g * 256:(g + 1) * 256], bct[:, g, :], xdd[:, d, g * 256:(g + 1) * 256], True, True,
                                ["bct", "xdd"], [pck])
                    self.cp(CS[:, c, d, :], pc[:], [pck], ["CS"], eng="act")
            P.barrier()
        S = sb("Sst", [128, 2, 512]); HCs = sb("HCs", [128, 2, 512])
        stg = sb("dstg", [128, 1040])

        def chain(dst, d, order, init_key):
            for c in order:
                v = dst[:, d, :].rearrange("p (h e) -> p h e", e=64)
                self.tt(v, v, EAT[:, c, d * 8:(d + 1) * 8].unsqueeze(2).to_broadcast([128, 8, 64]), ALU.mult,
                        [init_key, "EAT"], [init_key])
                self.tt(dst[:, d, :], dst[:, d, :], CS[:, c, d, :], ALU.add, [init_key, "CS"], [init_key])

        self.vop("memset", [], ["Sst"], S[:], 0.0)
        self.vop("memset", [], ["HCs"], HCs[:], 0.0)
        chain(S, 0, range(2, 10), "Sst")
        chain(S, 1, range(9, 1, -1), "Sst")
        chain(HCs, 0, (0, 1), "HCs")
        chain(HCs, 1, (1, 0), "HCs")
        self.cp(stg[:, 0:1024], S[:].rearrange("p d n -> p (d n)"), ["Sst"], ["dstg"])
        self.vop("tensor_reduce", ["ATB"], ["dstg"], out=stg[:, 1024:1040], in_=ATB[:, 2:10, :].rearrange("p c m -> p m c"),
                 axis=mybir.AxisListType.X, op=ALU.add)
        self.ld(self.cinD, stg[:], ["dstg"], ["cinD"])
        self.gather(self.cinD, self.coutD, "cinD", "coutD")
        H = sb("Hin", [128, 2, 512])
        with ExitStack() as st2:
            cod = self.coutD.rearrange("(r p) c -> p r c", p=128)
            GT = self.sb(st2, "GT", [128, 8, 16])
            self.ld(GT[:], cod[:, :, 1024:1040], ["coutD"], ["GT"])
            EXG = self.sb(st2, "EXG", [128, 8, 16])
            self.act(EXG[:], GT[:], AF.Exp, ["GT"], ["EXG"])
            N = self.sb(st2, "Nn", [128, 512])
            Gj = [self.sb(st2, "Gj", [128, 512]) for _ in range(2)]
            self.cp(H[:], HCs[:], ["HCs"], ["Hin"])
            nn = 0
            for d in range(2):
                order = range(8) if d == 0 else range(7, -1, -1)
                mo = 16 if d == 0 else 24
                for j in order:
                    gj = Gj[nn % 2]; gk = "Gj%d" % (nn % 2)
                    nn += 1
                    self.ld(gj[:], cod[:, j, d * 512:(d + 1) * 512], ["coutD"], [gk])
                    self.tt(N[:].rearrange("p (h e) -> p h e", e=64), H[:, d, :].rearrange("p (h e) -> p h e", e=64),
                            EXG[:, j, d * 8:(d + 1) * 8].unsqueeze(2).to_broadcast([128, 8, 64]), ALU.mult, ["Hin", "EXG"], ["Nn"])
                    self.tt(N[:], N[:], gj[:], ALU.add, ["Nn", gk], ["Nn"])
                    self.tt(N[:], N[:], H[:, d, :], ALU.subtract, ["Nn", "Hin"], ["Nn"])
                    self.stt(H[:, d, :], N[:], fl[:, mo + j:mo + j + 1], H[:, d, :], ALU.mult, ALU.add, ["Nn", "flags", "Hin"], ["Hin"])
            P.barrier()
        SP = CS
        Tt = sb("Ttmp", [128, 512])

        def chain2(init_ap, d, order, ik):
            self.cp(S[:, d, :], init_ap, [ik, "Sst"], ["Sst"])
            for c in order:
                self.tt(Tt[:].rearrange("p (h e) -> p h e", e=64), S[:, d, :].rearrange("p (h e) -> p h e", e=64),
                        EAT[:, c, d * 8:(d + 1) * 8].unsqueeze(2).to_broadcast([128, 8, 64]), ALU.mult, ["Sst", "EAT"], ["Tt"])
                self.tt(Tt[:], Tt[:], CS[:, c, d, :], ALU.add, ["Tt", "CS"], ["Tt"])
                self.cp(CS[:, c, d, :], S[:, d, :], ["Sst", "CS"], ["CS"])
                self.cp(S[:, d, :], Tt[:], ["Tt"], ["Sst"])

        chain2(H[:, 0, :], 0, range(2, 10), "Hin")
        chain2(H[:, 1, :], 1, range(9, 1, -1), "Hin")
        if need_ctx:
            zz = sb("zz", [128, 512])
            self.vop("memset", [], ["zz"], zz[:], 0.0)
            chain2(zz[:], 0, (0, 1), "zz")
            chain2(zz[:], 1, (1, 0), "zz")
        P.barrier()
        mk = sb("mk", [128, 2, 128])
        self.ld(mk[:, 0, :], I["maskF"], [], ["mk"]); self.ld(mk[:, 1, :], I["maskB"], [], ["mk"])
        md = sb("m2d", [128, 4]); nw = sb("m2nw", [128, 4])
        self.ld(md[:], I["m2d"][l], [], ["md"]); self.ld(nw[:], I["m2nw"][l], [], ["nw"])
        onesn = sb("onesn", [128, 128])
        self.vop("memset", [], ["onesn"], onesn[:], 1.0 / 256.0)
        pg = self.ps(st, "pg", [128, 2, 128])
        prs = [self.ps(st, "pr", [128, 128]) for _ in range(2)]
        pys = [self.ps(st, "py", [128, 128]) for _ in range(2)]
        pms = self.ps(st, "pms", [128, 128])
        Gm = sb("Gm", [128, 2, 2, 128])
        E1 = sb("E1", [128, 128]); STb = sb("STb", [128, 128], BF16); ER = sb("ER", [128, 128]); ccs = sb("ccs", [128, 128], BF16)
        Y = sb("Yt", [128, 4, 128]); Z = sb("Zt", [128, 4, 128]); gsq = sb("gsq", [128, 4, 128]); rsd = sb("rsd", [128, 128])
        n = 0
        for c in range(0 if need_ctx else 2, NCH):
            cs = c * 128
            pcol = cs if c < 2 else C_LAT[0] + (c - 2) * 128
            for g in range(2):
                self.mm(pg[:, g, :], BM[:, g, cs:cs + 128], CMb[:, g, cs:cs + 128], True, True, ["BM", "CMb"], ["pg"])
            for d in range(2):
                for g in range(2):
                    self.tt(Gm[:, d, g, :], pg[:, g, :], mk[:, d, :], ALU.mult, ["pg", "mk"], ["Gm"])
            ac16 = TM[:, c, 1, :].rearrange("p (a b) -> p a b", b=8)[:, 0:5:4, :]
            for hp in range(4):
                py = pys[hp % 2]; pyk = "py%d" % (hp % 2)
                for hh in range(2):
                    h = hp * 2 + hh
                    g = h // 4
                    for d in range(2):
                        pr = prs[n % 2]; prk = "pr%d" % (n % 2)
                        n += 1
                        self.mm(pr[:], selh[:, d * 8 + h, :], AC[0:40, cs:cs + 128], True, True, ["selh", "AC"], [prk])
                        self.ts(E1[:], pr[:], ac16[:, d, h:h + 1], 0.0, ALU.subtract, ALU.min, [prk, "TM"], ["E1"])
                        self.act(E1[:], E1[:], AF.Exp, ["E1"], ["E1"])
                        self.tt(STb[:], E1[:], Gm[:, d, g, :], ALU.mult, ["E1", "Gm"], ["STb"])
                        self.act(ER[:], pr[:], AF.Exp, [prk], ["ER"])
                        self.tt(ccs[:], CMb[:, g, cs:cs + 128], ER[:], ALU.mult, ["CMb", "ER"], ["ccs"])
                        self.mm(py[hh * 64:(hh + 1) * 64, :], XD[:, c, d, h * 64:(h + 1) * 64], STb[:], d == 0, False, ["XD", "STb"], [pyk])
                        self.mm(py[hh * 64:(hh + 1) * 64, :], SP[:, c, d, h * 64:(h + 1) * 64], ccs[:], False, d == 1, ["CS", "ccs"], [pyk])
                self.stt(Y[:, hp, :], XS[:, hp, cs:cs + 128], md[:, hp:hp + 1], py[:], ALU.mult, ALU.add, ["XS", "md", pyk], ["Yt"])
            self.ld(Z[:], self.proj[2048:2560, pcol:pcol + 128].rearrange("(j p) n -> p j n", p=128), ["proj"], ["Zt"])
            self.act(Z[:], Z[:], AF.Silu, ["Zt"], ["Zt"])
            self.tt(Y[:], Y[:], Z[:], ALU.mult, ["Yt", "Zt"], ["Yt"])
            self.tt(gsq[:], Y[:], Y[:], ALU.mult, ["Yt"], ["gsq"])
            for gg in range(2):
                self.mm(pms[:], onesn[:], gsq[:, 2 * gg, :], True, False, ["onesn", "gsq"], ["pms"])
                self.mm(pms[:], onesn[:], gsq[:, 2 * gg + 1, :], False, True, ["onesn", "gsq"], ["pms"])
                self.act(rsd[:], pms[:], AF.Sqrt, ["pms"], ["rsd"], bias=self.epsc[:, 0:1])
                self.vop("reciprocal", ["rsd"], ["rsd"], out=rsd[:], in_=rsd[:])
                for jj in range(2):
                    j = 2 * gg + jj
                    self.stt(self.mix[:, 12 + j, cs:cs + 128], Y[:, j, :], nw[:, j:j + 1], rsd[:], ALU.mult, ALU.mult,
                             ["Yt", "nw", "rsd"], ["mix"])
        P.barrier()


KB.mix_ssd = mix_ssd
def ln_apply(self, st, R, rkey, blocks, lnp, gi, bi, writer):
    onesD = self.sb(st, "onesD", [128, 128])
    self.vop("memset", [], ["onesD"], onesD[:], 1.0 / D)
    pm = self.ps(st, "lnpm", [128, 512]); pq = self.ps(st, "lnpq", [128, 512])
    sq = [self.sb(st, "lnsq", [128, 512]) for _ in range(2)]
    mean = self.sb(st, "lnmean", [128, 512]); rstd = self.sb(st, "lnrstd", [128, 512]); tmp = self.sb(st, "lntmp", [128, 512])
    outs = [self.sb(st, "lnout", [128, 512]) for _ in range(2)]
    for (a, b) in blocks:
        n = b - a
        for k in range(16):
            self.mm(pm[:, 0:n], onesD[:], R[:, k, a:b], k == 0, k == 15, ["onesD", rkey], ["lnpm"])
        for k in range(16):
            s_ = sq[k % 2]; sk = "lnsq%d" % (k % 2)
            self.act(s_[:, 0:n], R[:, k, a:b], AF.Square, [rkey], [sk])
            self.mm(pq[:, 0:n], onesD[:], s_[:, 0:n], k == 0, k == 15, ["onesD", sk], ["lnpq"])
        self.cp(mean[:, 0:n], pm[:, 0:n], ["lnpm"], ["lnmean"])
        self.tt(tmp[:, 0:n], mean[:, 0:n], mean[:, 0:n], ALU.mult, ["lnmean"], ["lntmp"])
        self.tt(tmp[:, 0:n], pq[:, 0:n], tmp[:, 0:n], ALU.subtract, ["lnpq", "lntmp"], ["lntmp"])
        self.act(rstd[:, 0:n], tmp[:, 0:n], AF.Sqrt, ["lntmp"], ["lnrstd"], bias=self.epsc[:, 0:1])
        self.vop("reciprocal", ["lnrstd"], ["lnrstd"], out=rstd[:, 0:n], in_=rstd[:, 0:n])
        for k in range(16):
            o = outs[k % 2]; ok = "lnout%d" % (k % 2)
            self.tt(o[:, 0:n], R[:, k, a:b], mean[:, 0:n], ALU.subtract, [rkey, "lnmean"], [ok])
            self.tt(o[:, 0:n], o[:, 0:n], rstd[:, 0:n], ALU.mult, [ok, "lnrstd"], [ok])
            self.ts(o[:, 0:n], o[:, 0:n], lnp[:, gi, k:k + 1], lnp[:, bi, k:k + 1], ALU.mult, ALU.add, [ok, "lnp"], [ok])
            writer(k, a, b, o[:, 0:n], ok)


def phase_outproj(self, l):
    P = self.P
    I = self.I
    need_ctx = l < NL - 1
    blocks = RB if need_ctx else RB[1:]
    with ExitStack() as st:
        lnp = self.sb(st, "lnp", [128, 4, 16])
        self.ld(lnp[:], I["ln_p"][l], [], ["lnp"])
        R = self.sb(st, "Rres", [128, 16, NR])
        with ExitStack() as st2:
            wv = self.W["w_out"][l].rearrange("(k p) n -> p k n", p=128)
            slabs = [self.sb(st2, "ow", [128, 16, 512], BF16) for _ in range(2)]
            hst = [self.sb(st2, "ohst", [128, NR]) for _ in range(2)]
            pss = [self.ps(st2, "ops", [128, 512]) for _ in range(4)]
            pi = 0
            for cb in range(4):
                sl = slabs[cb % 2]; sk = "ow%d" % (cb % 2)
                self.ld(sl[:], wv[:, :, cb * 512:(cb + 1) * 512], [], [sk])
                for mi in range(4):
                    m = cb * 4 + mi
                    hs = hst[m % 2]; hk = "ohst%d" % (m % 2)
                    if need_ctx:
                        self.ld(hs[:, 0:CTX], self.hbuf[:, m, 0:CTX], ["hbuf"], [hk])
                    self.ld(hs[:, CTX:NR], self.hbuf[:, m, C_LAT[0]:C_LAT[1]], ["hbuf"], [hk])
                    a0 = blocks[0][0]
                    self.act(hs[:, a0:NR], hs[:, a0:NR], AF.Copy, [hk], [hk], scale=ALPHA)
                    for (a, b) in blocks:
                        pt = pss[pi % 4]; pk = "ops%d" % (pi % 4)
                        pi += 1
                        for k in range(16):
                            self.mm(pt[:, 0:b - a], sl[:, k, mi * 128:(mi + 1) * 128], self.mix[:, k, a:b], k == 0, k == 15,
                                    [sk, "mix"], [pk])
                        col = 1 if a < CTX else 0
                        self.stt(R[:, m, a:b], pt[:, 0:b - a], self.mod[:, 32 + m, col:col + 1], hs[:, a:b], ALU.mult, ALU.add,
                                 [pk, "mod", hk], ["Rres"])
            P.barrier()
        with ExitStack() as st2:
            def writer(k, a, b, tile, key):
                self.ld(self.h1buf[:, k, a:b], tile, [key], ["h1buf"])
            self.ln_apply(st2, R, "Rres", blocks, lnp, 0, 1, writer)
            P.barrier()


def phase_ffn(self, l):
    P = self.P
    I = self.I
    fl = self.flags
    need_ctx = l < NL - 1
    h1 = self.h1buf
    with ExitStack() as st:
        self.exchange(st, h1[:, :, CTX:CTX + 64], h1[:, :, NR - 64:NR], h1[:, :, NR:NR + 64], h1[:, :, NR + 64:NR + 128],
                      64, self.cin2, self.cout2, ["h1buf"], ["h1buf"], "e2")
    NE = NR + 128
    with ExitStack() as st:
        hm2 = self.sb(st, "hm2", [128, 16, NE], BF16)
        fcw = self.sb(st, "fcw", [128, NFT, 9]); fcb = self.sb(st, "fcb", [128, NFT])
        self.ld(fcw[:], I["fcw"][l], [], ["fcw"]); self.ld(fcb[:], I["fcb"][l], [], ["fcb"])
        with ExitStack() as st2:
            hst = [self.sb(st2, "fhst", [128, NE]) for _ in range(2)]
            a0 = 0 if need_ctx else CTX
            for k in range(16):
                hs = hst[k % 2]; hk = "fhst%d" % (k % 2)
                self.ld(hs[:, a0:NE], h1[:, k, a0:NE], ["h1buf"], [hk])
                if need_ctx:
                    self.ts(hm2[:, k, 0:CTX], hs[:, 0:CTX], self.mod1p[:, 64 + k, 1:2], self.mod[:, 48 + k, 1:2], ALU.mult, ALU.add,
                            [hk, "mod", "mod1p"], ["hm2"])
                self.ts(hm2[:, k, CTX:NE], hs[:, CTX:NE], self.mod1p[:, 64 + k, 0:1], self.mod[:, 48 + k, 0:1], ALU.mult, ALU.add,
                        [hk, "mod", "mod1p"], ["hm2"])
            P.barrier()
        wv = self.W["ffn_w_up"][l].rearrange("(k p) n -> p k n", p=128)
        slabs = [self.sb(st, "uw", [128, 16, 256], BF16) for _ in range(2)]
        G0s = [self.ps(st, "fG0", [128, 512]) for _ in range(2)]
        G1s = [self.ps(st, "fG1", [128, 128]) for _ in range(2)]
        Vs = [self.ps(st, "fV", [128, 512]) for _ in range(2)]
        gsb = self.sb(st, "gsb", [128, 640]); cv = self.sb(st, "fcv", [128, 512]); gt = self.sb(st, "fgt", [128, 512])
        acts = [self.sb(st, "fact", [128, 512], BF16) for _ in range(2)]
        T, Bo = NR, NR + 64
        latblocks = [
            ([(T, T + 64, 0), (CTX, CTX + 448, 64)], [(CTX + 448, CTX + 576, 0)], (CTX, CTX + 512), 32, None),
            ([(CTX + 448, CTX + 960, 0)], [(CTX + 960, CTX + 1024, 0), (Bo, Bo + 64, 64)], (CTX + 512, CTX + 1024), None, 33),
        ]
        n = 0
        for f in range(NFT):
            sl = slabs[f % 2]; sk = "uw%d" % (f % 2)
            self.ld(sl[:, :, 0:128], wv[:, :, f * 128:(f + 1) * 128], [], [sk])
            self.ld(sl[:, :, 128:256], wv[:, :, FH + f * 128:FH + (f + 1) * 128], [], [sk])
            for (g0p, g1p, (va, vb), mtop, mbot) in latblocks:
                G0 = G0s[n % 2]; G1 = G1s[n % 2]; V = Vs[n % 2]
                k0, k1, kv = "fG0%d" % (n % 2), "fG1%d" % (n % 2), "fV%d" % (n % 2)
                ab = acts[n % 2]; ak = "fact%d" % (n % 2)
                n += 1
                for (pt, pk, pieces) in ((G0, k0, g0p), (G1, k1, g1p)):
                    for (a, b, off) in pieces:
                        for k in range(16):
                            self.mm(pt[:, off:off + b - a], sl[:, k, 0:128], hm2[:, k, a:b], k == 0, k == 15, [sk, "hm2"], [pk])
                for k in range(16):
                    self.mm(V[:], sl[:, k, 128:256], hm2[:, k, va:vb], k == 0, k == 15, [sk, "hm2"], [kv])
                self.cp(gsb[:, 0:512], G0[:], [k0], ["gsb"], eng="act")
                self.cp(gsb[:, 512:640], G1[:], [k1], ["gsb"], eng="act")
                if mtop is not None:
                    self.ts(gsb[:, 0:64], gsb[:, 0:64], fl[:, mtop:mtop + 1], None, ALU.mult, None, ["gsb", "flags"], ["gsb"])
                if mbot is not None:
                    self.ts(gsb[:, 576:640], gsb[:, 576:640], fl[:, mbot:mbot + 1], None, ALU.mult, None, ["gsb", "flags"], ["gsb"])
                g3 = gsb[:].rearrange("p (r c) -> p r c", c=64)
                o3 = cv[:].rearrange("p (r c) -> p r c", c=64)
                self.ts(o3, g3[:, 1:9, :], fcw[:, f, 4:5], None, ALU.mult, None, ["gsb", "fcw"], ["fcv"])
                for dy in range(3):
                    for dx in range(3):
                        if dy == 1 and dx == 1:
                            continue
                        if dx == 0:
                            src, dst = g3[:, dy:dy + 8, 0:63], o3[:, :, 1:64]
                        elif dx == 1:
                            src, dst = g3[:, dy:dy + 8, :], o3
                        else:
                            src, dst = g3[:, dy:dy + 8, 1:64], o3[:, :, 0:63]
                        w = fcw[:, f, dy * 3 + dx:dy * 3 + dx + 1]
                        self.stt(dst, src, w, dst, ALU.mult, ALU.add, ["gsb", "fcw", "fcv"], ["fcv"])
                self.ts(cv[:], cv[:], fcb[:, f:f + 1], None, ALU.add, None, ["fcv", "fcb"], ["fcv"])
                self.gelu(cv[:], cv[:], gt[:], ["fcv"], ["fcv"], "fgt")
                self.tt(ab[:], cv[:], V[:], ALU.mult, ["fcv", kv], [ak])
                self.ld(self.actbuf[f, :, va:vb], ab[:], [ak], ["actbuf"])
            if need_ctx:
                G0 = G0s[n % 2]; V = Vs[n % 2]
                k0, kv = "fG0%d" % (n % 2), "fV%d" % (n % 2)
                ab = acts[n % 2]; ak = "fact%d" % (n % 2)
                n += 1
                for k in range(16):
                    self.mm(G0[:, 0:CTX], sl[:, k, 0:128], hm2[:, k, 0:CTX], k == 0, k == 15, [sk, "hm2"], [k0])
                for k in range(16):
                    self.mm(V[:, 0:CTX], sl[:, k, 128:256], hm2[:, k, 0:CTX], k == 0, k == 15, [sk, "hm2"], [kv])
                self.cp(gsb[:, 0:CTX], G0[:, 0:CTX], [k0], ["gsb"], eng="act")
                self.ts(cv[:, 0:CTX], gsb[:, 0:CTX], fcw[:, f, 4:5], None, ALU.mult, None, ["gsb", "fcw"], ["fcv"])
                self.stt(cv[:, 1:CTX], gsb[:, 0:CTX - 1], fcw[:, f, 3:4], cv[:, 1:CTX], ALU.mult, ALU.add, ["gsb", "fcw", "fcv"], ["fcv"])
                self.stt(cv[:, 0:CTX - 1], gsb[:, 1:CTX], fcw[:, f, 5:6], cv[:, 0:CTX - 1], ALU.mult, ALU.add, ["gsb", "fcw", "fcv"], ["fcv"])
                self.ts(cv[:, 0:CTX], cv[:, 0:CTX], fcb[:, f:f + 1], None, ALU.add, None, ["fcv", "fcb"], ["fcv"])
                self.gelu(cv[:, 0:CTX], cv[:, 0:CTX], gt[:, 0:CTX], ["fcv"], ["fcv"], "fgt")
                self.tt(ab[:, 0:CTX], cv[:, 0:CTX], V[:, 0:CTX], ALU.mult, ["fcv", kv], [ak])
                self.ld(self.actbuf[f, :, 0:CTX], ab[:, 0:CTX], [ak], ["actbuf"])
        P.barrier()
    blocks = RB if need_ctx else RB[1:]
    a0 = blocks[0][0]
    with ExitStack() as st:
        A = self.sb(st, "Aact", [128, NFT, NR], BF16)
        for f in range(NFT):
            self.ld(A[:, f, a0:NR], self.actbuf[f, :, a0:NR], ["actbuf"], ["Aact"], q=("sp" if f % 2 == 0 else "act"))
        wv = self.W["ffn_w_down"][l].rearrange("(f p) n -> p f n", p=128)
        slabs = [self.sb(st, "dw", [128, NFT, 128], BF16) for _ in range(2)]
        hst = [self.sb(st, "dhst", [128, NR]) for _ in range(2)]
        pss = [self.ps(st, "dps", [128, 512]) for _ in range(4)]
        pi = 0
        for m in range(16):
            sl = slabs[m % 2]; sk = "dw%d" % (m % 2)
            self.ld(sl[:], wv[:, :, m * 128:(m + 1) * 128], [], [sk])
            hs = hst[m % 2]; hk = "dhst%d" % (m % 2)
            self.ld(hs[:, a0:NR], h1buf_real(self, m, a0), ["h1buf"], [hk])
            self.act(hs[:, a0:NR], hs[:, a0:NR], AF.Copy, [hk], [hk], scale=ALPHA)
            for (a, b) in blocks:
                pt = pss[pi % 4]; pk = "dps%d" % (pi % 4)
                pi += 1
                for f in range(NFT):
                    self.mm(pt[:, 0:b - a], sl[:, f, :], A[:, f, a:b], f == 0, f == NFT - 1, [sk, "Aact"], [pk])
                col = 1 if a < CTX else 0
                self.stt(hs[:, a:b], pt[:, 0:b - a], self.mod[:, 80 + m, col:col + 1], hs[:, a:b], ALU.mult, ALU.add,
                         [pk, "mod", hk], [hk])
            self.ld(self.r2buf[:, m, a0:NR], hs[:, a0:NR], [hk], ["r2buf"])
        P.barrier()
    with ExitStack() as st:
        lnp = self.sb(st, "lnp2", [128, 4, 16])
        self.ld(lnp[:], I["ln_p"][l], [], ["lnp"])
        R = self.sb(st, "R2", [128, 16, NR])
        for k in range(16):
            self.ld(R[:, k, a0:NR], self.r2buf[:, k, a0:NR], ["r2buf"], ["R2"], q=("sp" if k % 2 == 0 else "act"))

        def writer(k, a, b, tile, key):
            if a < CTX:
                self.ld(self.hbuf[:, k, a:b], tile, [key], ["hbuf"])
            else:
                self.ld(self.hbuf[:, k, C_LAT[0] + a - CTX:C_LAT[0] + b - CTX], tile, [key], ["hbuf"])
        self.ln_apply(st, R, "R2", blocks, lnp, 2, 3, writer)
        P.barrier()


def h1buf_real(self, m, a0):
    return self.h1buf[:, m, a0:NR]


KB.ln_apply = ln_apply
KB.phase_outproj = phase_outproj
KB.phase_ffn = phase_ffn
def build(com, per0, nlayers=NL, dbg=(), stop=None):
    kb = KB(com, per0, nlayers, dbg)
    P = kb.P
    with ExitStack() as st:
        kb.phase_setup(st)
        for l in range(nlayers):
            if stop == "setup":
                kb.dump("hbuf", kb.hbuf, [128, 16, NT], "hbuf")
                break
            kb.phase_e0()
            if stop == "e0":
                kb.dump("hbuf", kb.hbuf, [128, 16, NT], "hbuf")
                break
            kb.phase_ada(l)
            if stop == "ada":
                break
            kb.phase_inproj(l)
            if l == 0:
                kb.dump("proj0", kb.proj[0:INW], [INW, NT], "proj")
            if stop == "inproj":
                break
            with ExitStack() as stl:
                kb.mix = kb.sb(stl, "mix", [128, 16, NR], BF16)
                kb.phase_mixers(l)
                if stop == "mix":
                    break
                kb.phase_outproj(l)
            if stop == "ln1":
                kb.dump("h1", kb.h1buf[:, :, 0:NR], [128, 16, NR], "h1buf")
                break
            kb.phase_ffn(l)
            if stop == "ffn":
                kb.dump("h2", kb.hbuf, [128, 16, NT], "hbuf")
                break
        if stop is None:
            kb.ld(kb.out, kb.hbuf[:, :, C_LAT[0]:C_LAT[1]], ["hbuf"], ["out"])
        P.emit()
    return kb


_CACHE = {}


def run(inputs, nlayers=NL, dbg=(), stop=None):
    com, per = host_prep({k: np.asarray(v) for k, v in inputs.items()})
    kb = build(com, per[0], nlayers, dbg, stop)
    print("ninstr", kb.P.ninstr(), flush=True)
    in_maps = [dict(com, **per[c]) for c in range(8)]
    res = run_bass_kernel_spmd(kb.nc, in_maps, core_ids=list(range(8)))
    return res.results


def kernel(**inputs):
    r = run(inputs)
    out = np.zeros((2, 4096, D), np.float32)
    for c in range(8):
        b, k = divmod(c, 4)
        o = r[c]["out"]
        out[b, k * LAT:(k + 1) * LAT, :] = o.transpose(2, 1, 0).reshape(LAT, D)
    return out
```

```python
import math
from contextlib import ExitStack
import numpy as np
import concourse.bass as bass
import concourse.mybir as mybir
from concourse.bass_utils import run_bass_kernel_spmd

D = 2048
KT = 16
CTX = 256
LAT = 1024
HAL = 8
NT = CTX + 2 * HAL + LAT
C_CTX = (0, 256)
C_HL = (256, 264)
C_HR = (264, 272)
C_LAT = (272, 1296)
NR = CTX + LAT
INW = 3600
NMT = 29
FH = 5632
NFT = 44
ALPHA = 4 ** 0.25
LN_EPS = 1e-5
NL = 2
TWO_PI = 2.0 * math.pi
BIGW = ("w_in", "w_out", "ffn_w_up", "ffn_w_down")


def host_prep(inp):
    f = np.float32
    L = NL
    com = {}
    for k in ("s5_glu_w", "pool_w"):
        com[k] = np.ascontiguousarray(inp[k], dtype=f)
    cv3 = np.stack([inp["c"][0].reshape(16, 128).T, inp["c"][1].reshape(16, 128).T, inp["c_ctx"].reshape(16, 128).T], 2)
    com["cvec3"] = np.ascontiguousarray(cv3, dtype=f)
    lnp = np.stack([inp[k].reshape(L, 16, 128).transpose(0, 2, 1) for k in ("ln1_g", "ln1_b", "ln2_g", "ln2_b")], 2)
    com["ln_p"] = np.ascontiguousarray(lnp)
    s5p = np.zeros((L, 2, 128, 3, 16), f)
    for l in range(L):
        for d in range(2):
            s5p[l, d, :, 0, :] = inp["s5_a_re"][l, d].reshape(16, 128).T
            s5p[l, d, :, 1, :] = inp["s5_a_im"][l, d].reshape(16, 128).T
            s5p[l, d, :, 2, :] = np.repeat(inp["s5_log_step"][l, d], 64).reshape(16, 128).T
    com["s5p"] = s5p
    BT = np.zeros((L, 2, 2, 128, 4, 2, 128), f)
    CT = np.zeros((L, 2, 2, 128, 16, 64), f)
    for l in range(L):
        for d in range(2):
            for ri, (bk, ck) in enumerate((("s5_b_re", "s5_c_re"), ("s5_b_im", "s5_c_im"))):
                Bm = inp[bk][l, d]
                Cm = inp[ck][l, d]
                for t in range(16):
                    q, r = divmod(t, 4)
                    rb, e = divmod(r, 2)
                    for gi in range(2):
                        g = 2 * t + gi
                        p0 = 64 * rb + 32 * e + gi * 16
                        BT[l, d, ri, p0:p0 + 16, q, e, gi * 64:(gi + 1) * 64] = Bm[g].T
                        CT[l, d, ri, gi * 64:(gi + 1) * 64, t, 32 * e + gi * 16:32 * e + gi * 16 + 16] = Cm[g].T
    com["s5BT"] = BT
    com["s5CT"] = CT
    com["s5d"] = np.ascontiguousarray(inp["s5_d"].reshape(L, 4, 128).transpose(0, 2, 1))
    com["s5glub"] = np.ascontiguousarray(inp["s5_glu_b"].reshape(L, 4, 128).transpose(0, 2, 1))
    com["sglnG"] = np.ascontiguousarray(np.broadcast_to(inp["sg_ln_g"].reshape(L, 1, 4, 128), (L, 128, 4, 128)))
    com["sglnB"] = np.ascontiguousarray(np.broadcast_to(inp["sg_ln_b"].reshape(L, 1, 4, 128), (L, 128, 4, 128)))
    com["sgwT"] = np.ascontiguousarray(inp["sg_w"].transpose(0, 3, 1, 2))
    com["sgb"] = np.ascontiguousarray(np.broadcast_to(inp["sg_b"].reshape(L, 1, 4, 128), (L, 128, 4, 128)))
    com["poolb"] = np.ascontiguousarray(inp["pool_b"].reshape(L, 4, 128).transpose(0, 2, 1))
    com["pools"] = np.ascontiguousarray(inp["pool_scale"].reshape(L, 4, 128).transpose(0, 2, 1))
    com["m2cw"] = np.ascontiguousarray(inp["m2_conv_w"].reshape(L, 4, 8, 128).transpose(0, 3, 2, 1))
    com["m2cb"] = np.ascontiguousarray(inp["m2_conv_b"].reshape(L, 8, 128).transpose(0, 2, 1))
    dtb = np.zeros((L, 40, 1), f)
    alog = np.zeros((L, 40, 1), f)
    for l in range(L):
        dtb[l, 0:8, 0] = inp["m2_dt_bias"][l, 0]
        dtb[l, 32:40, 0] = inp["m2_dt_bias"][l, 1]
        alog[l, 0:8, 0] = inp["m2_a_log"][l, 0]
        alog[l, 32:40, 0] = inp["m2_a_log"][l, 1]
    com["m2dtb"] = dtb
    com["m2alog"] = alog
    com["m2d"] = np.ascontiguousarray(np.repeat(inp["m2_d"], 64, axis=1).reshape(L, 4, 128).transpose(0, 2, 1))
    com["m2nw"] = np.ascontiguousarray(inp["m2_norm_w"].reshape(L, 4, 128).transpose(0, 2, 1))
    com["fcw"] = np.ascontiguousarray(inp["ffn_conv_w"].reshape(L, 9, NFT, 128).transpose(0, 3, 2, 1))
    com["fcb"] = np.ascontiguousarray(inp["ffn_conv_b"].reshape(L, NFT, 128).transpose(0, 2, 1))
    com["ident"] = np.eye(128, dtype=f)
    com["iota1"] = np.ascontiguousarray(np.broadcast_to(np.arange(1, 257, dtype=f)[None, :], (128, 256)))
    jj = np.arange(128)[:, None]
    ii = np.arange(128)[None, :]
    com["maskF"] = (jj <= ii).astype(f)
    com["maskB"] = (jj >= ii).astype(f)
    cm = np.ones((40, 2, NR), f)
    cm[:, 0, 0::128] = 0.0
    cm[:, 1, 127::128] = 0.0
    com["cmask"] = cm
    selh = np.zeros((40, 16, 128), f)
    for d in range(2):
        for h in range(8):
            selh[32 * d + h, d * 8 + h, :] = 1.0
    com["selh"] = selh
    per = []
    for c in range(8):
        b, k = divmod(c, 4)
        pc = {}
        xs = inp["x"][b, k * LAT:(k + 1) * LAT, :]
        pc["x_fm"] = np.ascontiguousarray(xs.reshape(LAT, 16, 128).transpose(2, 1, 0))
        pc["ctx_fm"] = np.ascontiguousarray(inp["ctx"][b].reshape(CTX, 16, 128).transpose(2, 1, 0))
        pc["w_ada_c"] = np.ascontiguousarray(inp["w_ada"][:, :, c * 1536:(c + 1) * 1536], dtype=f)
        pc["b_ada_c"] = np.ascontiguousarray(inp["b_ada"][:, c * 1536:(c + 1) * 1536].reshape(L, 12, 128).transpose(0, 2, 1), dtype=f)
        fl = np.zeros((128, 36), f)
        if k > 0:
            fl[:, c - 1] = 1.0
            fl[:, 32] = 1.0
        if k < 3:
            fl[:, 8 + c + 1] = 1.0
            fl[:, 33] = 1.0
        for j in range(8):
            if j // 4 == b and j < c:
                fl[:, 16 + j] = 1.0
            if j // 4 == b and j > c:
                fl[:, 24 + j] = 1.0
        fl[:, 34 + b] = 1.0
        pc["flags"] = fl
        for wk in BIGW:
            w = inp[wk]
            rows = w.shape[1] // 8
            pc[wk] = np.ascontiguousarray(w[:, c * rows:(c + 1) * rows, :], dtype=f)
        per.append(pc)
    return com, per

ENGS = ("pe", "dve", "act", "pool", "sp")
NDMA_SEM = 6


class Prog:
    def __init__(self, nc, same_engine_sync=True):
        self.nc = nc
        self.same = same_engine_sync
        self.ops = {e: [] for e in ENGS}
        self.sems = {}
        self.semval = {}
        self.seen = {e: {} for e in ENGS}
        self.last_w = {}
        self.readers = {}
        self.dma_rr = {e: 0 for e in ENGS}
        self._ctx = []
        for e in ENGS:
            self._mksem("c_" + e)
            for i in range(NDMA_SEM):
                self._mksem("d_%s_%d" % (e, i))
        self._mksem("cc")
        self._blk_cm = nc.Block()
        self.block = self._blk_cm.__enter__()
        self.n = {e: 0 for e in ENGS}

    def _mksem(self, name):
        cm = self.nc.semaphore(name)
        s = cm.__enter__()
        self._ctx.append(cm)
        self.sems[name] = s
        self.semval[name] = 0

    def _collect(self, eng, reads, writes, is_pe_acc=False):
        need = {}

        def add(ev):
            if ev is None:
                return
            s, v, src = ev
            if src == eng and (not self.same or (eng == "pe")):
                return
            if self.seen[eng].get(s, 0) >= v:
                return
            if need.get(s, 0) < v:
                need[s] = v

        for k in reads:
            add(self.last_w.get(k))
        for k in writes:
            add(self.last_w.get(k))
            for r in self.readers.get(k, ()):
                add(r)
        for s, v in need.items():
            self.seen[eng][s] = v
        return list(need.items())

    def _commit(self, ev, reads, writes):
        for k in reads:
            self.readers.setdefault(k, []).append(ev)
        for k in writes:
            self.last_w[k] = ev
            self.readers[k] = []

    def op(self, eng, fn, reads=(), writes=()):
        waits = self._collect(eng, reads, writes)
        s = "c_" + eng
        self.semval[s] += 1
        v = self.semval[s]
        self._emit(eng, waits, fn, s, 1)
        self._commit((s, v, eng), reads, writes)

    def dma(self, q, fn, reads=(), writes=(), block=False):
        i = self.dma_rr[q]
        self.dma_rr[q] = (i + 1) % NDMA_SEM
        s = "d_%s_%d" % (q, i)
        waits = self._collect(q, reads, writes)
        pv = self.semval[s]
        if pv > 0 and self.seen[q].get(s, 0) < pv:
            waits.append((s, pv))
            self.seen[q][s] = pv
        self.semval[s] += 16
        v = self.semval[s]
        self._emit(q, waits, fn, s, 16)
        self._commit((s, v, "dma"), reads, writes)
        if block:
            self.seen[q][s] = v
            self._emit(q, [(s, v)], None)

    def collective(self, fn, reads=(), writes=()):
        q = "pool"
        waits = self._collect(q, reads, writes)
        s = "cc"
        self.semval[s] += 1
        v = self.semval[s]
        self._emit(q, waits, fn, s, 1)
        self._commit((s, v, "cc"), reads, writes)
        self.seen[q][s] = v
        self._emit(q, [(s, v)], None)

    ENGMAP = {"pe": "tensor", "dve": "vector", "act": "scalar", "pool": "gpsimd", "sp": "sync"}

    def _emit(self, e, waits, fn, s=None, inc=0):
        self.n[e] += 1

        def body(engobj):
            for ws, wv in waits:
                engobj.wait_ge(self.sems[ws], wv)
            if fn is not None:
                inst = fn(engobj)
                inst.then_inc(self.sems[s], inc)

        getattr(self.block, self.ENGMAP[e])(body)

    def barrier(self):
        for e in ENGS:
            waits = []
            for s, v in self.semval.items():
                if v > 0 and self.seen[e].get(s, 0) < v:
                    waits.append((s, v))
                    self.seen[e][s] = v
            if waits:
                self._emit(e, waits, None)
        self.last_w.clear()
        self.readers.clear()

    def emit(self, final_keys=()):
        self.barrier()
        self._blk_cm.__exit__(None, None, None)
        for cm in reversed(self._ctx):
            cm.__exit__(None, None, None)
        self._ctx = []

    def ninstr(self):
        return dict(self.n)
F32 = mybir.dt.float32
BF16 = mybir.dt.bfloat16
I32 = mybir.dt.int32
ALU = mybir.AluOpType
AF = mybir.ActivationFunctionType
TOKB = [(0, 272), (272, 784), (784, 1296)]
RB = [(0, 256), (256, 768), (768, 1280)]


class KB:
    def __init__(self, com, per0, nlayers=NL, dbg=()):
        self.nc = nc = bass.Bass("TRN2", target_bir_lowering=False)
        self.P = Prog(nc)
        self.cms = []
        self.I = {}
        for k, v in list(com.items()) + list(per0.items()):
            self.I[k] = nc.dram_tensor(k, list(v.shape), F32, kind="ExternalInput").ap()
        self.out = nc.dram_tensor("out", [128, 16, LAT], F32, kind="ExternalOutput").ap()
        self.dbg = {}
        self.dbg_want = dbg
        self.nlayers = nlayers
        dr = lambda n, s, dt=F32: nc.dram_tensor(n, s, dt).ap()
        self.hbuf = dr("hbuf", [128, 16, NT])
        self.proj = dr("proj", [NMT * 128, NT])
        self.h1buf = dr("h1buf", [128, 16, NR + 128])
        self.actbuf = dr("actbuf", [NFT, 128, NR], BF16)
        self.r2buf = dr("r2buf", [128, 16, NR])
        self.cin0 = dr("cin0", [256, 128]); self.cout0 = dr("cout0", [2048, 128])
        self.cin2 = dr("cin2", [256, 1024]); self.cout2 = dr("cout2", [2048, 1024])
        self.cinM = dr("cinM", [128, 36]); self.coutM = dr("coutM", [1024, 36])
        self.cinA = dr("cinA", [128, 64]); self.coutA = dr("coutA", [1024, 64])
        self.cinD = dr("cinD", [128, 1040]); self.coutD = dr("coutD", [1024, 1040])
        self.uid = 0

    def sb(self, st, name, shape, dt=F32):
        self.uid += 1
        return st.enter_context(self.nc.sbuf_tensor("%s_%d" % (name, self.uid), shape, dt))

    def ps(self, st, name, shape, dt=F32):
        self.uid += 1
        return st.enter_context(self.nc.psum_tensor("%s_%d" % (name, self.uid), shape, dt))

    def dump(self, name, src_ap, shape, key):
        if name not in self.dbg_want:
            return
        t = self.nc.dram_tensor("dbg_" + name, shape, F32, kind="ExternalOutput").ap()
        self.dbg[name] = t
        n0 = shape[0]
        step = 128 if n0 > 128 else (n0 if len(shape) < 3 else 16)
        for a in range(0, n0, step):
            b = min(n0, a + step)
            self.P.dma("sp", lambda e, a=a, b=b: e.dma_start(out=t[a:b], in_=src_ap[a:b]), reads=[key], writes=["dbg_" + name])

    def mm(self, out, lhsT, rhs, start, stop, r, w):
        self.P.op("pe", lambda e: e.matmul(out, lhsT=lhsT, rhs=rhs, start=start, stop=stop), reads=r, writes=w)

    def tr(self, out, in_, ident, r, w):
        self.P.op("pe", lambda e: e.transpose(out, in_=in_, identity=ident), reads=r, writes=w)

    def tt(self, out, a, b, op, r, w, eng="dve"):
        self.P.op(eng, lambda e: e.tensor_tensor(out=out, in0=a, in1=b, op=op), reads=r, writes=w)

    def ts(self, out, a, s1, s2, op0, op1, r, w, eng="dve"):
        if op1 is None:
            self.P.op(eng, lambda e: e.tensor_scalar(out=out, in0=a, scalar1=s1, scalar2=None, op0=op0), reads=r, writes=w)
        else:
            self.P.op(eng, lambda e: e.tensor_scalar(out=out, in0=a, scalar1=s1, scalar2=s2, op0=op0, op1=op1), reads=r, writes=w)

    def stt(self, out, a, s, b, op0, op1, r, w):
        self.P.op("dve", lambda e: e.scalar_tensor_tensor(out=out, in0=a, scalar=s, in1=b, op0=op0, op1=op1), reads=r, writes=w)

    def act(self, out, a, func, r, w, bias=0.0, scale=1.0):
        self.P.op("act", lambda e: e.activation(out=out, in_=a, func=func, bias=bias, scale=scale), reads=r, writes=w)

    def cp(self, out, a, r, w, eng="dve"):
        if eng == "act":
            self.P.op("act", lambda e: e.activation(out=out, in_=a, func=AF.Copy), reads=r, writes=w)
        else:
            self.P.op(eng, lambda e: e.tensor_copy(out=out, in_=a), reads=r, writes=w)

    def ld(self, out, src, r, w, q="sp", slow=False):
        if slow:
            self.P.dma(q, lambda e: e.dma_start(out=out, in_=src, allow_slow_non_contiguous=True), reads=r, writes=w)
        else:
            self.P.dma(q, lambda e: e.dma_start(out=out, in_=src), reads=r, writes=w)

    def gelu(self, out, x, t, r, w, tmpkey):
        self.tt(t, x, x, ALU.mult, r, [tmpkey])
        self.ts(t, t, 0.044715, 1.0, ALU.mult, ALU.add, [tmpkey], [tmpkey])
        self.tt(t, t, x, ALU.mult, r + [tmpkey], [tmpkey])
        self.act(t, t, AF.Sigmoid, [tmpkey], [tmpkey], scale=1.5957691216057308)
        self.tt(out, t, x, ALU.mult, r + [tmpkey], w)

    def vop(self, name, r, w, *args, **kw):
        self.P.op("dve", lambda e: getattr(e, name)(*args, **kw), reads=r, writes=w)

    def exchange(self, st0, first_ap, last_ap, hl_ap, hr_ap, n, cin, cout, rkeys, wkeys, tag):
        P = self.P
        w = 16 * n
        ci = cin.rearrange("(s p) (k j) -> s p k j", s=2, k=16)
        self.ld(ci[0], first_ap, rkeys, [tag + "cin"], q="sp")
        self.ld(ci[1], last_ap, rkeys, [tag + "cin"], q="sp")
        P.collective(lambda e: e.collective_compute("AllGather", ALU.bypass, replica_groups=[list(range(8))],
                                                    ins=[cin.opt()], outs=[cout.opt()]),
                     reads=[tag + "cin"], writes=[tag + "cout"])
        co = cout.rearrange("(r s p) c -> s p r c", r=8, s=2)
        with ExitStack() as st:
            g = self.sb(st, "exg", [128, 8, w])
            acc = self.sb(st, "exa", [128, w])
            fl = self.flags
            for side, (src, dst, so) in enumerate(((co[1], hl_ap, 0), (co[0], hr_ap, 8))):
                gk, ak = tag + "g", tag + "a"
                self.ld(g[:], src, [tag + "cout"], [gk])
                self.ts(acc[:], g[:, 0, :], fl[:, so:so + 1], None, ALU.mult, None, [gk], [ak])
                for j in range(1, 8):
                    self.stt(acc[:], g[:, j, :], fl[:, so + j:so + j + 1], acc[:], ALU.mult, ALU.add, [gk, ak], [ak])
                self.ld(dst, acc[:].rearrange("p (k j) -> p k j", k=16), [ak], wkeys)
            P.barrier()
def phase_setup(self, st):
    P = self.P
    I = self.I
    self.flags_t = self.sb(st, "flags", [128, 36])
    self.flags = self.flags_t
    self.ld(self.flags_t[:], I["flags"], [], ["flags"])
    self.ident = self.sb(st, "ident", [128, 128])
    self.ld(self.ident[:], I["ident"], [], ["ident"])
    self.epsc = self.sb(st, "epsc", [128, 2])
    self.P.op("dve", lambda e: e.memset(self.epsc[:, 0:1], LN_EPS), writes=["epsc"])
    self.P.op("dve", lambda e: e.memset(self.epsc[:, 1:2], 1.0), writes=["epsc"])
    self.mod = self.sb(st, "mod", [128, 96, 2])
    self.mod1p = self.sb(st, "mod1p", [128, 96, 2])
    self.cv = self.sb(st, "cv", [128, 16, 3])
    self.cvb = self.sb(st, "cvb", [128, 16, 3], BF16)
    self.ld(self.cv[:], I["cvec3"], [], ["cv"])
    self.act(self.cvb[:], self.cv[:], AF.Silu, ["cv"], ["cvb"])
    self.ld(self.hbuf[:, :, C_LAT[0]:C_LAT[1]], I["x_fm"], [], ["hbuf"])
    self.ld(self.hbuf[:, :, 0:CTX], I["ctx_fm"], [], ["hbuf"])
    self.W = {}
    self.Wpart = {}
    for wk in BIGW:
        rows, cols = I[wk][0].shape
        self.Wpart[wk] = self.nc.dram_tensor("wp_%s" % wk, [rows, cols], BF16).ap()
        self.W[wk] = self.nc.dram_tensor("wf_%s" % wk, [rows * 8, cols], BF16).ap()
    P.barrier()


def prep_weights(self, l):
    P = self.P
    I = self.I
    for wk in BIGW:
        src = I[wk][l]
        rows, cols = src.shape
        part = self.Wpart[wk]
        full = self.W[wk]
        nd = (cols + 2047) // 2048
        rstep = max(1, 768 // nd)
        for a in range(0, rows, rstep):
            b = min(rows, a + rstep)
            P.dma("pool", lambda e, a=a, b=b, part=part, src=src: e.dma_start(out=part[a:b], in_=src[a:b], max_dma_last_dim=8192),
                  reads=[], writes=["wp_%s" % wk], block=True)
        P.collective(lambda e, part=part, full=full: e.collective_compute(
            "AllGather", ALU.bypass, replica_groups=[list(range(8))], ins=[part.opt()], outs=[full.opt()]),
            reads=["wp_%s" % wk], writes=["wf_%s" % wk])
    P.barrier()


def phase_e0(self):
    hb = self.hbuf
    with ExitStack() as st:
        self.exchange(st, hb[:, :, C_LAT[0]:C_LAT[0] + 8], hb[:, :, C_LAT[1] - 8:C_LAT[1]],
                      hb[:, :, C_HL[0]:C_HL[1]], hb[:, :, C_HR[0]:C_HR[1]], 8, self.cin0, self.cout0,
                      ["hbuf"], ["hbuf"], "e0")


def phase_ada(self, l):
    P = self.P
    I = self.I
    fl = self.flags
    with ExitStack() as st:
        bada = self.sb(st, "bada", [128, 12])
        self.ld(bada[:], I["b_ada_c"][l], [], ["bada"])
        wsl = self.sb(st, "adaw", [128, 16, 1536], BF16)
        wv = I["w_ada_c"][l].rearrange("(k p) n -> p k n", p=128)
        for cb in range(3):
            self.P.dma("pool", lambda e, cb=cb: e.dma_start(out=wsl[:, :, cb * 512:(cb + 1) * 512], in_=wv[:, :, cb * 512:(cb + 1) * 512]),
                       reads=[], writes=["adaw%d" % cb], block=True)
        pp = self.ps(st, "adaps", [128, 12, 3])
        for j in range(12):
            for k in range(16):
                self.mm(pp[:, j, :], wsl[:, k, j * 128:(j + 1) * 128], self.cvb[:, k, :], k == 0, k == 15,
                        ["adaw%d" % (j // 4), "cvb"], ["adaps"])
        mloc = self.sb(st, "mloc", [128, 12, 3])
        self.tt(mloc[:], pp[:], bada[:].unsqueeze(2).to_broadcast([128, 12, 3]), ALU.add, ["adaps", "bada"], ["mloc"])
        self.ld(self.cinM, mloc[:].rearrange("p j c -> p (j c)"), ["mloc"], ["cinM"])
        self.gather(self.cinM, self.coutM, "cinM", "coutM")
        mg = self.sb(st, "mg", [128, 96, 3])
        self.ld(mg[:].rearrange("p (r j) c -> p r (j c)", r=8), self.coutM.rearrange("(r p) c -> p r c", p=128), ["coutM"], ["mg"])
        self.cp(self.mod[:, :, 1], mg[:, :, 2], ["mg"], ["mod"])
        self.ts(self.mod[:, :, 0], mg[:, :, 0], fl[:, 34:35], None, ALU.mult, None, ["mg", "flags", "mod"], ["mod"])
        self.stt(self.mod[:, :, 0], mg[:, :, 1], fl[:, 35:36], self.mod[:, :, 0], ALU.mult, ALU.add, ["mg", "flags", "mod"], ["mod"])
        self.ts(self.mod1p[:], self.mod[:], 1.0, None, ALU.add, None, ["mod"], ["mod1p"])
        self.dump("mod%d" % l, self.mod[:], [128, 96, 2], "mod")
        P.barrier()


def phase_inproj(self, l):
    P = self.P
    I = self.I
    with ExitStack() as st:
        hm = self.sb(st, "hm", [128, 16, NT], BF16)
        hst = [self.sb(st, "hst", [128, NT]) for _ in range(2)]
        for k in range(16):
            hs = hst[k % 2]
            hk = "hst%d" % (k % 2)
            self.ld(hs[:], self.hbuf[:, k, :], ["hbuf"], [hk])
            self.ts(hm[:, k, 0:CTX], hs[:, 0:CTX], self.mod1p[:, 16 + k, 1:2], self.mod[:, k, 1:2], ALU.mult, ALU.add,
                    [hk, "mod", "mod1p"], ["hm"])
            self.ts(hm[:, k, CTX:NT], hs[:, CTX:NT], self.mod1p[:, 16 + k, 0:1], self.mod[:, k, 0:1], ALU.mult, ALU.add,
                    [hk, "mod", "mod1p"], ["hm"])
        wv = self.W["w_in"].rearrange("(k p) n -> p k n", p=128)
        slabs = [self.sb(st, "inw", [128, 16, 512], BF16) for _ in range(2)]
        stg = [self.sb(st, "pstg", [128, NT]) for _ in range(2)]
        pss = [self.ps(st, "pjps", [128, 512]) for _ in range(4)]
        pi = 0
        for cb in range(8):
            c0 = cb * 512
            cw = min(512, INW - c0)
            sl = slabs[cb % 2]
            sk = "inw%d" % (cb % 2)
            self.ld(sl[:, :, 0:cw], wv[:, :, c0:c0 + cw], [], [sk], q="sp")
            for mi in range((cw + 127) // 128):
                m = cb * 4 + mi
                mw = min(128, cw - mi * 128)
                sg = stg[m % 2]
                gk = "pstg%d" % (m % 2)
                for (a, b) in TOKB:
                    pt = pss[pi % 4]
                    pk = "pjps%d" % (pi % 4)
                    pi += 1
                    for k in range(16):
                        self.mm(pt[0:mw, 0:b - a], sl[:, k, mi * 128:mi * 128 + mw], hm[:, k, a:b], k == 0, k == 15,
                                [sk, "hm"], [pk])
                    self.cp(sg[0:mw, a:b], pt[0:mw, 0:b - a], [pk], [gk], eng="act")
                self.ld(self.proj[m * 128:m * 128 + mw, :], sg[0:mw, :], [gk], ["proj"])
        P.barrier()


KB.phase_setup = phase_setup
KB.prep_weights = prep_weights
KB.phase_e0 = phase_e0
KB.phase_ada = phase_ada
KB.phase_inproj = phase_inproj
def winchain(self, E, A, Bt, Lext, level, rk, keyA, keyB):
    self.tt(A[:, 1:Lext], E[:, 0:Lext - 1], E[:, 1:Lext], ALU.add, rk, [keyA])
    if level == 1:
        return A, keyA
    self.tt(Bt[:, 2:Lext - 1], A[:, 1:Lext - 2], A[:, 3:Lext], ALU.add, [keyA], [keyB])
    if level == 2:
        return Bt, keyB
    self.tt(A[:, 4:Lext - 3], Bt[:, 2:Lext - 5], Bt[:, 6:Lext - 1], ALU.add, [keyB], [keyA])
    if level == 3:
        return A, keyA
    self.tt(Bt[:, 8:Lext - 7], A[:, 4:Lext - 11], A[:, 12:Lext - 3], ALU.add, [keyA], [keyB])
    return Bt, keyB


def mix_gating(self, l, need_ctx):
    I = self.I
    P = self.P
    with ExitStack() as st:
        lnG = self.sb(st, "lnG", [128, 4, 128]); lnB = self.sb(st, "lnB", [128, 4, 128])
        wsf = self.sb(st, "wsf", [128, 4, 128]); wsb = self.sb(st, "wsb", [128, 4, 128], BF16)
        bsR = self.sb(st, "bsR", [128, 4, 128])
        self.ld(lnG[:], I["sglnG"][l], [], ["lnG"]); self.ld(lnB[:], I["sglnB"][l], [], ["lnB"])
        self.ld(wsf[:], I["sgwT"][l], [], ["wsf"]); self.ld(bsR[:], I["sgb"][l], [], ["bsR"])
        self.cp(wsb[:], wsf[:], ["wsf"], ["wsb"])
        U = self.sb(st, "gU", [128, NR]); V = self.sb(st, "gV", [128, NR]); T = self.sb(st, "gT", [128, NR])
        tps = [self.ps(st, "gtp", [128, 128]) for _ in range(2)]
        sps = [self.ps(st, "gsp", [128, 128]) for _ in range(2)]
        st6 = self.sb(st, "st6", [128, 6]); mv = self.sb(st, "mv", [128, 2]); rs = self.sb(st, "rs", [128, 1])
        vn = self.sb(st, "vn", [128, 128]); vnb = self.sb(st, "vnb", [128, 128], BF16); sv = self.sb(st, "sv", [128, 128])
        c0 = 0 if need_ctx else 2
        a0 = c0 * 128
        n = 0
        for hd in range(4):
            ru = 512 + hd * 128
            rv = 1024 + hd * 128
            for (tile, key, r0) in ((U, "gU", ru), (V, "gV", rv)):
                if need_ctx:
                    self.ld(tile[:, 0:CTX], self.proj[r0:r0 + 128, 0:CTX], ["proj"], [key])
                self.ld(tile[:, CTX:NR], self.proj[r0:r0 + 128, C_LAT[0]:C_LAT[1]], ["proj"], [key])
                self.gelu(tile[:, a0:NR], tile[:, a0:NR], T[:, a0:NR], [key], [key], "gT")
            for c in range(c0, 10):
                cs = c * 128
                tp = tps[n % 2]; tk = "gtp%d" % (n % 2)
                sp = sps[n % 2]; sk = "gsp%d" % (n % 2)
                n += 1
                self.tr(tp[:], V[:, cs:cs + 128], self.ident[:], ["gV", "ident"], [tk])
                self.vop("bn_stats", [tk], ["st6"], out=st6[:], in_=tp[:])
                self.vop("bn_aggr", ["st6"], ["mv"], out=mv[:], in_=st6[:])
                self.act(rs[:], mv[:, 1:2], AF.Sqrt, ["mv"], ["rs"], bias=self.epsc[:, 0:1])
                self.vop("reciprocal", ["rs"], ["rs"], out=rs[:], in_=rs[:])
                self.ts(vn[:], tp[:], mv[:, 0:1], rs[:, 0:1], ALU.subtract, ALU.mult, [tk, "mv", "rs"], ["vn"])
                self.tt(vn[:], vn[:], lnG[:, hd, :], ALU.mult, ["vn", "lnG"], ["vn"])
                self.tt(vnb[:], vn[:], lnB[:, hd, :], ALU.add, ["vn", "lnB"], ["vnb"])
                self.mm(sp[:], vnb[:], wsb[:, hd, :], True, True, ["vnb", "wsb"], [sk])
                self.tt(sv[:], sp[:], bsR[:, hd, :], ALU.add, [sk, "bsR"], ["sv"])
                self.tt(self.mix[:, 4 + hd, cs:cs + 128], sv[:], U[:, cs:cs + 128], ALU.mult, ["sv", "gU"], ["mix"])
        P.barrier()


def mix_pool(self, l, need_ctx):
    I = self.I
    P = self.P
    fl = self.flags
    with ExitStack() as st:
        pwf = self.sb(st, "pwf", [128, 4, 128]); pwb = self.sb(st, "pwb", [128, 4, 128], BF16)
        pb = self.sb(st, "pb", [128, 4]); psc = self.sb(st, "psc", [128, 4])
        self.ld(pwf[:], I["pool_w"][l].rearrange("g c d -> c g d"), [], ["pwf"])
        self.cp(pwb[:], pwf[:], ["pwf"], ["pwb"])
        self.ld(pb[:], I["poolb"][l], [], ["pb"]); self.ld(psc[:], I["pools"][l], [], ["psc"])
        LX = LAT + 16
        E = self.sb(st, "pE", [128, LX]); A = self.sb(st, "pA", [128, LX]); Bt = self.sb(st, "pB", [128, LX])
        rc = self.sb(st, "prc", [128, 4, LAT]); df = self.sb(st, "pdf", [128, LAT], BF16)
        mn = self.sb(st, "pmn", [128, LAT])
        pss = [self.ps(st, "pps", [128, 512]) for _ in range(2)]
        segs = [("lat", LAT, CTX)]
        if need_ctx:
            segs.append(("ctx", CTX, 0))
        n = 0
        for (sname, Lr, mo) in segs:
            Lext = Lr + 16
            self.vop("memset", [], ["pE"], E[:, 0:Lext], 1.0)
            if sname == "lat":
                self.cp(E[:, 0:8], fl[:, 32:33].to_broadcast([128, 8]), ["flags"], ["pE"])
                self.cp(E[:, 8 + Lr:Lext], fl[:, 33:34].to_broadcast([128, 8]), ["flags"], ["pE"])
            else:
                self.vop("memset", [], ["pE"], E[:, 0:8], 0.0)
                self.vop("memset", [], ["pE"], E[:, 8 + Lr:Lext], 0.0)
            for g in range(4):
                W, wk = self.winchain(E, A, Bt, Lext, g + 1, ["pE"], "pA", "pB")
                self.vop("reciprocal", [wk], ["prc"], out=rc[:, g, 0:Lr], in_=W[:, 8:8 + Lr])
            for g in range(4):
                r0 = 1536 + g * 128
                if sname == "lat":
                    self.ld(E[:, 0:8], self.proj[r0:r0 + 128, C_HL[0]:C_HL[1]], ["proj"], ["pE"])
                    self.ld(E[:, 8:8 + Lr], self.proj[r0:r0 + 128, C_LAT[0]:C_LAT[1]], ["proj"], ["pE"])
                    self.ld(E[:, 8 + Lr:Lext], self.proj[r0:r0 + 128, C_HR[0]:C_HR[1]], ["proj"], ["pE"])
                    self.ts(E[:, 0:8], E[:, 0:8], fl[:, 32:33], None, ALU.mult, None, ["pE", "flags"], ["pE"])
                    self.ts(E[:, 8 + Lr:Lext], E[:, 8 + Lr:Lext], fl[:, 33:34], None, ALU.mult, None, ["pE", "flags"], ["pE"])
                else:
                    self.vop("memset", [], ["pE"], E[:, 0:8], 0.0)
                    self.vop("memset", [], ["pE"], E[:, 8 + Lr:Lext], 0.0)
                    self.ld(E[:, 8:8 + Lr], self.proj[r0:r0 + 128, 0:CTX], ["proj"], ["pE"])
                W, wk = self.winchain(E, A, Bt, Lext, g + 1, ["pE"], "pA", "pB")
                self.tt(mn[:, 0:Lr], W[:, 8:8 + Lr], rc[:, g, 0:Lr], ALU.mult, [wk, "prc"], ["pmn"])
                self.tt(df[:, 0:Lr], mn[:, 0:Lr], E[:, 8:8 + Lr], ALU.subtract, ["pmn", "pE"], ["pdf"])
                for a in range(0, Lr, 512):
                    b = min(Lr, a + 512)
                    pt = pss[n % 2]; pk = "pps%d" % (n % 2)
                    n += 1
                    self.mm(pt[:, 0:b - a], pwb[:, g, :], df[:, a:b], True, True, ["pwb", "pdf"], [pk])
                    self.ts(self.mix[:, 8 + g, mo + a:mo + b], pt[:, 0:b - a], pb[:, g:g + 1], psc[:, g:g + 1], ALU.add, ALU.mult,
                            [pk, "pb", "psc"], ["mix"])
        P.barrier()


def phase_mixers(self, l):
    need_ctx = l < NL - 1
    if not any(k in self.dbg_want for k in ("mixA", "mixD", "mixAD")):
        self.mix_gating(l, need_ctx)
        self.mix_pool(l, need_ctx)
    if "mixBC" in self.dbg_want:
        with ExitStack() as st:
            t = self.sb(st, "dmix", [128, 8, NR])
            self.cp(t[:], self.mix[:, 4:12, :], ["mix"], ["dmix"])
            self.dump("mixBC", t[:], [128, 8, NR], "dmix")
            self.P.barrier()
        return
    if "mixA" in self.dbg_want or "mixAD" in self.dbg_want:
        self.mix_s5(l, need_ctx)
    if "mixD" in self.dbg_want or "mixAD" in self.dbg_want:
        self.mix_ssd(l, need_ctx)
    if "mixAD" in self.dbg_want or "mixA" in self.dbg_want or "mixD" in self.dbg_want:
        nm = [k for k in ("mixA", "mixD", "mixAD") if k in self.dbg_want][0]
        with ExitStack() as st:
            t = self.sb(st, "dmix", [128, 8, NR])
            self.vop("memset", [], ["dmix"], t[:], 0.0)
            if nm != "mixD":
                self.cp(t[:, 0:4], self.mix[:, 0:4, :], ["mix"], ["dmix"])
            if nm != "mixA":
                self.cp(t[:, 4:8], self.mix[:, 12:16, :], ["mix"], ["dmix"])
            self.dump(nm, t[:], [128, 8, NR], "dmix")
            self.P.barrier()
        return
    self.mix_s5(l, need_ctx)
    self.mix_ssd(l, need_ctx)


KB.winchain = winchain
KB.mix_gating = mix_gating
KB.mix_pool = mix_pool
KB.phase_mixers = phase_mixers
def sincos(self, ang, sn, cs, ki, red, m, n, rk, tag, tk="s"):
    for (dst, off) in ((sn, 0.0), (cs, math.pi / 2)):
        kk, kr, km = tk + "ki", tk + "red", tk + "m"
        if off == 0.0:
            src = ang
            self.ts(ki, src, 1.0 / TWO_PI, None, ALU.mult, None, rk, [kk])
        else:
            self.ts(red, ang, off, None, ALU.add, None, rk, [kr])
            src = red
            self.ts(ki, src, 1.0 / TWO_PI, None, ALU.mult, None, [kr], [kk])
        self.stt(red, ki, -TWO_PI, src, ALU.mult, ALU.add, [kk, kr] + rk, [kr])
        self.ts(m, red, math.pi, -TWO_PI, ALU.is_gt, ALU.mult, [kr], [km])
        self.tt(red, red, m, ALU.add, [kr, km], [kr])
        self.ts(m, red, -math.pi, TWO_PI, ALU.is_lt, ALU.mult, [kr], [km])
        self.tt(red, red, m, ALU.add, [kr, km], [kr])
        self.act(dst, red, AF.Sin, [kr], [tag + "out"])


def cmul(self, ore, oim, are, aim, bre, bim, t1, t2, rk, wk, k1, k2):
    self.tt(t1, are, bre, ALU.mult, rk, [k1])
    self.tt(t2, aim, bim, ALU.mult, rk, [k2])
    self.tt(t1, t1, t2, ALU.subtract, [k1, k2], [k1])
    self.tt(t2, are, bim, ALU.mult, rk, [k2])
    self.tt(oim, aim, bre, ALU.mult, rk, wk)
    self.tt(oim, oim, t2, ALU.add, wk + [k2], wk)
    self.cp(ore, t1, [k1], wk)


def mix_s5(self, l, need_ctx):
    I = self.I
    P = self.P
    fl = self.flags
    NB = 256
    with ExitStack() as st:
        sb = lambda n, s, dt=F32: self.sb(st, n, s, dt)
        prm = sb("s5prm", [128, 2, 3, 16])
        self.ld(prm[:], I["s5p"][l].rearrange("d p k t -> p d k t"), [], ["prm"])
        lre, lim, lst = prm[:, :, 0, :], prm[:, :, 1, :], prm[:, :, 2, :]
        stp = sb("stp", [128, 2, 16]); zre = sb("zre", [128, 2, 16]); th = sb("th", [128, 2, 16]); rr = sb("rr", [128, 2, 16])
        self.act(stp[:], lst, AF.Exp, ["prm"], ["stp"])
        self.tt(zre[:], lre, stp[:], ALU.mult, ["prm", "stp"], ["zre"])
        self.tt(th[:], lim, stp[:], ALU.mult, ["prm", "stp"], ["th"])
        self.act(rr[:], zre[:], AF.Exp, ["zre"], ["rr"])
        ki = sb("ski", [128, 2, 16], I32); red = sb("sred", [128, 2, 16]); mm_ = sb("smm", [128, 2, 16])
        ang = sb("sang", [128, 2, 16])
        sn1 = sb("sn1", [128, 2, 16]); cs1 = sb("cs1", [128, 2, 16])
        self.sincos(th[:], sn1[:], cs1[:], ki[:], red[:], mm_[:], 32, ["th"], "sc1")
        pw = {}
        for (nm, mult) in (("256", 256.0), ("1024", 1024.0)):
            s_ = sb("sn" + nm, [128, 2, 16]); c_ = sb("cs" + nm, [128, 2, 16]); rp = sb("rp" + nm, [128, 2, 16])
            self.ts(ang[:], th[:], mult, None, ALU.mult, None, ["th"], ["ang"])
            self.sincos(ang[:], s_[:], c_[:], ki[:], red[:], mm_[:], 32, ["ang"], "sc" + nm)
            self.act(rp[:], zre[:], AF.Exp, ["zre"], ["rp" + nm], scale=mult)
            pre = sb("pre" + nm, [128, 2, 16]); pim = sb("pim" + nm, [128, 2, 16])
            self.tt(pre[:], rp[:], c_[:], ALU.mult, ["rp" + nm, "sc" + nm + "out"], ["p" + nm])
            self.tt(pim[:], rp[:], s_[:], ALU.mult, ["rp" + nm, "sc" + nm + "out"], ["p" + nm])
            pw[nm] = (pre, pim)
        t1 = sb("t1", [128, 2, 16]); t2 = sb("t2", [128, 2, 16]); t3 = sb("t3", [128, 2, 16])
        nre = sb("nre", [128, 2, 16]); nim = sb("nim", [128, 2, 16]); den = sb("den", [128, 2, 16])
        bsr = sb("bsr", [128, 2, 16]); bsi = sb("bsi", [128, 2, 16])
        self.tt(nre[:], rr[:], cs1[:], ALU.mult, ["rr", "sc1out"], ["nre"])
        self.ts(nre[:], nre[:], -1.0, None, ALU.add, None, ["nre"], ["nre"])
        self.tt(nim[:], rr[:], sn1[:], ALU.mult, ["rr", "sc1out"], ["nim"])
        self.tt(den[:], lre, lre, ALU.mult, ["prm"], ["den"])
        self.tt(t1[:], lim, lim, ALU.mult, ["prm"], ["t1"])
        self.tt(den[:], den[:], t1[:], ALU.add, ["den", "t1"], ["den"])
        self.vop("reciprocal", ["den"], ["den"], out=den[:], in_=den[:])
        self.tt(t1[:], nre[:], lre, ALU.mult, ["nre", "prm"], ["t1"])
        self.tt(t2[:], nim[:], lim, ALU.mult, ["nim", "prm"], ["t2"])
        self.tt(t1[:], t1[:], t2[:], ALU.add, ["t1", "t2"], ["t1"])
        self.tt(bsr[:], t1[:], den[:], ALU.mult, ["t1", "den"], ["bsr"])
        self.tt(t1[:], nim[:], lre, ALU.mult, ["nim", "prm"], ["t1"])
        self.tt(t2[:], nre[:], lim, ALU.mult, ["nre", "prm"], ["t2"])
        self.tt(t1[:], t1[:], t2[:], ALU.subtract, ["t1", "t2"], ["t1"])
        self.tt(bsi[:], t1[:], den[:], ALU.mult, ["t1", "den"], ["bsi"])
        CeR = sb("CeR", [128, 2, 16, 64], BF16); nCeI = sb("nCeI", [128, 2, 16, 64], BF16)
        BTb = sb("BTb", [128, 2, 2, 4, 2, 128], BF16)
        with ExitStack() as st2:
            ctf = self.sb(st2, "ctf", [128, 2, 2, 16, 64])
            self.ld(ctf[:], I["s5CT"][l].rearrange("d r p t c -> p d r t c"), [], ["ctf"])
            x1 = self.sb(st2, "cx1", [128, 16, 64]); x2 = self.sb(st2, "cx2", [128, 16, 64])
            for d in range(2):
                br = bsr[:, d, :].unsqueeze(2).to_broadcast([128, 16, 64])
                bi = bsi[:, d, :].unsqueeze(2).to_broadcast([128, 16, 64])
                self.tt(x1[:], ctf[:, d, 0], br, ALU.mult, ["ctf", "bsr"], ["cx1"])
                self.tt(x2[:], ctf[:, d, 1], bi, ALU.mult, ["ctf", "bsi"], ["cx2"])
                self.tt(CeR[:, d], x1[:], x2[:], ALU.subtract, ["cx1", "cx2"], ["CeR"])
                self.tt(x1[:], ctf[:, d, 0], bi, ALU.mult, ["ctf", "bsi"], ["cx1"])
                self.tt(x2[:], ctf[:, d, 1], br, ALU.mult, ["ctf", "bsr"], ["cx2"])
                self.tt(x1[:], x1[:], x2[:], ALU.add, ["cx1", "cx2"], ["cx1"])
                self.ts(nCeI[:, d], x1[:], -1.0, None, ALU.mult, None, ["cx1"], ["nCeI"])
            btf = self.sb(st2, "btf", [128, 2, 2, 4, 2, 128])
            self.ld(btf[:], I["s5BT"][l].rearrange("d r p q e n -> p d r q e n"), [], ["btf"])
            self.cp(BTb[:], btf[:], ["btf"], ["BTb"])
            P.barrier()
        tab = sb("tab", [128, 2, 16, 3, NB])
        with ExitStack() as st2:
            io = self.sb(st2, "io", [128, NB])
            self.ld(io[:], I["iota1"], [], ["io"])
            ta = self.sb(st2, "ta", [128, NB]); tki = self.sb(st2, "tki", [128, NB], I32)
            tred = self.sb(st2, "tred", [128, NB]); tm = self.sb(st2, "tm", [128, NB])
            for d in range(2):
                for t in range(16):
                    self.ts(ta[:], io[:], th[:, d, t:t + 1], None, ALU.mult, None, ["io", "th"], ["ta"])
                    self.sincos(ta[:], tab[:, d, t, 1, :], tab[:, d, t, 0, :], tki[:], tred[:], tm[:], NB, ["ta"], "tb", tk="t")
                    self.ts(tab[:, d, t, 2, :], tab[:, d, t, 1, :], -1.0, None, ALU.mult, None, ["tbout"], ["tbout"])
            P.barrier()
        ub = sb("ub", [128, 4, NR], BF16)
        sd = sb("s5d", [128, 4]); gb = sb("glub", [128, 4])
        self.ld(sd[:], I["s5d"][l], [], ["sd"]); self.ld(gb[:], I["s5glub"][l], [], ["gb"])
        gwb = sb("gwb", [128, 4, 512], BF16)
        with ExitStack() as st2:
            uf = self.sb(st2, "uf", [128, NR])
            for q in range(4):
                self.ld(uf[:, 0:CTX], self.proj[q * 128:(q + 1) * 128, 0:CTX], ["proj"], ["uf"])
                self.ld(uf[:, CTX:NR], self.proj[q * 128:(q + 1) * 128, C_LAT[0]:C_LAT[1]], ["proj"], ["uf"])
                self.cp(ub[:, q, :], uf[:], ["uf"], ["ub"])
            gwf = self.sb(st2, "gwf", [128, 4, 512])
            self.ld(gwf[:], I["s5_glu_w"][l].rearrange("(q p) n -> p q n", p=128), [], ["gwf"])
            self.cp(gwb[:], gwf[:], ["gwf"], ["gwb"])
            P.barrier()
        EZ = sb("EZ", [128, 2, 2, 16, 5])
        ET = sb("ET", [128, 2, 2, 16, 4])
        HC = sb("HC", [128, 2, 2, 16])
        ZERO = sb("zero", [128, 1])
        self.vop("memset", [], ["zero"], ZERO[:], 0.0)
        self.vop("memset", [], ["EZ"], EZ[:], 0.0)
        vps = [self.ps(st, "vps", [128, 2, NB]) for _ in range(2)]
        yps = [self.ps(st, "yps", [128, NB]) for _ in range(2)]
        gps = [self.ps(st, "gps", [128, NB]) for _ in range(2)]
        W2 = sb("W2", [128, 2, NB]); M1 = sb("M1", [128, 2, NB]); M2 = sb("M2", [128, 2, NB]); G2 = sb("G2", [128, 2, NB])
        H2 = sb("H2", [128, 2, NB], BF16)
        e1 = sb("e1", [128, 2, 1])
        cnt = [0]

        def block(d, t, col0, init_re, init_im, exit_ap, full, ypt, ypk, first, ikeys, last=False):
            q, r = divmod(t, 4)
            vp = vps[cnt[0] % 2]; vk = "vps%d" % (cnt[0] % 2)
            cnt[0] += 1
            rv = (lambda ap: ap[:, :, ::-1]) if d == 1 else (lambda ap: ap)
            rv2 = (lambda ap: ap[:, ::-1]) if d == 1 else (lambda ap: ap)
            rb, e_ = divmod(r, 2)
            usl = ub[64 * rb:64 * rb + 64, q, col0:col0 + NB]
            self.mm(vp[:, 0, :], BTb[64 * rb:64 * rb + 64, d, 0, q, e_, :], usl, True, True, ["BTb", "ub"], [vk])
            self.mm(vp[:, 1, :], BTb[64 * rb:64 * rb + 64, d, 1, q, e_, :], usl, True, True, ["BTb", "ub"], [vk])
            cc = rv(tab[:, d, t, 0:1, :]).to_broadcast([128, 2, NB])
            ss = rv(tab[:, d, t, 1:3, :])
            ns = rv(tab[:, d, t, 2:0:-1, :])
            self.tt(M1[:], vp[:], cc, ALU.mult, [vk, "tab"], ["M1"])
            self.tt(M2[:], vp[:, ::-1, :], ss, ALU.mult, [vk, "tab"], ["M2"])
            self.tt(W2[:], M1[:], M2[:], ALU.add, ["M1", "M2"], ["W2"])
            rbc = rr[:, d, t:t + 1].to_broadcast([128, NB])
            self.P.op("dve", lambda e: e.tensor_tensor_scan(out=rv2(G2[:, 0, :]), data0=rbc, data1=rv2(W2[:, 0, :]), initial=init_re,
                                                            op0=ALU.mult, op1=ALU.add), reads=["rr", "W2"] + ikeys, writes=["G2"])
            self.P.op("dve", lambda e: e.tensor_tensor_scan(out=rv2(G2[:, 1, :]), data0=rbc, data1=rv2(W2[:, 1, :]), initial=init_im,
                                                            op0=ALU.mult, op1=ALU.add), reads=["rr", "W2"] + ikeys, writes=["G2"])
            if exit_ap is not None:
                lc = 0 if d == 1 else NB - 1
                cL = tab[:, d, t, 0, NB - 1:NB]
                nsL = tab[:, d, t, 2:0:-1, NB - 1:NB]
                self.tt(e1[:], G2[:, ::-1, lc:lc + 1], nsL, ALU.mult, ["G2", "tab"], ["e1"])
                self.stt(exit_ap, G2[:, :, lc:lc + 1], cL, e1[:], ALU.mult, ALU.add, ["G2", "tab", "e1"], ["EZ"])
            if full:
                self.tt(M1[:], G2[:], cc, ALU.mult, ["G2", "tab"], ["M1"])
                self.tt(M2[:], G2[:, ::-1, :], ns, ALU.mult, ["G2", "tab"], ["M2"])
                self.tt(H2[:], M1[:], M2[:], ALU.add, ["M1", "M2"], ["H2"])
                self.mm(ypt[64 * rb:64 * rb + 64, :], CeR[:, d, t, :], H2[:, 0, :], first, False, ["CeR", "H2"], [ypk])
                self.mm(ypt[64 * rb:64 * rb + 64, :], nCeI[:, d, t, :], H2[:, 1, :], False, last, ["nCeI", "H2"], [ypk])

        zb = sb("zb", [128, 4, NB], BF16); yv = sb("yv", [128, NB]); gt = sb("gt", [128, NB]); uf2 = sb("uf2", [128, NB])
        sg = sb("sg", [128, NB])
        ycnt = [0]

        def full_block(col0, pcol0, inits):
            for q in range(4):
                ypt = yps[ycnt[0] % 2]; ypk = "yps%d" % (ycnt[0] % 2)
                ycnt[0] += 1
                for r in range(4):
                    t = 4 * q + r
                    for d in range(2):
                        ire, iim, ik = inits(d, t)
                        block(d, t, col0, ire, iim, None, True, ypt, ypk, (d == 0 and r % 2 == 0), ik, last=(d == 1 and r % 2 == 1))
                self.ld(uf2[:], self.proj[q * 128:(q + 1) * 128, pcol0:pcol0 + NB], ["proj"], ["uf2"])
                self.stt(yv[:], uf2[:], sd[:, q:q + 1], ypt[:], ALU.mult, ALU.add, ["uf2", "sd", ypk], ["yv"])
                self.gelu(yv[:], yv[:], gt[:], ["yv"], ["yv"], "gt")
                self.cp(zb[:, q, :], yv[:], ["yv"], ["zb"])
                if q == 3:
                    zf_keep = None
            for m in range(4):
                gp = gps[m % 2]; gk = "gps%d" % (m % 2)
                for q in range(4):
                    self.mm(gp[:], gwb[:, q, m * 128:(m + 1) * 128], zb[:, q, :], q == 0, q == 3, ["gwb", "zb"], [gk])
                self.act(sg[:], gp[:], AF.Sigmoid, [gk, "gb"], ["sg"], bias=gb[:, m:m + 1])
                self.tt(self.mix[:, m, col0:col0 + NB], sg[:], zb[:, m, :], ALU.mult, ["sg", "zb"], ["mix"])

        if need_ctx:
            full_block(0, 0, lambda d, t: (0.0, 0.0, []))
        for d in range(2):
            for t in range(16):
                block(d, t, 0, 0.0, 0.0, HC[:, d, :, t:t + 1], False, None, None, False, [])
        for d in range(2):
            for t in range(16):
                for k in range(4):
                    b = k if d == 0 else 3 - k
                    col0 = CTX + b * NB
                    block(d, t, col0, EZ[:, d, 0, t, k:k + 1], EZ[:, d, 1, t, k:k + 1],
                          EZ[:, d, :, t, k + 1:k + 2], False, None, None, False, ["EZ"])
        stg = sb("s5stg", [128, 64])
        self.cp(stg[:].rearrange("p (a b t) -> p a b t", a=2, b=2), EZ[:, :, :, :, 4], ["EZ"], ["s5stg"])
        self.ld(self.cinA, stg[:], ["s5stg"], ["cinA"])
        self.gather(self.cinA, self.coutA, "cinA", "coutA")
        s5_finish(self, EZ, ET, HC, pw, full_block, sb, need_ctx)
        P.barrier()


def gather(self, cin, cout, rk, wk):
    self.P.collective(lambda e: e.collective_compute("AllGather", ALU.bypass, replica_groups=[list(range(8))],
                                                     ins=[cin.opt()], outs=[cout.opt()]),
                      reads=[rk], writes=[wk])


def s5_finish(self, EZ, ET, HC, pw, full_block, sb, need_ctx):
    fl = self.flags
    NB = 256
    G = sb("s5G", [128, 8, 64])
    self.ld(G[:], self.coutA.rearrange("(r p) c -> p r c", p=128), ["coutA"], ["s5G"])
    Gv = G[:].rearrange("p r (a b t) -> p r a b t", a=2, b=2)
    H = sb("s5H", [128, 2, 2, 16]); N1 = sb("s5N1", [128, 2, 16])
    ca1 = sb("ca1", [128, 16]); ca2 = sb("ca2", [128, 16])
    p1024 = pw["1024"]
    p256 = pw["256"]
    self.cp(H[:], HC[:], ["EZ"], ["s5H"])
    for d in range(2):
        order = range(8) if d == 0 else range(7, -1, -1)
        mo = 16 if d == 0 else 24
        hr, hi = H[:, d, 0, :], H[:, d, 1, :]
        pr, pi = p1024[0][:, d, :], p1024[1][:, d, :]
        n1, n2 = N1[:, 0, :], N1[:, 1, :]
        for j in order:
            self.cmul(n1, n2, pr, pi, hr, hi, ca1[:], ca2[:], ["s5H", "p1024"], ["s5N1"], "ca1", "ca2")
            self.tt(n1, n1, Gv[:, j, d, 0, :], ALU.add, ["s5N1", "s5G"], ["s5N1"])
            self.tt(n2, n2, Gv[:, j, d, 1, :], ALU.add, ["s5N1", "s5G"], ["s5N1"])
            self.tt(n1, n1, hr, ALU.subtract, ["s5N1", "s5H"], ["s5N1"])
            self.tt(n2, n2, hi, ALU.subtract, ["s5N1", "s5H"], ["s5N1"])
            mj = fl[:, mo + j:mo + j + 1]
            self.stt(hr, n1, mj, hr, ALU.mult, ALU.add, ["s5N1", "flags", "s5H"], ["s5H"])
            self.stt(hi, n2, mj, hi, ALU.mult, ALU.add, ["s5N1", "flags", "s5H"], ["s5H"])
    X = sb("s5X", [128, 2, 2, 16]); Y = sb("s5Y", [128, 2, 2, 16])
    self.cp(X[:], H[:], ["s5H"], ["s5X"])
    for k in range(4):
        self.tt(ET[:, :, :, :, k], EZ[:, :, :, :, k], X[:], ALU.add, ["EZ", "s5X"], ["ET"])
        if k < 3:
            for d in range(2):
                self.cmul(Y[:, d, 0, :], Y[:, d, 1, :], p256[0][:, d, :], p256[1][:, d, :], X[:, d, 0, :], X[:, d, 1, :],
                          ca1[:], ca2[:], ["s5X", "p256"], ["s5Y"], "ca1", "ca2")
            self.cp(X[:], Y[:], ["s5Y"], ["s5X"])
    for b in range(4):
        def inits(d, t, b=b):
            k = b if d == 0 else 3 - b
            return ET[:, d, 0, t, k:k + 1], ET[:, d, 1, t, k:k + 1], ["ET"]
        full_block(CTX + b * NB, C_LAT[0] + b * NB, inits)


KB.sincos = sincos
KB.cmul = cmul
KB.mix_s5 = mix_s5
KB.gather = gather
def mix_ssd(self, l, need_ctx):
    I = self.I
    P = self.P
    fl = self.flags
    NCH = 10
    with ExitStack() as st:
        sb = lambda n, s, dt=F32: self.sb(st, n, s, dt)
        XS = sb("XS", [128, 4, NR]); BM = sb("BM", [128, 2, NR], BF16); CMb = sb("CMb", [128, 2, NR], BF16)
        cw = sb("m2cw", [128, 8, 4]); cb = sb("m2cb", [128, 8])
        self.ld(cw[:], I["m2cw"][l], [], ["cw"]); self.ld(cb[:], I["m2cb"][l], [], ["cb"])
        identb = sb("identb", [128, 128], BF16)
        self.cp(identb[:], self.ident[:], ["ident"], ["identb"])
        with ExitStack() as st2:
            E = self.sb(st2, "cE", [128, LAT + 3]); acc = self.sb(st2, "cacc", [128, LAT])
            for j in range(8):
                r0 = 2560 + j * 128
                for (sname, Lr, mo) in (("lat", LAT, CTX), ("ctx", CTX, 0)):
                    if sname == "lat":
                        self.ld(E[:, 0:2], self.proj[r0:r0 + 128, C_HL[1] - 2:C_HL[1]], ["proj"], ["cE"])
                        self.ld(E[:, 2:2 + Lr], self.proj[r0:r0 + 128, C_LAT[0]:C_LAT[1]], ["proj"], ["cE"])
                        self.ld(E[:, 2 + Lr:3 + Lr], self.proj[r0:r0 + 128, C_HR[0]:C_HR[0] + 1], ["proj"], ["cE"], slow=True)
                        self.ts(E[:, 0:2], E[:, 0:2], fl[:, 32:33], None, ALU.mult, None, ["cE", "flags"], ["cE"])
                        self.ts(E[:, 2 + Lr:3 + Lr], E[:, 2 + Lr:3 + Lr], fl[:, 33:34], None, ALU.mult, None, ["cE", "flags"], ["cE"])
                    else:
                        self.vop("memset", [], ["cE"], E[:, 0:2], 0.0)
                        self.vop("memset", [], ["cE"], E[:, 2 + Lr:3 + Lr], 0.0)
                        self.ld(E[:, 2:2 + Lr], self.proj[r0:r0 + 128, 0:CTX], ["proj"], ["cE"])
                    self.ts(acc[:, 0:Lr], E[:, 0:Lr], cw[:, j, 0:1], None, ALU.mult, None, ["cE", "cw"], ["cacc"])
                    for k in range(1, 4):
                        self.stt(acc[:, 0:Lr], E[:, k:k + Lr], cw[:, j, k:k + 1], acc[:, 0:Lr], ALU.mult, ALU.add, ["cE", "cw", "cacc"], ["cacc"])
                    if j < 4:
                        dst, dk = XS[:, j, mo:mo + Lr], "XS"
                    elif j < 6:
                        dst, dk = BM[:, j - 4, mo:mo + Lr], "BM"
                    else:
                        dst, dk = CMb[:, j - 6, mo:mo + Lr], "CMb"
                    self.act(dst, acc[:, 0:Lr], AF.Silu, ["cacc", "cb"], [dk], bias=cb[:, j:j + 1])
            P.barrier()
        DT = sb("DT", [40, NR]); AC = sb("AC", [40, NR])
        TM = sb("TM", [128, NCH, 2, 40])
        ATB = sb("ATB", [128, NCH, 16]); EAT = sb("EAT", [128, NCH, 16])
        selh = sb("selh", [40, 16, 128])
        self.ld(selh[:], I["selh"], [], ["selh"])
        ones40 = sb("ones40", [40, 128])
        self.vop("memset", [], ["ones40"], ones40[:], 1.0)
        with ExitStack() as st2:
            dtb = self.sb(st2, "dtb", [40, 1]); alg = self.sb(st2, "alg", [40, 1]); av = self.sb(st2, "av", [40, 1])
            self.ld(dtb[:], I["m2dtb"][l], [], ["dtb"]); self.ld(alg[:], I["m2alog"][l], [], ["alg"])
            cmk = self.sb(st2, "cmk", [40, 2, NR])
            self.ld(cmk[:], I["cmask"], [], ["cmk"])
            self.vop("memset", [], ["DT"], DT[:], 0.0)
            for (p0, r0) in ((0, 3584), (32, 3592)):
                self.ld(DT[p0:p0 + 8, 0:CTX], self.proj[r0:r0 + 8, 0:CTX], ["proj", "DT"], ["DT"])
                self.ld(DT[p0:p0 + 8, CTX:NR], self.proj[r0:r0 + 8, C_LAT[0]:C_LAT[1]], ["proj", "DT"], ["DT"])
            self.act(DT[:], DT[:], AF.Exp, ["DT", "dtb"], ["DT"], bias=dtb[:, 0:1])
            self.act(DT[:], DT[:], AF.Ln, ["DT"], ["DT"], bias=self.epsc[0:40, 1:2])
            self.act(av[:], alg[:], AF.Exp, ["alg"], ["av"])
            self.ts(av[:], av[:], -1.0, None, ALU.mult, None, ["av"], ["av"])
            DTA = self.sb(st2, "DTA", [40, NR]); AC1 = self.sb(st2, "AC1", [40, NR])
            self.ts(DTA[:], DT[:], av[:, 0:1], None, ALU.mult, None, ["DT", "av"], ["DTA"])
            self.P.op("dve", lambda e: e.tensor_tensor_scan(out=AC[:], data0=cmk[:, 0, :], data1=DTA[:], initial=0.0,
                                                            op0=ALU.mult, op1=ALU.add), reads=["cmk", "DTA"], writes=["AC"])
            self.P.op("dve", lambda e: e.tensor_tensor_scan(out=AC1[:, ::-1], data0=cmk[:, 1, ::-1], data1=DTA[:, ::-1], initial=0.0,
                                                            op0=ALU.mult, op1=ALU.add), reads=["cmk", "DTA"], writes=["AC1"])
            self.cp(AC[32:40, :], AC1[32:40, :], ["AC1", "AC"], ["AC"])
            ATC = self.sb(st2, "ATC", [40, NCH]); RH = self.sb(st2, "RH", [40, NCH, 16])
            self.vop("memset", [], ["ATC"], ATC[:], 0.0)
            self.cp(ATC[0:8, :], AC[0:8, 127:NR:128], ["AC", "ATC"], ["ATC"])
            self.cp(ATC[32:40, :], AC[32:40, 0:NR:128], ["AC", "ATC"], ["ATC"])
            self.tt(RH[:], selh[:, :, 0:1].rearrange("p m o -> p o m").to_broadcast([40, NCH, 16]),
                    ATC[:].unsqueeze(2).to_broadcast([40, NCH, 16]), ALU.mult, ["selh", "ATC"], ["RH"])
            pa = self.ps(st2, "pa", [128, NCH * 16])
            self.mm(pa[:], ones40[:], RH[:].rearrange("p c m -> p (c m)"), True, True, ["ones40", "RH"], ["pa"])
            self.cp(ATB[:].rearrange("p c m -> p (c m)"), pa[:], ["pa"], ["ATB"])
            self.act(EAT[:], ATB[:], AF.Exp, ["ATB"], ["EAT"])
            pt = self.ps(st2, "ptm", [128, 2, 40])
            for c in range(NCH):
                cs = c * 128
                self.tr(pt[:, 0, :], DT[0:40, cs:cs + 128], self.ident[0:40, 0:40], ["DT", "ident"], ["ptm"])
                self.tr(pt[:, 1, :], AC[0:40, cs:cs + 128], self.ident[0:40, 0:40], ["AC", "ident"], ["ptm"])
                self.cp(TM[:, c], pt[:], ["ptm"], ["TM"])
            P.barrier()
        XD = sb("XD", [128, NCH, 2, 512], BF16)
        CS = sb("CS", [128, NCH, 2, 512], BF16)
        c0 = 0
        with ExitStack() as st2:
            pxs = [self.ps(st2, "pxs", [128, 512]) for _ in range(2)]
            pbc = self.ps(st2, "pbc", [128, 2, 128], BF16)
            pcs = [self.ps(st2, "pcs", [128, 512]) for _ in range(2)]
            bct = self.sb(st2, "bct", [128, 2, 128], BF16)
            de = self.sb(st2, "de", [128, 16]); cf = self.sb(st2, "cf", [128, 16])
            xdd = self.sb(st2, "xdd", [128, 2, 512], BF16)
            for c in range(NCH):
                cs = c * 128
                px = pxs[c % 2]; pxk = "pxs%d" % (c % 2)
                for j in range(4):
                    self.tr(px[:, j * 128:(j + 1) * 128], XS[:, j, cs:cs + 128], self.ident[:], ["XS", "ident"], [pxk])
                for g in range(2):
                    self.tr(pbc[:, g, :], BM[:, g, cs:cs + 128], identb[:], ["BM", "identb"], ["pbc"])
                self.cp(bct[:], pbc[:], ["pbc"], ["bct"])
                dt16 = TM[:, c, 0, :].rearrange("p (a b) -> p a b", b=8)[:, 0:5:4, :]
                ac16 = TM[:, c, 1, :].rearrange("p (a b) -> p a b", b=8)[:, 0:5:4, :]
                self.tt(de[:].rearrange("p (a b) -> p a b", b=8), ATB[:, c, :].rearrange("p (a b) -> p a b", b=8), ac16,
                        ALU.subtract, ["ATB", "TM"], ["de"])
                self.act(de[:], de[:], AF.Exp, ["de"], ["de"])
                self.tt(cf[:].rearrange("p (a b) -> p a b", b=8), de[:].rearrange("p (a b) -> p a b", b=8), dt16, ALU.mult,
                        ["de", "TM"], ["cf"])
                pxv = px[:].rearrange("p (h e) -> p h e", e=64)
                for d in range(2):
                    self.tt(XD[:, c, d, :].rearrange("p (h e) -> p h e", e=64), pxv,
                            dt16[:, d, :].unsqueeze(2).to_broadcast([128, 8, 64]), ALU.mult, [pxk, "TM"], ["XD"])
                    self.tt(xdd[:, d, :].rearrange("p (h e) -> p h e", e=64), pxv,
                            cf[:, d * 8:(d + 1) * 8].unsqueeze(2).to_broadcast([128, 8, 64]), ALU.mult, [pxk, "cf"], ["xdd"])
                for d in range(2):
                    pc = pcs[d]; pck = "pcs%d" % d
                    for g in range(2):
                        self.mm(pc[:, g * 256:(g + 1) * 256], bct[:, g, :], xdd[:, d, g * 256:(g + 1) * 256], True, True,
                                ["bct", "xdd"], [pck])
                    self.cp(CS[:, c, d, :], pc[:], [pck], ["CS"], eng="act")
            P.barrier()
        S = sb("Sst", [128, 2, 512]); HCs = sb("HCs", [128, 2, 512])
        stg = sb("dstg", [128, 1040])

        def chain(dst, d, order, init_key):
            for c in order:
                v = dst[:, d, :].rearrange("p (h e) -> p h e", e=64)
                self.tt(v, v, EAT[:, c, d * 8:(d + 1) * 8].unsqueeze(2).to_broadcast([128, 8, 64]), ALU.mult,
                        [init_key, "EAT"], [init_key])
                self.tt(dst[:, d, :], dst[:, d, :], CS[:, c, d, :], ALU.add, [init_key, "CS"], [init_key])

        self.vop("memset", [], ["Sst"], S[:], 0.0)
        self.vop("memset", [], ["HCs"], HCs[:], 0.0)
        chain(S, 0, range(2, 10), "Sst")
        chain(S, 1, range(9, 1, -1), "Sst")
        chain(HCs, 0, (0, 1), "HCs")
        chain(HCs, 1, (1, 0), "HCs")
        self.cp(stg[:, 0:1024], S[:].rearrange("p d n -> p (d n)"), ["Sst"], ["dstg"])
        self.vop("tensor_reduce", ["ATB"], ["dstg"], out=stg[:, 1024:1040], in_=ATB[:, 2:10, :].rearrange("p c m -> p m c"),
                 axis=mybir.AxisListType.X, op=ALU.add)
        self.ld(self.cinD, stg[:], ["dstg"], ["cinD"])
        self.gather(self.cinD, self.coutD, "cinD", "coutD")
        H = sb("Hin", [128, 2, 512])
        with ExitStack() as st2:
            cod = self.coutD.rearrange("(r p) c -> p r c", p=128)
            GT = self.sb(st2, "GT", [128, 8, 16])
            self.ld(GT[:], cod[:, :, 1024:1040], ["coutD"], ["GT"])
            EXG = self.sb(st2, "EXG", [128, 8, 16])
            self.act(EXG[:], GT[:], AF.Exp, ["GT"], ["EXG"])
            N = self.sb(st2, "Nn", [128, 512])
            Gj = [self.sb(st2, "Gj", [128, 512]) for _ in range(2)]
            self.cp(H[:], HCs[:], ["HCs"], ["Hin"])
            nn = 0
            for d in range(2):
                order = range(8) if d == 0 else range(7, -1, -1)
                mo = 16 if d == 0 else 24
                for j in order:
                    gj = Gj[nn % 2]; gk = "Gj%d" % (nn % 2)
                    nn += 1
                    self.ld(gj[:], cod[:, j, d * 512:(d + 1) * 512], ["coutD"], [gk])
                    self.tt(N[:].rearrange("p (h e) -> p h e", e=64), H[:, d, :].rearrange("p (h e) -> p h e", e=64),
                            EXG[:, j, d * 8:(d + 1) * 8].unsqueeze(2).to_broadcast([128, 8, 64]), ALU.mult, ["Hin", "EXG"], ["Nn"])
                    self.tt(N[:], N[:], gj[:], ALU.add, ["Nn", gk], ["Nn"])
                    self.tt(N[:], N[:], H[:, d, :], ALU.subtract, ["Nn", "Hin"], ["Nn"])
                    self.stt(H[:, d, :], N[:], fl[:, mo + j:mo + j + 1], H[:, d, :], ALU.mult, ALU.add, ["Nn", "flags", "Hin"], ["Hin"])
            P.barrier()
        SP = CS
        Tt = sb("Ttmp", [128, 512])

        def chain2(init_ap, d, order, ik):
            self.cp(S[:, d, :], init_ap, [ik, "Sst"], ["Sst"])
            for c in order:
                self.tt(Tt[:].rearrange("p (h e) -> p h e", e=64), S[:, d, :].rearrange("p (h e) -> p h e", e=64),
                        EAT[:, c, d * 8:(d + 1) * 8].unsqueeze(2).to_broadcast([128, 8, 64]), ALU.mult, ["Sst", "EAT"], ["Tt"])
                self.tt(Tt[:], Tt[:], CS[:, c, d, :], ALU.add, ["Tt", "CS"], ["Tt"])
                self.cp(CS[:, c, d, :], S[:, d, :], ["Sst", "CS"], ["CS"])
                self.cp(S[:, d, :], Tt[:], ["Tt"], ["Sst"])

        chain2(H[:, 0, :], 0, range(2, 10), "Hin")
        chain2(H[:, 1, :], 1, range(9, 1, -1), "Hin")
        if need_ctx:
            zz = sb("zz", [128, 512])
            self.vop("memset", [], ["zz"], zz[:], 0.0)
            chain2(zz[:], 0, (0, 1), "zz")
            chain2(zz[:], 1, (1, 0), "zz")
        P.barrier()
        mk = sb("mk", [128, 2, 128])
        self.ld(mk[:, 0, :], I["maskF"], [], ["mk"]); self.ld(mk[:, 1, :], I["maskB"], [], ["mk"])
        md = sb("m2d", [128, 4]); nw = sb("m2nw", [128, 4])
        self.ld(md[:], I["m2d"][l], [], ["md"]); self.ld(nw[:], I["m2nw"][l], [], ["nw"])
        onesn = sb("onesn", [128, 128])
        self.vop("memset", [], ["onesn"], onesn[:], 1.0 / 256.0)
        pg = self.ps(st, "pg", [128, 2, 128])
        prs = [self.ps(st, "pr", [128, 128]) for _ in range(2)]
        pys = [self.ps(st, "py", [128, 128]) for _ in range(2)]
        pms = self.ps(st, "pms", [128, 128])
        Gm = sb("Gm", [128, 2, 2, 128])
        E1 = sb("E1", [128, 128]); STb = sb("STb", [128, 128], BF16); ER = sb("ER", [128, 128]); ccs = sb("ccs", [128, 128], BF16)
        Y = sb("Yt", [128, 4, 128]); Z = sb("Zt", [128, 4, 128]); gsq = sb("gsq", [128, 4, 128]); rsd = sb("rsd", [128, 128])
        n = 0
        for c in range(0 if need_ctx else 2, NCH):
            cs = c * 128
            pcol = cs if c < 2 else C_LAT[0] + (c - 2) * 128
            for g in range(2):
                self.mm(pg[:, g, :], BM[:, g, cs:cs + 128], CMb[:, g, cs:cs + 128], True, True, ["BM", "CMb"], ["pg"])
            for d in range(2):
                for g in range(2):
                    self.tt(Gm[:, d, g, :], pg[:, g, :], mk[:, d, :], ALU.mult, ["pg", "mk"], ["Gm"])
            ac16 = TM[:, c, 1, :].rearrange("p (a b) -> p a b", b=8)[:, 0:5:4, :]
            for hp in range(4):
                py = pys[hp % 2]; pyk = "py%d" % (hp % 2)
                for hh in range(2):
                    h = hp * 2 + hh
                    g = h // 4
                    for d in range(2):
                        pr = prs[n % 2]; prk = "pr%d" % (n % 2)
                        n += 1
                        self.mm(pr[:], selh[:, d * 8 + h, :], AC[0:40, cs:cs + 128], True, True, ["selh", "AC"], [prk])
                        self.ts(E1[:], pr[:], ac16[:, d, h:h + 1], 0.0, ALU.subtract, ALU.min, [prk, "TM"], ["E1"])
                        self.act(E1[:], E1[:], AF.Exp, ["E1"], ["E1"])
                        self.tt(STb[:], E1[:], Gm[:, d, g, :], ALU.mult, ["E1", "Gm"], ["STb"])
                        self.act(ER[:], pr[:], AF.Exp, [prk], ["ER"])
                        self.tt(ccs[:], CMb[:, g, cs:cs + 128], ER[:], ALU.mult, ["CMb", "ER"], ["ccs"])
                        self.mm(py[hh * 64:(hh + 1) * 64, :], XD[:, c, d, h * 64:(h + 1) * 64], STb[:], d == 0, False, ["XD", "STb"], [pyk])
                        self.mm(py[hh * 64:(hh + 1) * 64, :], SP[:, c, d, h * 64:(h + 1) * 64], ccs[:], False, d == 1, ["CS", "ccs"], [pyk])
                self.stt(Y[:, hp, :], XS[:, hp, cs:cs + 128], md[:, hp:hp + 1], py[:], ALU.mult, ALU.add, ["XS", "md", pyk], ["Yt"])
            self.ld(Z[:], self.proj[2048:2560, pcol:pcol + 128].rearrange("(j p) n -> p j n", p=128), ["proj"], ["Zt"])
            self.act(Z[:], Z[:], AF.Silu, ["Zt"], ["Zt"])
            self.tt(Y[:], Y[:], Z[:], ALU.mult, ["Yt", "Zt"], ["Yt"])
            self.tt(gsq[:], Y[:], Y[:], ALU.mult, ["Yt"], ["gsq"])
            for gg in range(2):
                self.mm(pms[:], onesn[:], gsq[:, 2 * gg, :], True, False, ["onesn", "gsq"], ["pms"])
                self.mm(pms[:], onesn[:], gsq[:, 2 * gg + 1, :], False, True, ["onesn", "gsq"], ["pms"])
                self.act(rsd[:], pms[:], AF.Sqrt, ["pms"], ["rsd"], bias=self.epsc[:, 0:1])
                self.vop("reciprocal", ["rsd"], ["rsd"], out=rsd[:], in_=rsd[:])
                for jj in range(2):
                    j = 2 * gg + jj
                    self.stt(self.mix[:, 12 + j, cs:cs + 128], Y[:, j, :], nw[:, j:j + 1], rsd[:], ALU.mult, ALU.mult,
                             ["Yt", "nw", "rsd"], ["mix"])
        P.barrier()


KB.mix_ssd = mix_ssd
def ln_apply(self, st, R, rkey, blocks, lnp, gi, bi, writer):
    onesD = self.sb(st, "onesD", [128, 128])
    self.vop("memset", [], ["onesD"], onesD[:], 1.0 / D)
    pm = self.ps(st, "lnpm", [128, 512]); pq = self.ps(st, "lnpq", [128, 512])
    sq = [self.sb(st, "lnsq", [128, 512]) for _ in range(2)]
    mean = self.sb(st, "lnmean", [128, 512]); rstd = self.sb(st, "lnrstd", [128, 512]); tmp = self.sb(st, "lntmp", [128, 512])
    outs = [self.sb(st, "lnout", [128, 512]) for _ in range(2)]
    for (a, b) in blocks:
        n = b - a
        for k in range(16):
            self.mm(pm[:, 0:n], onesD[:], R[:, k, a:b], k == 0, k == 15, ["onesD", rkey], ["lnpm"])
        for k in range(16):
            s_ = sq[k % 2]; sk = "lnsq%d" % (k % 2)
            self.act(s_[:, 0:n], R[:, k, a:b], AF.Square, [rkey], [sk])
            self.mm(pq[:, 0:n], onesD[:], s_[:, 0:n], k == 0, k == 15, ["onesD", sk], ["lnpq"])
        self.cp(mean[:, 0:n], pm[:, 0:n], ["lnpm"], ["lnmean"])
        self.tt(tmp[:, 0:n], mean[:, 0:n], mean[:, 0:n], ALU.mult, ["lnmean"], ["lntmp"])
        self.tt(tmp[:, 0:n], pq[:, 0:n], tmp[:, 0:n], ALU.subtract, ["lnpq", "lntmp"], ["lntmp"])
        self.act(rstd[:, 0:n], tmp[:, 0:n], AF.Sqrt, ["lntmp"], ["lnrstd"], bias=self.epsc[:, 0:1])
        self.vop("reciprocal", ["lnrstd"], ["lnrstd"], out=rstd[:, 0:n], in_=rstd[:, 0:n])
        for k in range(16):
            o = outs[k % 2]; ok = "lnout%d" % (k % 2)
            self.tt(o[:, 0:n], R[:, k, a:b], mean[:, 0:n], ALU.subtract, [rkey, "lnmean"], [ok])
            self.tt(o[:, 0:n], o[:, 0:n], rstd[:, 0:n], ALU.mult, [ok, "lnrstd"], [ok])
            self.ts(o[:, 0:n], o[:, 0:n], lnp[:, gi, k:k + 1], lnp[:, bi, k:k + 1], ALU.mult, ALU.add, [ok, "lnp"], [ok])
            writer(k, a, b, o[:, 0:n], ok)


def phase_outproj(self, l):
    P = self.P
    I = self.I
    need_ctx = l < NL - 1
    blocks = RB if need_ctx else RB[1:]
    with ExitStack() as st:
        lnp = self.sb(st, "lnp", [128, 4, 16])
        self.ld(lnp[:], I["ln_p"][l], [], ["lnp"])
        R = self.sb(st, "Rres", [128, 16, NR])
        with ExitStack() as st2:
            wv = self.W["w_out"].rearrange("(k p) n -> p k n", p=128)
            slabs = [self.sb(st2, "ow", [128, 16, 512], BF16) for _ in range(2)]
            hst = [self.sb(st2, "ohst", [128, NR]) for _ in range(2)]
            pss = [self.ps(st2, "ops", [128, 512]) for _ in range(4)]
            pi = 0
            for cb in range(4):
                sl = slabs[cb % 2]; sk = "ow%d" % (cb % 2)
                self.ld(sl[:], wv[:, :, cb * 512:(cb + 1) * 512], [], [sk])
                for mi in range(4):
                    m = cb * 4 + mi
                    hs = hst[m % 2]; hk = "ohst%d" % (m % 2)
                    if need_ctx:
                        self.ld(hs[:, 0:CTX], self.hbuf[:, m, 0:CTX], ["hbuf"], [hk])
                    self.ld(hs[:, CTX:NR], self.hbuf[:, m, C_LAT[0]:C_LAT[1]], ["hbuf"], [hk])
                    a0 = blocks[0][0]
                    self.act(hs[:, a0:NR], hs[:, a0:NR], AF.Copy, [hk], [hk], scale=ALPHA)
                    for (a, b) in blocks:
                        pt = pss[pi % 4]; pk = "ops%d" % (pi % 4)
                        pi += 1
                        for k in range(16):
                            self.mm(pt[:, 0:b - a], sl[:, k, mi * 128:(mi + 1) * 128], self.mix[:, k, a:b], k == 0, k == 15,
                                    [sk, "mix"], [pk])
                        col = 1 if a < CTX else 0
                        self.stt(R[:, m, a:b], pt[:, 0:b - a], self.mod[:, 32 + m, col:col + 1], hs[:, a:b], ALU.mult, ALU.add,
                                 [pk, "mod", hk], ["Rres"])
            P.barrier()
        with ExitStack() as st2:
            def writer(k, a, b, tile, key):
                self.ld(self.h1buf[:, k, a:b], tile, [key], ["h1buf"])
            self.ln_apply(st2, R, "Rres", blocks, lnp, 0, 1, writer)
            P.barrier()


def phase_ffn(self, l):
    P = self.P
    I = self.I
    fl = self.flags
    need_ctx = l < NL - 1
    h1 = self.h1buf
    with ExitStack() as st:
        self.exchange(st, h1[:, :, CTX:CTX + 64], h1[:, :, NR - 64:NR], h1[:, :, NR:NR + 64], h1[:, :, NR + 64:NR + 128],
                      64, self.cin2, self.cout2, ["h1buf"], ["h1buf"], "e2")
    NE = NR + 128
    LT0 = CTX + 64
    with ExitStack() as st:
        hm2 = self.sb(st, "hm2", [128, 16, NE], BF16)
        fcw = self.sb(st, "fcw", [128, NFT, 9]); fcb = self.sb(st, "fcb", [128, NFT])
        self.ld(fcw[:], I["fcw"][l], [], ["fcw"]); self.ld(fcb[:], I["fcb"][l], [], ["fcb"])
        with ExitStack() as st2:
            hst = [self.sb(st2, "fhst", [128, NE]) for _ in range(2)]
            for k in range(16):
                hs = hst[k % 2]; hk = "fhst%d" % (k % 2)
                if need_ctx:
                    self.ld(hs[:, 0:CTX], h1[:, k, 0:CTX], ["h1buf"], [hk])
                self.ld(hs[:, CTX:LT0], h1[:, k, NR:NR + 64], ["h1buf"], [hk])
                self.ld(hs[:, LT0:LT0 + LAT], h1[:, k, CTX:NR], ["h1buf"], [hk])
                self.ld(hs[:, LT0 + LAT:NE], h1[:, k, NR + 64:NR + 128], ["h1buf"], [hk])
                if need_ctx:
                    self.ts(hm2[:, k, 0:CTX], hs[:, 0:CTX], self.mod1p[:, 64 + k, 1:2], self.mod[:, 48 + k, 1:2], ALU.mult, ALU.add,
                            [hk, "mod", "mod1p"], ["hm2"])
                self.ts(hm2[:, k, CTX:NE], hs[:, CTX:NE], self.mod1p[:, 64 + k, 0:1], self.mod[:, 48 + k, 0:1], ALU.mult, ALU.add,
                        [hk, "mod", "mod1p"], ["hm2"])
            P.barrier()
        wv = self.W["ffn_w_up"].rearrange("(k p) n -> p k n", p=128)
        slabs = [self.sb(st, "uw", [128, 16, 256], BF16) for _ in range(2)]
        if need_ctx:
            ggrp = [(0, 512), (512, 1024), (1024, NE)]
            vgrp = [(0, 512), (512, 1024), (1024, LT0 + LAT)]
        else:
            ggrp = [(CTX, CTX + 384), (CTX + 384, CTX + 768), (CTX + 768, NE)]
            vgrp = [(LT0, LT0 + 512), (LT0 + 512, LT0 + LAT)]
        g0 = ggrp[0][0]
        v0 = vgrp[0][0]
        Gs = [self.ps(st, "fG", [128, 512]) for _ in ggrp]
        Vs = [self.ps(st, "fV", [128, 512]) for _ in vgrp]
        gsb = self.sb(st, "gsb", [128, NE]); cv = self.sb(st, "fcv", [128, NR]); gt = self.sb(st, "fgt", [128, NR])
        acts = [self.sb(st, "fact", [128, NR], BF16) for _ in range(2)]
        for f in range(NFT):
            sl = slabs[f % 2]; sk = "uw%d" % (f % 2)
            self.ld(sl[:, :, 0:128], wv[:, :, f * 128:(f + 1) * 128], [], [sk])
            self.ld(sl[:, :, 128:256], wv[:, :, FH + f * 128:FH + (f + 1) * 128], [], [sk])
            ab = acts[f % 2]; ak = "fact%d" % (f % 2)
            for gi, (a, b) in enumerate(ggrp):
                for k in range(16):
                    self.mm(Gs[gi][:, 0:b - a], sl[:, k, 0:128], hm2[:, k, a:b], k == 0, k == 15, [sk, "hm2"], ["fG%d" % gi])
            for vi, (a, b) in enumerate(vgrp):
                for k in range(16):
                    self.mm(Vs[vi][:, 0:b - a], sl[:, k, 128:256], hm2[:, k, a:b], k == 0, k == 15, [sk, "hm2"], ["fV%d" % vi])
            for gi, (a, b) in enumerate(ggrp):
                self.cp(gsb[:, a:b], Gs[gi][:, 0:b - a], ["fG%d" % gi], ["gsb"], eng="act")
            self.ts(gsb[:, CTX:LT0], gsb[:, CTX:LT0], fl[:, 32:33], None, ALU.mult, None, ["gsb", "flags"], ["gsb"])
            self.ts(gsb[:, LT0 + LAT:NE], gsb[:, LT0 + LAT:NE], fl[:, 33:34], None, ALU.mult, None, ["gsb", "flags"], ["gsb"])
            g3 = gsb[:, CTX:NE].rearrange("p (r c) -> p r c", c=64)
            o3 = cv[:, CTX:NR].rearrange("p (r c) -> p r c", c=64)
            self.ts(o3, g3[:, 1:17, :], fcw[:, f, 4:5], None, ALU.mult, None, ["gsb", "fcw"], ["fcv"])
            for dy in range(3):
                for dx in range(3):
                    if dy == 1 and dx == 1:
                        continue
                    if dx == 0:
                        src, dst = g3[:, dy:dy + 16, 0:63], o3[:, :, 1:64]
                    elif dx == 1:
                        src, dst = g3[:, dy:dy + 16, :], o3
                    else:
                        src, dst = g3[:, dy:dy + 16, 1:64], o3[:, :, 0:63]
                    w = fcw[:, f, dy * 3 + dx:dy * 3 + dx + 1]
                    self.stt(dst, src, w, dst, ALU.mult, ALU.add, ["gsb", "fcw", "fcv"], ["fcv"])
            c0 = 0 if need_ctx else CTX
            if need_ctx:
                self.ts(cv[:, 0:CTX], gsb[:, 0:CTX], fcw[:, f, 4:5], None, ALU.mult, None, ["gsb", "fcw"], ["fcv"])
                self.stt(cv[:, 1:CTX], gsb[:, 0:CTX - 1], fcw[:, f, 3:4], cv[:, 1:CTX], ALU.mult, ALU.add, ["gsb", "fcw", "fcv"], ["fcv"])
                self.stt(cv[:, 0:CTX - 1], gsb[:, 1:CTX], fcw[:, f, 5:6], cv[:, 0:CTX - 1], ALU.mult, ALU.add, ["gsb", "fcw", "fcv"], ["fcv"])
            self.ts(cv[:, c0:NR], cv[:, c0:NR], fcb[:, f:f + 1], None, ALU.add, None, ["fcv", "fcb"], ["fcv"])
            self.gelu(cv[:, c0:NR], cv[:, c0:NR], gt[:, c0:NR], ["fcv"], ["fcv"], "fgt")
            for vi, (a, b) in enumerate(vgrp):
                pieces = []
                if a < CTX:
                    pieces.append((a, min(b, CTX), a))
                la, lb = max(a, LT0), b
                if lb > la:
                    pieces.append((la, lb, la - 64))
                for (pa, pb, ra) in pieces:
                    self.tt(ab[:, ra:ra + pb - pa], cv[:, ra:ra + pb - pa], Vs[vi][:, pa - a:pb - a], ALU.mult, ["fcv", "fV%d" % vi], [ak])
            self.ld(self.actbuf[f, :, c0:NR], ab[:, c0:NR], [ak], ["actbuf"])
        P.barrier()
    blocks = RB if need_ctx else RB[1:]
    a0 = blocks[0][0]
    with ExitStack() as st:
        A = self.sb(st, "Aact", [128, NFT, NR], BF16)
        for f in range(NFT):
            self.ld(A[:, f, a0:NR], self.actbuf[f, :, a0:NR], ["actbuf"], ["Aact"], q=("sp" if f % 2 == 0 else "act"))
        wv = self.W["ffn_w_down"].rearrange("(f p) n -> p f n", p=128)
        slabs = [self.sb(st, "dw", [128, NFT, 128], BF16) for _ in range(2)]
        hst = [self.sb(st, "dhst", [128, NR]) for _ in range(2)]
        pss = [self.ps(st, "dps", [128, 512]) for _ in range(4)]
        pi = 0
        for m in range(16):
            sl = slabs[m % 2]; sk = "dw%d" % (m % 2)
            self.ld(sl[:], wv[:, :, m * 128:(m + 1) * 128], [], [sk])
            hs = hst[m % 2]; hk = "dhst%d" % (m % 2)
            self.ld(hs[:, a0:NR], h1buf_real(self, m, a0), ["h1buf"], [hk])
            self.act(hs[:, a0:NR], hs[:, a0:NR], AF.Copy, [hk], [hk], scale=ALPHA)
            for (a, b) in blocks:
                pt = pss[pi % 4]; pk = "dps%d" % (pi % 4)
                pi += 1
                for f in range(NFT):
                    self.mm(pt[:, 0:b - a], sl[:, f, :], A[:, f, a:b], f == 0, f == NFT - 1, [sk, "Aact"], [pk])
                col = 1 if a < CTX else 0
                self.stt(hs[:, a:b], pt[:, 0:b - a], self.mod[:, 80 + m, col:col + 1], hs[:, a:b], ALU.mult, ALU.add,
                         [pk, "mod", hk], [hk])
            self.ld(self.r2buf[:, m, a0:NR], hs[:, a0:NR], [hk], ["r2buf"])
        P.barrier()
    with ExitStack() as st:
        lnp = self.sb(st, "lnp2", [128, 4, 16])
        self.ld(lnp[:], I["ln_p"][l], [], ["lnp"])
        R = self.sb(st, "R2", [128, 16, NR])
        for k in range(16):
            self.ld(R[:, k, a0:NR], self.r2buf[:, k, a0:NR], ["r2buf"], ["R2"], q=("sp" if k % 2 == 0 else "act"))

        def writer(k, a, b, tile, key):
            if a < CTX:
                self.ld(self.hbuf[:, k, a:b], tile, [key], ["hbuf"])
            else:
                self.ld(self.hbuf[:, k, C_LAT[0] + a - CTX:C_LAT[0] + b - CTX], tile, [key], ["hbuf"])
        self.ln_apply(st, R, "R2", blocks, lnp, 2, 3, writer)
        P.barrier()


def h1buf_real(self, m, a0):
    return self.h1buf[:, m, a0:NR]


KB.ln_apply = ln_apply
KB.phase_outproj = phase_outproj
KB.phase_ffn = phase_ffn
def build(com, per0, nlayers=NL, dbg=(), stop=None):
    stop2 = None
    if stop and stop.startswith("L2"):
        stop2 = stop[2:]
        stop = None
    kb = KB(com, per0, nlayers, dbg)
    P = kb.P
    with ExitStack() as st:
        kb.phase_setup(st)
        for l in range(nlayers):
            if stop == "setup":
                kb.dump("hbuf", kb.hbuf, [128, 16, NT], "hbuf")
                break
            def s2(name):
                return l == 1 and stop2 == name
            kb.prep_weights(l)
            kb.phase_e0()
            if s2("e0"):
                break
            if stop == "e0":
                kb.dump("hbuf", kb.hbuf, [128, 16, NT], "hbuf")
                break
            kb.phase_ada(l)
            if stop == "ada":
                break
            kb.phase_inproj(l)
            if s2("inproj"):
                break
            if l == 0:
                kb.dump("proj0", kb.proj[0:INW], [INW, NT], "proj")
            if stop == "inproj":
                break
            with ExitStack() as stl:
                kb.mix = kb.sb(stl, "mix", [128, 16, NR], BF16)
                kb.phase_mixers(l)
                if stop == "mix":
                    break
                if s2("mix"):
                    break
                kb.phase_outproj(l)
                if s2("ln1"):
                    break
            if stop == "ln1":
                kb.dump("h1", kb.h1buf[:, :, 0:NR], [128, 16, NR], "h1buf")
                break
            kb.phase_ffn(l)
            if stop == "ffn":
                kb.dump("h2", kb.hbuf, [128, 16, NT], "hbuf")
                break
        if stop is None:
            kb.ld(kb.out, kb.hbuf[:, :, C_LAT[0]:C_LAT[1]], ["hbuf"], ["out"])
        P.emit()
    return kb


_CACHE = {}


def run(inputs, nlayers=NL, dbg=(), stop=None, trace=False):
    com, per = host_prep({k: np.asarray(v) for k, v in inputs.items()})
    kb = build(com, per[0], nlayers, dbg, stop)
    print("ninstr", kb.P.ninstr(), flush=True)
    in_maps = [dict(com, **per[c]) for c in range(8)]
    if trace:
        res = run_bass_kernel_spmd(kb.nc, in_maps, core_ids=list(range(8)), trace=True)
        print("EXEC_NS", res.exec_time_ns, flush=True)
    else:
        res = run_bass_kernel_spmd(kb.nc, in_maps, core_ids=list(range(8)))
    return res.results


def kernel(**inputs):
    r = run(inputs)
    out = np.zeros((2, 4096, D), np.float32)
    for c in range(8):
        b, k = divmod(c, 4)
        o = r[c]["out"]
        out[b, k * LAT:(k + 1) * LAT, :] = o.transpose(2, 1, 0).reshape(LAT, D)
    return out
```

```python
import math
from contextlib import ExitStack
import numpy as np
import concourse.bass as bass
import concourse.mybir as mybir
from concourse.bass_utils import run_bass_kernel_spmd

D = 2048
KT = 16
CTX = 256
LAT = 1024
HAL = 8
NT = CTX + 2 * HAL + LAT
C_CTX = (0, 256)
C_HL = (256, 264)
C_HR = (264, 272)
C_LAT = (272, 1296)
NR = CTX + LAT
INW = 3600
NMT = 29
FH = 5632
NFT = 44
ALPHA = 4 ** 0.25
LN_EPS = 1e-5
NL = 2
TWO_PI = 2.0 * math.pi
BIGW = ("w_in", "w_out", "ffn_w_up", "ffn_w_down")


def host_prep(inp):
    f = np.float32
    L = NL
    com = {}
    for k in ("s5_glu_w", "pool_w"):
        com[k] = np.ascontiguousarray(inp[k], dtype=f)
    cv3 = np.stack([inp["c"][0].reshape(16, 128).T, inp["c"][1].reshape(16, 128).T, inp["c_ctx"].reshape(16, 128).T], 2)
    com["cvec3"] = np.ascontiguousarray(cv3, dtype=f)
    lnp = np.stack([inp[k].reshape(L, 16, 128).transpose(0, 2, 1) for k in ("ln1_g", "ln1_b", "ln2_g", "ln2_b")], 2)
    com["ln_p"] = np.ascontiguousarray(lnp)
    s5p = np.zeros((L, 2, 128, 3, 16), f)
    for l in range(L):
        for d in range(2):
            s5p[l, d, :, 0, :] = inp["s5_a_re"][l, d].reshape(16, 128).T
            s5p[l, d, :, 1, :] = inp["s5_a_im"][l, d].reshape(16, 128).T
            s5p[l, d, :, 2, :] = np.repeat(inp["s5_log_step"][l, d], 64).reshape(16, 128).T
    com["s5p"] = s5p
    BT = np.zeros((L, 2, 2, 128, 4, 2, 128), f)
    CT = np.zeros((L, 2, 2, 128, 16, 64), f)
    for l in range(L):
        for d in range(2):
            for ri, (bk, ck) in enumerate((("s5_b_re", "s5_c_re"), ("s5_b_im", "s5_c_im"))):
                Bm = inp[bk][l, d]
                Cm = inp[ck][l, d]
                for t in range(16):
                    q, r = divmod(t, 4)
                    rb, e = divmod(r, 2)
                    for gi in range(2):
                        g = 2 * t + gi
                        p0 = 64 * rb + 32 * e + gi * 16
                        BT[l, d, ri, p0:p0 + 16, q, e, gi * 64:(gi + 1) * 64] = Bm[g].T
                        CT[l, d, ri, gi * 64:(gi + 1) * 64, t, 32 * e + gi * 16:32 * e + gi * 16 + 16] = Cm[g].T
    com["s5BT"] = BT
    com["s5CT"] = CT
    com["s5d"] = np.ascontiguousarray(inp["s5_d"].reshape(L, 4, 128).transpose(0, 2, 1))
    com["s5glub"] = np.ascontiguousarray(inp["s5_glu_b"].reshape(L, 4, 128).transpose(0, 2, 1))
    com["sglnG"] = np.ascontiguousarray(np.broadcast_to(inp["sg_ln_g"].reshape(L, 1, 4, 128), (L, 128, 4, 128)))
    com["sglnB"] = np.ascontiguousarray(np.broadcast_to(inp["sg_ln_b"].reshape(L, 1, 4, 128), (L, 128, 4, 128)))
    com["sgwT"] = np.ascontiguousarray(inp["sg_w"].transpose(0, 3, 1, 2))
    com["sgb"] = np.ascontiguousarray(np.broadcast_to(inp["sg_b"].reshape(L, 1, 4, 128), (L, 128, 4, 128)))
    com["poolb"] = np.ascontiguousarray(inp["pool_b"].reshape(L, 4, 128).transpose(0, 2, 1))
    com["pools"] = np.ascontiguousarray(inp["pool_scale"].reshape(L, 4, 128).transpose(0, 2, 1))
    com["m2cw"] = np.ascontiguousarray(inp["m2_conv_w"].reshape(L, 4, 8, 128).transpose(0, 3, 2, 1))
    com["m2cb"] = np.ascontiguousarray(inp["m2_conv_b"].reshape(L, 8, 128).transpose(0, 2, 1))
    dtb = np.zeros((L, 40, 1), f)
    alog = np.zeros((L, 40, 1), f)
    for l in range(L):
        dtb[l, 0:8, 0] = inp["m2_dt_bias"][l, 0]
        dtb[l, 32:40, 0] = inp["m2_dt_bias"][l, 1]
        alog[l, 0:8, 0] = inp["m2_a_log"][l, 0]
        alog[l, 32:40, 0] = inp["m2_a_log"][l, 1]
    com["m2dtb"] = dtb
    com["m2alog"] = alog
    com["m2d"] = np.ascontiguousarray(np.repeat(inp["m2_d"], 64, axis=1).reshape(L, 4, 128).transpose(0, 2, 1))
    com["m2nw"] = np.ascontiguousarray(inp["m2_norm_w"].reshape(L, 4, 128).transpose(0, 2, 1))
    com["fcw"] = np.ascontiguousarray(inp["ffn_conv_w"].reshape(L, 9, NFT, 128).transpose(0, 3, 2, 1))
    com["fcb"] = np.ascontiguousarray(inp["ffn_conv_b"].reshape(L, NFT, 128).transpose(0, 2, 1))
    com["ident"] = np.eye(128, dtype=f)
    com["iota1"] = np.ascontiguousarray(np.broadcast_to(np.arange(1, 257, dtype=f)[None, :], (128, 256)))
    jj = np.arange(128)[:, None]
    ii = np.arange(128)[None, :]
    com["maskF"] = (jj <= ii).astype(f)
    com["maskB"] = (jj >= ii).astype(f)
    cm = np.ones((40, 2, NR), f)
    cm[:, 0, 0::128] = 0.0
    cm[:, 1, 127::128] = 0.0
    com["cmask"] = cm
    selh = np.zeros((40, 16, 128), f)
    for d in range(2):
        for h in range(8):
            selh[32 * d + h, d * 8 + h, :] = 1.0
    com["selh"] = selh
    per = []
    for c in range(8):
        b, k = divmod(c, 4)
        pc = {}
        xs = inp["x"][b, k * LAT:(k + 1) * LAT, :]
        pc["x_fm"] = np.ascontiguousarray(xs.reshape(LAT, 16, 128).transpose(2, 1, 0))
        pc["ctx_fm"] = np.ascontiguousarray(inp["ctx"][b].reshape(CTX, 16, 128).transpose(2, 1, 0))
        pc["w_ada_c"] = np.ascontiguousarray(inp["w_ada"][:, :, c * 1536:(c + 1) * 1536], dtype=f)
        pc["b_ada_c"] = np.ascontiguousarray(inp["b_ada"][:, c * 1536:(c + 1) * 1536].reshape(L, 12, 128).transpose(0, 2, 1), dtype=f)
        fl = np.zeros((128, 36), f)
        if k > 0:
            fl[:, c - 1] = 1.0
            fl[:, 32] = 1.0
        if k < 3:
            fl[:, 8 + c + 1] = 1.0
            fl[:, 33] = 1.0
        for j in range(8):
            if j // 4 == b and j < c:
                fl[:, 16 + j] = 1.0
            if j // 4 == b and j > c:
                fl[:, 24 + j] = 1.0
        fl[:, 34 + b] = 1.0
        pc["flags"] = fl
        for wk in BIGW:
            w = inp[wk]
            rows = w.shape[1] // 8
            pc[wk] = np.ascontiguousarray(w[:, c * rows:(c + 1) * rows, :], dtype=f)
        per.append(pc)
    return com, per

ENGS = ("pe", "dve", "act", "pool", "sp")
NDMA_SEM = 6


class Prog:
    def __init__(self, nc, same_engine_sync=True):
        self.nc = nc
        self.same = same_engine_sync
        self.ops = {e: [] for e in ENGS}
        self.sems = {}
        self.semval = {}
        self.seen = {e: {} for e in ENGS}
        self.last_w = {}
        self.readers = {}
        self.dma_rr = {e: 0 for e in ENGS}
        self._ctx = []
        for e in ENGS:
            self._mksem("c_" + e)
            for i in range(NDMA_SEM):
                self._mksem("d_%s_%d" % (e, i))
        self._mksem("cc")
        for i in range(4):
            self._mksem("bg%d" % i)
        self._mksem("bgcc")
        self.bg_w = {}
        self.bg_rr = 0
        self._blk_cm = nc.Block()
        self.block = self._blk_cm.__enter__()
        self.n = {e: 0 for e in ENGS}

    def _mksem(self, name):
        cm = self.nc.semaphore(name)
        s = cm.__enter__()
        self._ctx.append(cm)
        self.sems[name] = s
        self.semval[name] = 0

    def _collect(self, eng, reads, writes, is_pe_acc=False):
        need = {}

        def add(ev):
            if ev is None:
                return
            s, v, src = ev
            if src == eng and (not self.same or (eng == "pe")):
                return
            if self.seen[eng].get(s, 0) >= v:
                return
            if need.get(s, 0) < v:
                need[s] = v

        for k in reads:
            add(self.last_w.get(k))
            add(self.bg_w.get(k))
        for k in writes:
            add(self.last_w.get(k))
            add(self.bg_w.get(k))
            for r in self.readers.get(k, ()):
                add(r)
        for s, v in need.items():
            self.seen[eng][s] = v
        return list(need.items())

    def _commit(self, ev, reads, writes):
        for k in reads:
            self.readers.setdefault(k, []).append(ev)
        for k in writes:
            self.last_w[k] = ev
            self.readers[k] = []

    def op(self, eng, fn, reads=(), writes=()):
        waits = self._collect(eng, reads, writes)
        s = "c_" + eng
        self.semval[s] += 1
        v = self.semval[s]
        self._emit(eng, waits, fn, s, 1)
        self._commit((s, v, eng), reads, writes)

    def dma(self, q, fn, reads=(), writes=(), block=False):
        i = self.dma_rr[q]
        self.dma_rr[q] = (i + 1) % NDMA_SEM
        s = "d_%s_%d" % (q, i)
        waits = self._collect(q, reads, writes)
        pv = self.semval[s]
        if pv > 0 and self.seen[q].get(s, 0) < pv:
            waits.append((s, pv))
            self.seen[q][s] = pv
        self.semval[s] += 16
        v = self.semval[s]
        self._emit(q, waits, fn, s, 16)
        self._commit((s, v, "dma"), reads, writes)
        if block:
            self.seen[q][s] = v
            self._emit(q, [(s, v)], None)

    def bg_dma(self, fn, reads=(), writes=()):
        q = "pool"
        i = self.bg_rr
        self.bg_rr = (i + 1) % 4
        s = "bg%d" % i
        waits = self._collect(q, reads, writes)
        self.semval[s] += 16
        v = self.semval[s]
        self._emit(q, waits, fn, s, 16)
        for k in writes:
            self.bg_w[k] = (s, v, "bg")
        self.seen[q][s] = v
        self._emit(q, [(s, v)], None)

    def bg_collective(self, fn, reads=(), writes=()):
        q = "pool"
        waits = self._collect(q, reads, writes)
        s = "bgcc"
        self.semval[s] += 1
        v = self.semval[s]
        self._emit(q, waits, fn, s, 1)
        for k in writes:
            self.bg_w[k] = (s, v, "bg")
        self.seen[q][s] = v
        self._emit(q, [(s, v)], None)

    def collective(self, fn, reads=(), writes=()):
        q = "pool"
        waits = self._collect(q, reads, writes)
        s = "cc"
        self.semval[s] += 1
        v = self.semval[s]
        self._emit(q, waits, fn, s, 1)
        self._commit((s, v, "cc"), reads, writes)
        self.seen[q][s] = v
        self._emit(q, [(s, v)], None)

    ENGMAP = {"pe": "tensor", "dve": "vector", "act": "scalar", "pool": "gpsimd", "sp": "sync"}

    def _emit(self, e, waits, fn, s=None, inc=0):
        self.n[e] += 1

        def body(engobj):
            for ws, wv in waits:
                engobj.wait_ge(self.sems[ws], wv)
            if fn is not None:
                inst = fn(engobj)
                inst.then_inc(self.sems[s], inc)

        getattr(self.block, self.ENGMAP[e])(body)

    def barrier(self, final=False):
        for e in ENGS:
            waits = []
            for s, v in self.semval.items():
                if s.startswith("bg") and not final:
                    continue
                if v > 0 and self.seen[e].get(s, 0) < v:
                    waits.append((s, v))
                    self.seen[e][s] = v
            if waits:
                self._emit(e, waits, None)
        self.last_w.clear()
        self.readers.clear()

    def emit(self, final_keys=()):
        self.barrier(final=True)
        self._blk_cm.__exit__(None, None, None)
        for cm in reversed(self._ctx):
            cm.__exit__(None, None, None)
        self._ctx = []

    def ninstr(self):
        return dict(self.n)
F32 = mybir.dt.float32
BF16 = mybir.dt.bfloat16
I32 = mybir.dt.int32
ALU = mybir.AluOpType
AF = mybir.ActivationFunctionType
TOKB = [(0, 272), (272, 784), (784, 1296)]
RB = [(0, 256), (256, 768), (768, 1280)]


class KB:
    def __init__(self, com, per0, nlayers=NL, dbg=()):
        self.nc = nc = bass.Bass("TRN2", target_bir_lowering=False)
        self.P = Prog(nc)
        self.cms = []
        self.I = {}
        for k, v in list(com.items()) + list(per0.items()):
            self.I[k] = nc.dram_tensor(k, list(v.shape), F32, kind="ExternalInput").ap()
        self.out = nc.dram_tensor("out", [128, 16, LAT], F32, kind="ExternalOutput").ap()
        self.dbg = {}
        self.dbg_want = dbg
        self.nlayers = nlayers
        dr = lambda n, s, dt=F32: nc.dram_tensor(n, s, dt).ap()
        self.hbuf = dr("hbuf", [128, 16, NT])
        self.proj = dr("proj", [NMT * 128, NT])
        self.h1buf = dr("h1buf", [128, 16, NR + 128])
        self.actbuf = dr("actbuf", [NFT, 128, NR], BF16)
        self.r2buf = dr("r2buf", [128, 16, NR])
        self.cin0 = dr("cin0", [256, 128]); self.cout0 = dr("cout0", [2048, 128])
        self.cin2 = dr("cin2", [256, 1024]); self.cout2 = dr("cout2", [2048, 1024])
        self.cinM = dr("cinM", [128, 36]); self.coutM = dr("coutM", [1024, 36])
        self.cinA = dr("cinA", [128, 64]); self.coutA = dr("coutA", [1024, 64])
        self.cinD = dr("cinD", [128, 1040]); self.coutD = dr("coutD", [1024, 1040])
        self.uid = 0

    def sb(self, st, name, shape, dt=F32):
        self.uid += 1
        return st.enter_context(self.nc.sbuf_tensor("%s_%d" % (name, self.uid), shape, dt))

    def ps(self, st, name, shape, dt=F32):
        self.uid += 1
        return st.enter_context(self.nc.psum_tensor("%s_%d" % (name, self.uid), shape, dt))

    def dump(self, name, src_ap, shape, key):
        if name not in self.dbg_want:
            return
        t = self.nc.dram_tensor("dbg_" + name, shape, F32, kind="ExternalOutput").ap()
        self.dbg[name] = t
        n0 = shape[0]
        step = 128 if n0 > 128 else (n0 if len(shape) < 3 else 16)
        for a in range(0, n0, step):
            b = min(n0, a + step)
            self.P.dma("sp", lambda e, a=a, b=b: e.dma_start(out=t[a:b], in_=src_ap[a:b]), reads=[key], writes=["dbg_" + name])

    def mm(self, out, lhsT, rhs, start, stop, r, w):
        self.P.op("pe", lambda e: e.matmul(out, lhsT=lhsT, rhs=rhs, start=start, stop=stop), reads=r, writes=w)

    def tr(self, out, in_, ident, r, w):
        self.P.op("pe", lambda e: e.transpose(out, in_=in_, identity=ident), reads=r, writes=w)

    def tt(self, out, a, b, op, r, w, eng="dve"):
        self.P.op(eng, lambda e: e.tensor_tensor(out=out, in0=a, in1=b, op=op), reads=r, writes=w)

    def ts(self, out, a, s1, s2, op0, op1, r, w, eng="dve"):
        if op1 is None:
            self.P.op(eng, lambda e: e.tensor_scalar(out=out, in0=a, scalar1=s1, scalar2=None, op0=op0), reads=r, writes=w)
        else:
            self.P.op(eng, lambda e: e.tensor_scalar(out=out, in0=a, scalar1=s1, scalar2=s2, op0=op0, op1=op1), reads=r, writes=w)

    def stt(self, out, a, s, b, op0, op1, r, w):
        self.P.op("dve", lambda e: e.scalar_tensor_tensor(out=out, in0=a, scalar=s, in1=b, op0=op0, op1=op1), reads=r, writes=w)

    def act(self, out, a, func, r, w, bias=0.0, scale=1.0):
        self.P.op("act", lambda e: e.activation(out=out, in_=a, func=func, bias=bias, scale=scale), reads=r, writes=w)

    def cp(self, out, a, r, w, eng="dve"):
        if eng == "act":
            self.P.op("act", lambda e: e.activation(out=out, in_=a, func=AF.Copy), reads=r, writes=w)
        else:
            self.P.op(eng, lambda e: e.tensor_copy(out=out, in_=a), reads=r, writes=w)

    def ld(self, out, src, r, w, q="sp", slow=False):
        if slow:
            self.P.dma(q, lambda e: e.dma_start(out=out, in_=src, allow_slow_non_contiguous=True), reads=r, writes=w)
        else:
            self.P.dma(q, lambda e: e.dma_start(out=out, in_=src), reads=r, writes=w)

    def gelu(self, out, x, t, r, w, tmpkey):
        self.tt(t, x, x, ALU.mult, r, [tmpkey])
        self.ts(t, t, 0.044715, 1.0, ALU.mult, ALU.add, [tmpkey], [tmpkey])
        self.tt(t, t, x, ALU.mult, r + [tmpkey], [tmpkey])
        self.act(t, t, AF.Sigmoid, [tmpkey], [tmpkey], scale=1.5957691216057308)
        self.tt(out, t, x, ALU.mult, r + [tmpkey], w)

    def vop(self, name, r, w, *args, **kw):
        self.P.op("dve", lambda e: getattr(e, name)(*args, **kw), reads=r, writes=w)

    def exchange(self, st0, first_ap, last_ap, hl_ap, hr_ap, n, cin, cout, rkeys, wkeys, tag):
        P = self.P
        w = 16 * n
        ci = cin.rearrange("(s p) (k j) -> s p k j", s=2, k=16)
        self.ld(ci[0], first_ap, rkeys, [tag + "cin"], q="sp")
        self.ld(ci[1], last_ap, rkeys, [tag + "cin"], q="sp")
        P.collective(lambda e: e.collective_compute("AllGather", ALU.bypass, replica_groups=[list(range(8))],
                                                    ins=[cin.opt()], outs=[cout.opt()]),
                     reads=[tag + "cin"], writes=[tag + "cout"])
        co = cout.rearrange("(r s p) c -> s p r c", r=8, s=2)
        with ExitStack() as st:
            g = self.sb(st, "exg", [128, 8, w])
            acc = self.sb(st, "exa", [128, w])
            fl = self.flags
            for side, (src, dst, so) in enumerate(((co[1], hl_ap, 0), (co[0], hr_ap, 8))):
                gk, ak = tag + "g", tag + "a"
                self.ld(g[:], src, [tag + "cout"], [gk])
                self.ts(acc[:], g[:, 0, :], fl[:, so:so + 1], None, ALU.mult, None, [gk], [ak])
                for j in range(1, 8):
                    self.stt(acc[:], g[:, j, :], fl[:, so + j:so + j + 1], acc[:], ALU.mult, ALU.add, [gk, ak], [ak])
                self.ld(dst, acc[:].rearrange("p (k j) -> p k j", k=16), [ak], wkeys)
            P.barrier()
def phase_setup(self, st):
    P = self.P
    I = self.I
    self.flags_t = self.sb(st, "flags", [128, 36])
    self.flags = self.flags_t
    self.ld(self.flags_t[:], I["flags"], [], ["flags"])
    self.ident = self.sb(st, "ident", [128, 128])
    self.ld(self.ident[:], I["ident"], [], ["ident"])
    self.epsc = self.sb(st, "epsc", [128, 2])
    self.P.op("dve", lambda e: e.memset(self.epsc[:, 0:1], LN_EPS), writes=["epsc"])
    self.P.op("dve", lambda e: e.memset(self.epsc[:, 1:2], 1.0), writes=["epsc"])
    self.mod = self.sb(st, "mod", [128, 96, 2])
    self.mod1p = self.sb(st, "mod1p", [128, 96, 2])
    self.cv = self.sb(st, "cv", [128, 16, 3])
    self.cvb = self.sb(st, "cvb", [128, 16, 3], BF16)
    self.ld(self.cv[:], I["cvec3"], [], ["cv"])
    self.act(self.cvb[:], self.cv[:], AF.Silu, ["cv"], ["cvb"])
    self.ld(self.hbuf[:, :, C_LAT[0]:C_LAT[1]], I["x_fm"], [], ["hbuf"])
    self.ld(self.hbuf[:, :, 0:CTX], I["ctx_fm"], [], ["hbuf"])
    self.W = {}
    self.Wpart = {}
    for wk in BIGW:
        rows, cols = I[wk][0].shape
        self.Wpart[wk] = self.nc.dram_tensor("wp_%s" % wk, [rows, cols], BF16).ap()
        self.W[wk] = self.nc.dram_tensor("wf_%s" % wk, [rows * 8, cols], BF16).ap()
    P.barrier()


def prep_weight(self, wk, l):
    P = self.P
    I = self.I
    if l >= self.nlayers:
        return
    src = I[wk][l]
    rows, cols = src.shape
    part = self.Wpart[wk]
    full = self.W[wk]
    nd = (cols + 2047) // 2048
    rstep = max(1, 768 // nd)
    for a in range(0, rows, rstep):
        b = min(rows, a + rstep)
        P.bg_dma(lambda e, a=a, b=b, part=part, src=src: e.dma_start(out=part[a:b], in_=src[a:b], max_dma_last_dim=8192),
                 reads=[], writes=["wp_%s" % wk])
    P.bg_collective(lambda e, part=part, full=full: e.collective_compute(
        "AllGather", ALU.bypass, replica_groups=[list(range(8))], ins=[part.opt()], outs=[full.opt()]),
        reads=["wp_%s" % wk], writes=["wf_%s" % wk])


def phase_e0(self):
    hb = self.hbuf
    with ExitStack() as st:
        self.exchange(st, hb[:, :, C_LAT[0]:C_LAT[0] + 8], hb[:, :, C_LAT[1] - 8:C_LAT[1]],
                      hb[:, :, C_HL[0]:C_HL[1]], hb[:, :, C_HR[0]:C_HR[1]], 8, self.cin0, self.cout0,
                      ["hbuf"], ["hbuf"], "e0")


def phase_ada(self, l):
    P = self.P
    I = self.I
    fl = self.flags
    with ExitStack() as st:
        bada = self.sb(st, "bada", [128, 12])
        self.ld(bada[:], I["b_ada_c"][l], [], ["bada"])
        wsl = self.sb(st, "adaw", [128, 16, 1536], BF16)
        wv = I["w_ada_c"][l].rearrange("(k p) n -> p k n", p=128)
        for cb in range(3):
            self.P.dma("pool", lambda e, cb=cb: e.dma_start(out=wsl[:, :, cb * 512:(cb + 1) * 512], in_=wv[:, :, cb * 512:(cb + 1) * 512]),
                       reads=[], writes=["adaw%d" % cb], block=True)
        pp = self.ps(st, "adaps", [128, 12, 3])
        for j in range(12):
            for k in range(16):
                self.mm(pp[:, j, :], wsl[:, k, j * 128:(j + 1) * 128], self.cvb[:, k, :], k == 0, k == 15,
                        ["adaw%d" % (j // 4), "cvb"], ["adaps"])
        mloc = self.sb(st, "mloc", [128, 12, 3])
        self.tt(mloc[:], pp[:], bada[:].unsqueeze(2).to_broadcast([128, 12, 3]), ALU.add, ["adaps", "bada"], ["mloc"])
        self.ld(self.cinM, mloc[:].rearrange("p j c -> p (j c)"), ["mloc"], ["cinM"])
        self.gather(self.cinM, self.coutM, "cinM", "coutM")
        mg = self.sb(st, "mg", [128, 96, 3])
        self.ld(mg[:].rearrange("p (r j) c -> p r (j c)", r=8), self.coutM.rearrange("(r p) c -> p r c", p=128), ["coutM"], ["mg"])
        self.cp(self.mod[:, :, 1], mg[:, :, 2], ["mg"], ["mod"])
        self.ts(self.mod[:, :, 0], mg[:, :, 0], fl[:, 34:35], None, ALU.mult, None, ["mg", "flags", "mod"], ["mod"])
        self.stt(self.mod[:, :, 0], mg[:, :, 1], fl[:, 35:36], self.mod[:, :, 0], ALU.mult, ALU.add, ["mg", "flags", "mod"], ["mod"])
        self.ts(self.mod1p[:], self.mod[:], 1.0, None, ALU.add, None, ["mod"], ["mod1p"])
        self.dump("mod%d" % l, self.mod[:], [128, 96, 2], "mod")
        P.barrier()


def phase_inproj(self, l):
    P = self.P
    I = self.I
    with ExitStack() as st:
        hm = self.sb(st, "hm", [128, 16, NT], BF16)
        hst = [self.sb(st, "hst", [128, NT]) for _ in range(2)]
        for k in range(16):
            hs = hst[k % 2]
            hk = "hst%d" % (k % 2)
            self.ld(hs[:], self.hbuf[:, k, :], ["hbuf"], [hk])
            self.ts(hm[:, k, 0:CTX], hs[:, 0:CTX], self.mod1p[:, 16 + k, 1:2], self.mod[:, k, 1:2], ALU.mult, ALU.add,
                    [hk, "mod", "mod1p"], ["hm"])
            self.ts(hm[:, k, CTX:NT], hs[:, CTX:NT], self.mod1p[:, 16 + k, 0:1], self.mod[:, k, 0:1], ALU.mult, ALU.add,
                    [hk, "mod", "mod1p"], ["hm"])
        wv = self.W["w_in"].rearrange("(k p) n -> p k n", p=128)
        slabs = [self.sb(st, "inw", [128, 16, 512], BF16) for _ in range(2)]
        stg = [self.sb(st, "pstg", [128, NT]) for _ in range(2)]
        pss = [self.ps(st, "pjps", [128, 512]) for _ in range(4)]
        pi = 0
        for cb in range(8):
            c0 = cb * 512
            cw = min(512, INW - c0)
            sl = slabs[cb % 2]
            sk = "inw%d" % (cb % 2)
            self.ld(sl[:, :, 0:cw], wv[:, :, c0:c0 + cw], ["wf_w_in"], [sk], q="sp")
            for mi in range((cw + 127) // 128):
                m = cb * 4 + mi
                mw = min(128, cw - mi * 128)
                sg = stg[m % 2]
                gk = "pstg%d" % (m % 2)
                for (a, b) in TOKB:
                    pt = pss[pi % 4]
                    pk = "pjps%d" % (pi % 4)
                    pi += 1
                    for k in range(16):
                        self.mm(pt[0:mw, 0:b - a], sl[:, k, mi * 128:mi * 128 + mw], hm[:, k, a:b], k == 0, k == 15,
                                [sk, "hm"], [pk])
                    self.cp(sg[0:mw, a:b], pt[0:mw, 0:b - a], [pk], [gk], eng="act")
                self.ld(self.proj[m * 128:m * 128 + mw, :], sg[0:mw, :], [gk], ["proj"])
        P.barrier()


KB.phase_setup = phase_setup
KB.prep_weight = prep_weight
KB.phase_e0 = phase_e0
KB.phase_ada = phase_ada
KB.phase_inproj = phase_inproj
def winchain(self, E, A, Bt, Lext, level, rk, keyA, keyB):
    self.tt(A[:, 1:Lext], E[:, 0:Lext - 1], E[:, 1:Lext], ALU.add, rk, [keyA])
    if level == 1:
        return A, keyA
    self.tt(Bt[:, 2:Lext - 1], A[:, 1:Lext - 2], A[:, 3:Lext], ALU.add, [keyA], [keyB])
    if level == 2:
        return Bt, keyB
    self.tt(A[:, 4:Lext - 3], Bt[:, 2:Lext - 5], Bt[:, 6:Lext - 1], ALU.add, [keyB], [keyA])
    if level == 3:
        return A, keyA
    self.tt(Bt[:, 8:Lext - 7], A[:, 4:Lext - 11], A[:, 12:Lext - 3], ALU.add, [keyA], [keyB])
    return Bt, keyB


def mix_gating(self, l, need_ctx):
    I = self.I
    P = self.P
    with ExitStack() as st:
        lnG = self.sb(st, "lnG", [128, 4, 128]); lnB = self.sb(st, "lnB", [128, 4, 128])
        wsf = self.sb(st, "wsf", [128, 4, 128]); wsb = self.sb(st, "wsb", [128, 4, 128], BF16)
        bsR = self.sb(st, "bsR", [128, 4, 128])
        self.ld(lnG[:], I["sglnG"][l], [], ["lnG"]); self.ld(lnB[:], I["sglnB"][l], [], ["lnB"])
        self.ld(wsf[:], I["sgwT"][l], [], ["wsf"]); self.ld(bsR[:], I["sgb"][l], [], ["bsR"])
        self.cp(wsb[:], wsf[:], ["wsf"], ["wsb"])
        U = self.sb(st, "gU", [128, NR]); V = self.sb(st, "gV", [128, NR]); T = self.sb(st, "gT", [128, NR])
        tps = [self.ps(st, "gtp", [128, 128]) for _ in range(2)]
        sps = [self.ps(st, "gsp", [128, 128]) for _ in range(2)]
        st6 = self.sb(st, "st6", [128, 6]); mv = self.sb(st, "mv", [128, 2]); rs = self.sb(st, "rs", [128, 1])
        vn = self.sb(st, "vn", [128, 128]); vnb = self.sb(st, "vnb", [128, 128], BF16); sv = self.sb(st, "sv", [128, 128])
        c0 = 0 if need_ctx else 2
        a0 = c0 * 128
        n = 0
        for hd in range(4):
            ru = 512 + hd * 128
            rv = 1024 + hd * 128
            for (tile, key, r0) in ((U, "gU", ru), (V, "gV", rv)):
                if need_ctx:
                    self.ld(tile[:, 0:CTX], self.proj[r0:r0 + 128, 0:CTX], ["proj"], [key])
                self.ld(tile[:, CTX:NR], self.proj[r0:r0 + 128, C_LAT[0]:C_LAT[1]], ["proj"], [key])
                self.gelu(tile[:, a0:NR], tile[:, a0:NR], T[:, a0:NR], [key], [key], "gT")
            for c in range(c0, 10):
                cs = c * 128
                tp = tps[n % 2]; tk = "gtp%d" % (n % 2)
                sp = sps[n % 2]; sk = "gsp%d" % (n % 2)
                n += 1
                self.tr(tp[:], V[:, cs:cs + 128], self.ident[:], ["gV", "ident"], [tk])
                self.vop("bn_stats", [tk], ["st6"], out=st6[:], in_=tp[:])
                self.vop("bn_aggr", ["st6"], ["mv"], out=mv[:], in_=st6[:])
                self.act(rs[:], mv[:, 1:2], AF.Sqrt, ["mv"], ["rs"], bias=self.epsc[:, 0:1])
                self.vop("reciprocal", ["rs"], ["rs"], out=rs[:], in_=rs[:])
                self.ts(vn[:], tp[:], mv[:, 0:1], rs[:, 0:1], ALU.subtract, ALU.mult, [tk, "mv", "rs"], ["vn"])
                self.tt(vn[:], vn[:], lnG[:, hd, :], ALU.mult, ["vn", "lnG"], ["vn"])
                self.tt(vnb[:], vn[:], lnB[:, hd, :], ALU.add, ["vn", "lnB"], ["vnb"])
                self.mm(sp[:], vnb[:], wsb[:, hd, :], True, True, ["vnb", "wsb"], [sk])
                self.tt(sv[:], sp[:], bsR[:, hd, :], ALU.add, [sk, "bsR"], ["sv"])
                self.tt(self.mix[:, 4 + hd, cs:cs + 128], sv[:], U[:, cs:cs + 128], ALU.mult, ["sv", "gU"], ["mix"])
        P.barrier()


def mix_pool(self, l, need_ctx):
    I = self.I
    P = self.P
    fl = self.flags
    with ExitStack() as st:
        pwf = self.sb(st, "pwf", [128, 4, 128]); pwb = self.sb(st, "pwb", [128, 4, 128], BF16)
        pb = self.sb(st, "pb", [128, 4]); psc = self.sb(st, "psc", [128, 4])
        self.ld(pwf[:], I["pool_w"][l].rearrange("g c d -> c g d"), [], ["pwf"])
        self.cp(pwb[:], pwf[:], ["pwf"], ["pwb"])
        self.ld(pb[:], I["poolb"][l], [], ["pb"]); self.ld(psc[:], I["pools"][l], [], ["psc"])
        LX = LAT + 16
        E = self.sb(st, "pE", [128, LX]); A = self.sb(st, "pA", [128, LX]); Bt = self.sb(st, "pB", [128, LX])
        rc = self.sb(st, "prc", [128, 4, LAT]); df = self.sb(st, "pdf", [128, LAT], BF16)
        mn = self.sb(st, "pmn", [128, LAT])
        pss = [self.ps(st, "pps", [128, 512]) for _ in range(2)]
        segs = [("lat", LAT, CTX)]
        if need_ctx:
            segs.append(("ctx", CTX, 0))
        n = 0
        for (sname, Lr, mo) in segs:
            Lext = Lr + 16
            self.vop("memset", [], ["pE"], E[:, 0:Lext], 1.0)
            if sname == "lat":
                self.cp(E[:, 0:8], fl[:, 32:33].to_broadcast([128, 8]), ["flags"], ["pE"])
                self.cp(E[:, 8 + Lr:Lext], fl[:, 33:34].to_broadcast([128, 8]), ["flags"], ["pE"])
            else:
                self.vop("memset", [], ["pE"], E[:, 0:8], 0.0)
                self.vop("memset", [], ["pE"], E[:, 8 + Lr:Lext], 0.0)
            for g in range(4):
                W, wk = self.winchain(E, A, Bt, Lext, g + 1, ["pE"], "pA", "pB")
                self.vop("reciprocal", [wk], ["prc"], out=rc[:, g, 0:Lr], in_=W[:, 8:8 + Lr])
            for g in range(4):
                r0 = 1536 + g * 128
                if sname == "lat":
                    self.ld(E[:, 0:8], self.proj[r0:r0 + 128, C_HL[0]:C_HL[1]], ["proj"], ["pE"])
                    self.ld(E[:, 8:8 + Lr], self.proj[r0:r0 + 128, C_LAT[0]:C_LAT[1]], ["proj"], ["pE"])
                    self.ld(E[:, 8 + Lr:Lext], self.proj[r0:r0 + 128, C_HR[0]:C_HR[1]], ["proj"], ["pE"])
                    self.ts(E[:, 0:8], E[:, 0:8], fl[:, 32:33], None, ALU.mult, None, ["pE", "flags"], ["pE"])
                    self.ts(E[:, 8 + Lr:Lext], E[:, 8 + Lr:Lext], fl[:, 33:34], None, ALU.mult, None, ["pE", "flags"], ["pE"])
                else:
                    self.vop("memset", [], ["pE"], E[:, 0:8], 0.0)
                    self.vop("memset", [], ["pE"], E[:, 8 + Lr:Lext], 0.0)
                    self.ld(E[:, 8:8 + Lr], self.proj[r0:r0 + 128, 0:CTX], ["proj"], ["pE"])
                W, wk = self.winchain(E, A, Bt, Lext, g + 1, ["pE"], "pA", "pB")
                self.tt(mn[:, 0:Lr], W[:, 8:8 + Lr], rc[:, g, 0:Lr], ALU.mult, [wk, "prc"], ["pmn"])
                self.tt(df[:, 0:Lr], mn[:, 0:Lr], E[:, 8:8 + Lr], ALU.subtract, ["pmn", "pE"], ["pdf"])
                for a in range(0, Lr, 512):
                    b = min(Lr, a + 512)
                    pt = pss[n % 2]; pk = "pps%d" % (n % 2)
                    n += 1
                    self.mm(pt[:, 0:b - a], pwb[:, g, :], df[:, a:b], True, True, ["pwb", "pdf"], [pk])
                    self.ts(self.mix[:, 8 + g, mo + a:mo + b], pt[:, 0:b - a], pb[:, g:g + 1], psc[:, g:g + 1], ALU.add, ALU.mult,
                            [pk, "pb", "psc"], ["mix"])
        P.barrier()


def phase_mixers(self, l):
    need_ctx = l < NL - 1
    if not any(k in self.dbg_want for k in ("mixA", "mixD", "mixAD")):
        self.mix_gating(l, need_ctx)
        self.mix_pool(l, need_ctx)
    if "mixBC" in self.dbg_want:
        with ExitStack() as st:
            t = self.sb(st, "dmix", [128, 8, NR])
            self.cp(t[:], self.mix[:, 4:12, :], ["mix"], ["dmix"])
            self.dump("mixBC", t[:], [128, 8, NR], "dmix")
            self.P.barrier()
        return
    if "mixA" in self.dbg_want or "mixAD" in self.dbg_want:
        self.mix_s5(l, need_ctx)
    if "mixD" in self.dbg_want or "mixAD" in self.dbg_want:
        self.mix_ssd(l, need_ctx)
    if "mixAD" in self.dbg_want or "mixA" in self.dbg_want or "mixD" in self.dbg_want:
        nm = [k for k in ("mixA", "mixD", "mixAD") if k in self.dbg_want][0]
        with ExitStack() as st:
            t = self.sb(st, "dmix", [128, 8, NR])
            self.vop("memset", [], ["dmix"], t[:], 0.0)
            if nm != "mixD":
                self.cp(t[:, 0:4], self.mix[:, 0:4, :], ["mix"], ["dmix"])
            if nm != "mixA":
                self.cp(t[:, 4:8], self.mix[:, 12:16, :], ["mix"], ["dmix"])
            self.dump(nm, t[:], [128, 8, NR], "dmix")
            self.P.barrier()
        return
    self.mix_s5(l, need_ctx)
    self.mix_ssd(l, need_ctx)


KB.winchain = winchain
KB.mix_gating = mix_gating
KB.mix_pool = mix_pool
KB.phase_mixers = phase_mixers
def sincos(self, ang, sn, cs, ki, red, m, n, rk, tag, tk="s"):
    for (dst, off) in ((sn, 0.0), (cs, math.pi / 2)):
        kk, kr, km = tk + "ki", tk + "red", tk + "m"
        if off == 0.0:
            src = ang
            self.ts(ki, src, 1.0 / TWO_PI, None, ALU.mult, None, rk, [kk])
        else:
            self.ts(red, ang, off, None, ALU.add, None, rk, [kr])
            src = red
            self.ts(ki, src, 1.0 / TWO_PI, None, ALU.mult, None, [kr], [kk])
        self.stt(red, ki, -TWO_PI, src, ALU.mult, ALU.add, [kk, kr] + rk, [kr])
        self.ts(m, red, math.pi, -TWO_PI, ALU.is_gt, ALU.mult, [kr], [km])
        self.tt(red, red, m, ALU.add, [kr, km], [kr])
        self.ts(m, red, -math.pi, TWO_PI, ALU.is_lt, ALU.mult, [kr], [km])
        self.tt(red, red, m, ALU.add, [kr, km], [kr])
        self.act(dst, red, AF.Sin, [kr], [tag + "out"])


def cmul(self, ore, oim, are, aim, bre, bim, t1, t2, rk, wk, k1, k2):
    self.tt(t1, are, bre, ALU.mult, rk, [k1])
    self.tt(t2, aim, bim, ALU.mult, rk, [k2])
    self.tt(t1, t1, t2, ALU.subtract, [k1, k2], [k1])
    self.tt(t2, are, bim, ALU.mult, rk, [k2])
    self.tt(oim, aim, bre, ALU.mult, rk, wk)
    self.tt(oim, oim, t2, ALU.add, wk + [k2], wk)
    self.cp(ore, t1, [k1], wk)


def mix_s5(self, l, need_ctx):
    I = self.I
    P = self.P
    fl = self.flags
    NB = 256
    with ExitStack() as st:
        sb = lambda n, s, dt=F32: self.sb(st, n, s, dt)
        prm = sb("s5prm", [128, 2, 3, 16])
        self.ld(prm[:], I["s5p"][l].rearrange("d p k t -> p d k t"), [], ["prm"])
        lre, lim, lst = prm[:, :, 0, :], prm[:, :, 1, :], prm[:, :, 2, :]
        stp = sb("stp", [128, 2, 16]); zre = sb("zre", [128, 2, 16]); th = sb("th", [128, 2, 16]); rr = sb("rr", [128, 2, 16])
        self.act(stp[:], lst, AF.Exp, ["prm"], ["stp"])
        self.tt(zre[:], lre, stp[:], ALU.mult, ["prm", "stp"], ["zre"])
        self.tt(th[:], lim, stp[:], ALU.mult, ["prm", "stp"], ["th"])
        self.act(rr[:], zre[:], AF.Exp, ["zre"], ["rr"])
        ki = sb("ski", [128, 2, 16], I32); red = sb("sred", [128, 2, 16]); mm_ = sb("smm", [128, 2, 16])
        ang = sb("sang", [128, 2, 16])
        sn1 = sb("sn1", [128, 2, 16]); cs1 = sb("cs1", [128, 2, 16])
        self.sincos(th[:], sn1[:], cs1[:], ki[:], red[:], mm_[:], 32, ["th"], "sc1")
        pw = {}
        for (nm, mult) in (("256", 256.0), ("1024", 1024.0)):
            s_ = sb("sn" + nm, [128, 2, 16]); c_ = sb("cs" + nm, [128, 2, 16]); rp = sb("rp" + nm, [128, 2, 16])
            self.ts(ang[:], th[:], mult, None, ALU.mult, None, ["th"], ["ang"])
            self.sincos(ang[:], s_[:], c_[:], ki[:], red[:], mm_[:], 32, ["ang"], "sc" + nm)
            self.act(rp[:], zre[:], AF.Exp, ["zre"], ["rp" + nm], scale=mult)
            pre = sb("pre" + nm, [128, 2, 16]); pim = sb("pim" + nm, [128, 2, 16])
            self.tt(pre[:], rp[:], c_[:], ALU.mult, ["rp" + nm, "sc" + nm + "out"], ["p" + nm])
            self.tt(pim[:], rp[:], s_[:], ALU.mult, ["rp" + nm, "sc" + nm + "out"], ["p" + nm])
            pw[nm] = (pre, pim)
        t1 = sb("t1", [128, 2, 16]); t2 = sb("t2", [128, 2, 16]); t3 = sb("t3", [128, 2, 16])
        nre = sb("nre", [128, 2, 16]); nim = sb("nim", [128, 2, 16]); den = sb("den", [128, 2, 16])
        bsr = sb("bsr", [128, 2, 16]); bsi = sb("bsi", [128, 2, 16])
        self.tt(nre[:], rr[:], cs1[:], ALU.mult, ["rr", "sc1out"], ["nre"])
        self.ts(nre[:], nre[:], -1.0, None, ALU.add, None, ["nre"], ["nre"])
        self.tt(nim[:], rr[:], sn1[:], ALU.mult, ["rr", "sc1out"], ["nim"])
        self.tt(den[:], lre, lre, ALU.mult, ["prm"], ["den"])
        self.tt(t1[:], lim, lim, ALU.mult, ["prm"], ["t1"])
        self.tt(den[:], den[:], t1[:], ALU.add, ["den", "t1"], ["den"])
        self.vop("reciprocal", ["den"], ["den"], out=den[:], in_=den[:])
        self.tt(t1[:], nre[:], lre, ALU.mult, ["nre", "prm"], ["t1"])
        self.tt(t2[:], nim[:], lim, ALU.mult, ["nim", "prm"], ["t2"])
        self.tt(t1[:], t1[:], t2[:], ALU.add, ["t1", "t2"], ["t1"])
        self.tt(bsr[:], t1[:], den[:], ALU.mult, ["t1", "den"], ["bsr"])
        self.tt(t1[:], nim[:], lre, ALU.mult, ["nim", "prm"], ["t1"])
        self.tt(t2[:], nre[:], lim, ALU.mult, ["nre", "prm"], ["t2"])
        self.tt(t1[:], t1[:], t2[:], ALU.subtract, ["t1", "t2"], ["t1"])
        self.tt(bsi[:], t1[:], den[:], ALU.mult, ["t1", "den"], ["bsi"])
        CeR = sb("CeR", [128, 2, 16, 64], BF16); nCeI = sb("nCeI", [128, 2, 16, 64], BF16)
        BTb = sb("BTb", [128, 2, 2, 4, 2, 128], BF16)
        with ExitStack() as st2:
            ctf = self.sb(st2, "ctf", [128, 2, 2, 16, 64])
            self.ld(ctf[:], I["s5CT"][l].rearrange("d r p t c -> p d r t c"), [], ["ctf"])
            x1 = self.sb(st2, "cx1", [128, 16, 64]); x2 = self.sb(st2, "cx2", [128, 16, 64])
            for d in range(2):
                br = bsr[:, d, :].unsqueeze(2).to_broadcast([128, 16, 64])
                bi = bsi[:, d, :].unsqueeze(2).to_broadcast([128, 16, 64])
                self.tt(x1[:], ctf[:, d, 0], br, ALU.mult, ["ctf", "bsr"], ["cx1"])
                self.tt(x2[:], ctf[:, d, 1], bi, ALU.mult, ["ctf", "bsi"], ["cx2"])
                self.tt(CeR[:, d], x1[:], x2[:], ALU.subtract, ["cx1", "cx2"], ["CeR"])
                self.tt(x1[:], ctf[:, d, 0], bi, ALU.mult, ["ctf", "bsi"], ["cx1"])
                self.tt(x2[:], ctf[:, d, 1], br, ALU.mult, ["ctf", "bsr"], ["cx2"])
                self.tt(x1[:], x1[:], x2[:], ALU.add, ["cx1", "cx2"], ["cx1"])
                self.ts(nCeI[:, d], x1[:], -1.0, None, ALU.mult, None, ["cx1"], ["nCeI"])
            btf = self.sb(st2, "btf", [128, 2, 2, 4, 2, 128])
            self.ld(btf[:], I["s5BT"][l].rearrange("d r p q e n -> p d r q e n"), [], ["btf"])
            self.cp(BTb[:], btf[:], ["btf"], ["BTb"])
            P.barrier()
        tab = sb("tab", [128, 2, 16, 3, NB])
        with ExitStack() as st2:
            io = self.sb(st2, "io", [128, NB])
            self.ld(io[:], I["iota1"], [], ["io"])
            ta = self.sb(st2, "ta", [128, NB]); tki = self.sb(st2, "tki", [128, NB], I32)
            tred = self.sb(st2, "tred", [128, NB]); tm = self.sb(st2, "tm", [128, NB])
            for d in range(2):
                for t in range(16):
                    self.ts(ta[:], io[:], th[:, d, t:t + 1], None, ALU.mult, None, ["io", "th"], ["ta"])
                    self.sincos(ta[:], tab[:, d, t, 1, :], tab[:, d, t, 0, :], tki[:], tred[:], tm[:], NB, ["ta"], "tb", tk="t")
                    self.ts(tab[:, d, t, 2, :], tab[:, d, t, 1, :], -1.0, None, ALU.mult, None, ["tbout"], ["tbout"])
            P.barrier()
        ub = sb("ub", [128, 4, NR], BF16)
        sd = sb("s5d", [128, 4]); gb = sb("glub", [128, 4])
        self.ld(sd[:], I["s5d"][l], [], ["sd"]); self.ld(gb[:], I["s5glub"][l], [], ["gb"])
        gwb = sb("gwb", [128, 4, 512], BF16)
        with ExitStack() as st2:
            uf = self.sb(st2, "uf", [128, NR])
            for q in range(4):
                self.ld(uf[:, 0:CTX], self.proj[q * 128:(q + 1) * 128, 0:CTX], ["proj"], ["uf"])
                self.ld(uf[:, CTX:NR], self.proj[q * 128:(q + 1) * 128, C_LAT[0]:C_LAT[1]], ["proj"], ["uf"])
                self.cp(ub[:, q, :], uf[:], ["uf"], ["ub"])
            gwf = self.sb(st2, "gwf", [128, 4, 512])
            self.ld(gwf[:], I["s5_glu_w"][l].rearrange("(q p) n -> p q n", p=128), [], ["gwf"])
            self.cp(gwb[:], gwf[:], ["gwf"], ["gwb"])
            P.barrier()
        EZ = sb("EZ", [128, 2, 2, 16, 5])
        ET = sb("ET", [128, 2, 2, 16, 4])
        HC = sb("HC", [128, 2, 2, 16])
        ZERO = sb("zero", [128, 1])
        self.vop("memset", [], ["zero"], ZERO[:], 0.0)
        self.vop("memset", [], ["EZ"], EZ[:], 0.0)
        vps = [self.ps(st, "vps", [128, 2, NB]) for _ in range(2)]
        yps = [self.ps(st, "yps", [128, NB]) for _ in range(2)]
        gps = [self.ps(st, "gps", [128, NB]) for _ in range(2)]
        W2 = sb("W2", [128, 2, NB]); M1 = sb("M1", [128, 2, NB]); M2 = sb("M2", [128, 2, NB]); G2 = sb("G2", [128, 2, NB])
        H2 = sb("H2", [128, 2, NB], BF16)
        e1 = sb("e1", [128, 2, 1])
        cnt = [0]

        def block(d, t, col0, init_re, init_im, exit_ap, full, ypt, ypk, first, ikeys, last=False):
            q, r = divmod(t, 4)
            vp = vps[cnt[0] % 2]; vk = "vps%d" % (cnt[0] % 2)
            cnt[0] += 1
            rv = (lambda ap: ap[:, :, ::-1]) if d == 1 else (lambda ap: ap)
            rv2 = (lambda ap: ap[:, ::-1]) if d == 1 else (lambda ap: ap)
            rb, e_ = divmod(r, 2)
            usl = ub[64 * rb:64 * rb + 64, q, col0:col0 + NB]
            self.mm(vp[:, 0, :], BTb[64 * rb:64 * rb + 64, d, 0, q, e_, :], usl, True, True, ["BTb", "ub"], [vk])
            self.mm(vp[:, 1, :], BTb[64 * rb:64 * rb + 64, d, 1, q, e_, :], usl, True, True, ["BTb", "ub"], [vk])
            cc = rv(tab[:, d, t, 0:1, :]).to_broadcast([128, 2, NB])
            ss = rv(tab[:, d, t, 1:3, :])
            ns = rv(tab[:, d, t, 2:0:-1, :])
            self.tt(M1[:], vp[:], cc, ALU.mult, [vk, "tab"], ["M1"])
            self.tt(M2[:], vp[:, ::-1, :], ss, ALU.mult, [vk, "tab"], ["M2"])
            self.tt(W2[:], M1[:], M2[:], ALU.add, ["M1", "M2"], ["W2"])
            rbc = rr[:, d, t:t + 1].to_broadcast([128, NB])
            self.P.op("dve", lambda e: e.tensor_tensor_scan(out=rv2(G2[:, 0, :]), data0=rbc, data1=rv2(W2[:, 0, :]), initial=init_re,
                                                            op0=ALU.mult, op1=ALU.add), reads=["rr", "W2"] + ikeys, writes=["G2"])
            self.P.op("dve", lambda e: e.tensor_tensor_scan(out=rv2(G2[:, 1, :]), data0=rbc, data1=rv2(W2[:, 1, :]), initial=init_im,
                                                            op0=ALU.mult, op1=ALU.add), reads=["rr", "W2"] + ikeys, writes=["G2"])
            if exit_ap is not None:
                lc = 0 if d == 1 else NB - 1
                cL = tab[:, d, t, 0, NB - 1:NB]
                nsL = tab[:, d, t, 2:0:-1, NB - 1:NB]
                self.tt(e1[:], G2[:, ::-1, lc:lc + 1], nsL, ALU.mult, ["G2", "tab"], ["e1"])
                self.stt(exit_ap, G2[:, :, lc:lc + 1], cL, e1[:], ALU.mult, ALU.add, ["G2", "tab", "e1"], ["EZ"])
            if full:
                self.tt(M1[:], G2[:], cc, ALU.mult, ["G2", "tab"], ["M1"])
                self.tt(M2[:], G2[:, ::-1, :], ns, ALU.mult, ["G2", "tab"], ["M2"])
                self.tt(H2[:], M1[:], M2[:], ALU.add, ["M1", "M2"], ["H2"])
                self.mm(ypt[64 * rb:64 * rb + 64, :], CeR[:, d, t, :], H2[:, 0, :], first, False, ["CeR", "H2"], [ypk])
                self.mm(ypt[64 * rb:64 * rb + 64, :], nCeI[:, d, t, :], H2[:, 1, :], False, last, ["nCeI", "H2"], [ypk])

        zb = sb("zb", [128, 4, NB], BF16); yv = sb("yv", [128, NB]); gt = sb("gt", [128, NB]); uf2 = sb("uf2", [128, NB])
        sg = sb("sg", [128, NB])
        ycnt = [0]

        def full_block(col0, pcol0, inits, exits=None):
            for q in range(4):
                ypt = yps[ycnt[0] % 2]; ypk = "yps%d" % (ycnt[0] % 2)
                ycnt[0] += 1
                for r in range(4):
                    t = 4 * q + r
                    for d in range(2):
                        ire, iim, ik = inits(d, t)
                        block(d, t, col0, ire, iim, (exits(d, t) if exits else None), True, ypt, ypk, (d == 0 and r % 2 == 0), ik,
                              last=(d == 1 and r % 2 == 1))
                self.ld(uf2[:], self.proj[q * 128:(q + 1) * 128, pcol0:pcol0 + NB], ["proj"], ["uf2"])
                self.stt(yv[:], uf2[:], sd[:, q:q + 1], ypt[:], ALU.mult, ALU.add, ["uf2", "sd", ypk], ["yv"])
                self.gelu(yv[:], yv[:], gt[:], ["yv"], ["yv"], "gt")
                self.cp(zb[:, q, :], yv[:], ["yv"], ["zb"])
                if q == 3:
                    zf_keep = None
            for m in range(4):
                gp = gps[m % 2]; gk = "gps%d" % (m % 2)
                for q in range(4):
                    self.mm(gp[:], gwb[:, q, m * 128:(m + 1) * 128], zb[:, q, :], q == 0, q == 3, ["gwb", "zb"], [gk])
                self.act(sg[:], gp[:], AF.Sigmoid, [gk, "gb"], ["sg"], bias=gb[:, m:m + 1])
                self.tt(self.mix[:, m, col0:col0 + NB], sg[:], zb[:, m, :], ALU.mult, ["sg", "zb"], ["mix"])

        if need_ctx:
            full_block(0, 0, lambda d, t: (0.0, 0.0, []), exits=lambda d, t: HC[:, d, :, t:t + 1])
        else:
            for d in range(2):
                for t in range(16):
                    block(d, t, 0, 0.0, 0.0, HC[:, d, :, t:t + 1], False, None, None, False, [])
        for d in range(2):
            for t in range(16):
                for k in range(4):
                    b = k if d == 0 else 3 - k
                    col0 = CTX + b * NB
                    block(d, t, col0, EZ[:, d, 0, t, k:k + 1], EZ[:, d, 1, t, k:k + 1],
                          EZ[:, d, :, t, k + 1:k + 2], False, None, None, False, ["EZ"])
        stg = sb("s5stg", [128, 64])
        self.cp(stg[:].rearrange("p (a b t) -> p a b t", a=2, b=2), EZ[:, :, :, :, 4], ["EZ"], ["s5stg"])
        self.ld(self.cinA, stg[:], ["s5stg"], ["cinA"])
        self.gather(self.cinA, self.coutA, "cinA", "coutA")
        s5_finish(self, EZ, ET, HC, pw, full_block, sb, need_ctx)
        P.barrier()


def gather(self, cin, cout, rk, wk):
    self.P.collective(lambda e: e.collective_compute("AllGather", ALU.bypass, replica_groups=[list(range(8))],
                                                     ins=[cin.opt()], outs=[cout.opt()]),
                      reads=[rk], writes=[wk])


def s5_finish(self, EZ, ET, HC, pw, full_block, sb, need_ctx):
    fl = self.flags
    NB = 256
    G = sb("s5G", [128, 8, 64])
    self.ld(G[:], self.coutA.rearrange("(r p) c -> p r c", p=128), ["coutA"], ["s5G"])
    Gv = G[:].rearrange("p r (a b t) -> p r a b t", a=2, b=2)
    H = sb("s5H", [128, 2, 2, 16]); N1 = sb("s5N1", [128, 2, 16])
    ca1 = sb("ca1", [128, 16]); ca2 = sb("ca2", [128, 16])
    p1024 = pw["1024"]
    p256 = pw["256"]
    self.cp(H[:], HC[:], ["EZ"], ["s5H"])
    for d in range(2):
        order = range(8) if d == 0 else range(7, -1, -1)
        mo = 16 if d == 0 else 24
        hr, hi = H[:, d, 0, :], H[:, d, 1, :]
        pr, pi = p1024[0][:, d, :], p1024[1][:, d, :]
        n1, n2 = N1[:, 0, :], N1[:, 1, :]
        for j in order:
            self.cmul(n1, n2, pr, pi, hr, hi, ca1[:], ca2[:], ["s5H", "p1024"], ["s5N1"], "ca1", "ca2")
            self.tt(n1, n1, Gv[:, j, d, 0, :], ALU.add, ["s5N1", "s5G"], ["s5N1"])
            self.tt(n2, n2, Gv[:, j, d, 1, :], ALU.add, ["s5N1", "s5G"], ["s5N1"])
            self.tt(n1, n1, hr, ALU.subtract, ["s5N1", "s5H"], ["s5N1"])
            self.tt(n2, n2, hi, ALU.subtract, ["s5N1", "s5H"], ["s5N1"])
            mj = fl[:, mo + j:mo + j + 1]
            self.stt(hr, n1, mj, hr, ALU.mult, ALU.add, ["s5N1", "flags", "s5H"], ["s5H"])
            self.stt(hi, n2, mj, hi, ALU.mult, ALU.add, ["s5N1", "flags", "s5H"], ["s5H"])
    X = sb("s5X", [128, 2, 2, 16]); Y = sb("s5Y", [128, 2, 2, 16])
    self.cp(X[:], H[:], ["s5H"], ["s5X"])
    for k in range(4):
        self.tt(ET[:, :, :, :, k], EZ[:, :, :, :, k], X[:], ALU.add, ["EZ", "s5X"], ["ET"])
        if k < 3:
            for d in range(2):
                self.cmul(Y[:, d, 0, :], Y[:, d, 1, :], p256[0][:, d, :], p256[1][:, d, :], X[:, d, 0, :], X[:, d, 1, :],
                          ca1[:], ca2[:], ["s5X", "p256"], ["s5Y"], "ca1", "ca2")
            self.cp(X[:], Y[:], ["s5Y"], ["s5X"])
    for b in range(4):
        def inits(d, t, b=b):
            k = b if d == 0 else 3 - b
            return ET[:, d, 0, t, k:k + 1], ET[:, d, 1, t, k:k + 1], ["ET"]
        full_block(CTX + b * NB, C_LAT[0] + b * NB, inits)


KB.sincos = sincos
KB.cmul = cmul
KB.mix_s5 = mix_s5
KB.gather = gather
def mix_ssd(self, l, need_ctx):
    I = self.I
    P = self.P
    fl = self.flags
    NCH = 10
    with ExitStack() as st:
        sb = lambda n, s, dt=F32: self.sb(st, n, s, dt)
        XS = sb("XS", [128, 4, NR]); BM = sb("BM", [128, 2, NR], BF16); CMb = sb("CMb", [128, 2, NR], BF16)
        cw = sb("m2cw", [128, 8, 4]); cb = sb("m2cb", [128, 8])
        self.ld(cw[:], I["m2cw"][l], [], ["cw"]); self.ld(cb[:], I["m2cb"][l], [], ["cb"])
        identb = sb("identb", [128, 128], BF16)
        self.cp(identb[:], self.ident[:], ["ident"], ["identb"])
        with ExitStack() as st2:
            E = self.sb(st2, "cE", [128, LAT + 3]); acc = self.sb(st2, "cacc", [128, LAT])
            for j in range(8):
                r0 = 2560 + j * 128
                for (sname, Lr, mo) in (("lat", LAT, CTX), ("ctx", CTX, 0)):
                    if sname == "lat":
                        self.ld(E[:, 0:2], self.proj[r0:r0 + 128, C_HL[1] - 2:C_HL[1]], ["proj"], ["cE"])
                        self.ld(E[:, 2:2 + Lr], self.proj[r0:r0 + 128, C_LAT[0]:C_LAT[1]], ["proj"], ["cE"])
                        self.ld(E[:, 2 + Lr:3 + Lr], self.proj[r0:r0 + 128, C_HR[0]:C_HR[0] + 1], ["proj"], ["cE"], slow=True)
                        self.ts(E[:, 0:2], E[:, 0:2], fl[:, 32:33], None, ALU.mult, None, ["cE", "flags"], ["cE"])
                        self.ts(E[:, 2 + Lr:3 + Lr], E[:, 2 + Lr:3 + Lr], fl[:, 33:34], None, ALU.mult, None, ["cE", "flags"], ["cE"])
                    else:
                        self.vop("memset", [], ["cE"], E[:, 0:2], 0.0)
                        self.vop("memset", [], ["cE"], E[:, 2 + Lr:3 + Lr], 0.0)
                        self.ld(E[:, 2:2 + Lr], self.proj[r0:r0 + 128, 0:CTX], ["proj"], ["cE"])
                    self.ts(acc[:, 0:Lr], E[:, 0:Lr], cw[:, j, 0:1], None, ALU.mult, None, ["cE", "cw"], ["cacc"])
                    for k in range(1, 4):
                        self.stt(acc[:, 0:Lr], E[:, k:k + Lr], cw[:, j, k:k + 1], acc[:, 0:Lr], ALU.mult, ALU.add, ["cE", "cw", "cacc"], ["cacc"])
                    if j < 4:
                        dst, dk = XS[:, j, mo:mo + Lr], "XS"
                    elif j < 6:
                        dst, dk = BM[:, j - 4, mo:mo + Lr], "BM"
                    else:
                        dst, dk = CMb[:, j - 6, mo:mo + Lr], "CMb"
                    self.act(dst, acc[:, 0:Lr], AF.Silu, ["cacc", "cb"], [dk], bias=cb[:, j:j + 1])
            P.barrier()
        DT = sb("DT", [40, NR]); AC = sb("AC", [40, NR])
        TM = sb("TM", [128, NCH, 2, 40])
        ATB = sb("ATB", [128, NCH, 16]); EAT = sb("EAT", [128, NCH, 16])
        selh = sb("selh", [40, 16, 128])
        self.ld(selh[:], I["selh"], [], ["selh"])
        ones40 = sb("ones40", [40, 128])
        self.vop("memset", [], ["ones40"], ones40[:], 1.0)
        with ExitStack() as st2:
            dtb = self.sb(st2, "dtb", [40, 1]); alg = self.sb(st2, "alg", [40, 1]); av = self.sb(st2, "av", [40, 1])
            self.ld(dtb[:], I["m2dtb"][l], [], ["dtb"]); self.ld(alg[:], I["m2alog"][l], [], ["alg"])
            cmk = self.sb(st2, "cmk", [40, 2, NR])
            self.ld(cmk[:], I["cmask"], [], ["cmk"])
            self.vop("memset", [], ["DT"], DT[:], 0.0)
            for (p0, r0) in ((0, 3584), (32, 3592)):
                self.ld(DT[p0:p0 + 8, 0:CTX], self.proj[r0:r0 + 8, 0:CTX], ["proj", "DT"], ["DT"])
                self.ld(DT[p0:p0 + 8, CTX:NR], self.proj[r0:r0 + 8, C_LAT[0]:C_LAT[1]], ["proj", "DT"], ["DT"])
            self.act(DT[:], DT[:], AF.Exp, ["DT", "dtb"], ["DT"], bias=dtb[:, 0:1])
            self.act(DT[:], DT[:], AF.Ln, ["DT"], ["DT"], bias=self.epsc[0:40, 1:2])
            self.act(av[:], alg[:], AF.Exp, ["alg"], ["av"])
            self.ts(av[:], av[:], -1.0, None, ALU.mult, None, ["av"], ["av"])
            DTA = self.sb(st2, "DTA", [40, NR]); AC1 = self.sb(st2, "AC1", [40, NR])
            self.ts(DTA[:], DT[:], av[:, 0:1], None, ALU.mult, None, ["DT", "av"], ["DTA"])
            self.P.op("dve", lambda e: e.tensor_tensor_scan(out=AC[:], data0=cmk[:, 0, :], data1=DTA[:], initial=0.0,
                                                            op0=ALU.mult, op1=ALU.add), reads=["cmk", "DTA"], writes=["AC"])
            self.P.op("dve", lambda e: e.tensor_tensor_scan(out=AC1[:, ::-1], data0=cmk[:, 1, ::-1], data1=DTA[:, ::-1], initial=0.0,
                                                            op0=ALU.mult, op1=ALU.add), reads=["cmk", "DTA"], writes=["AC1"])
            self.cp(AC[32:40, :], AC1[32:40, :], ["AC1", "AC"], ["AC"])
            ATC = self.sb(st2, "ATC", [40, NCH]); RH = self.sb(st2, "RH", [40, NCH, 16])
            self.vop("memset", [], ["ATC"], ATC[:], 0.0)
            self.cp(ATC[0:8, :], AC[0:8, 127:NR:128], ["AC", "ATC"], ["ATC"])
            self.cp(ATC[32:40, :], AC[32:40, 0:NR:128], ["AC", "ATC"], ["ATC"])
            self.tt(RH[:], selh[:, :, 0:1].rearrange("p m o -> p o m").to_broadcast([40, NCH, 16]),
                    ATC[:].unsqueeze(2).to_broadcast([40, NCH, 16]), ALU.mult, ["selh", "ATC"], ["RH"])
            pa = self.ps(st2, "pa", [128, NCH * 16])
            self.mm(pa[:], ones40[:], RH[:].rearrange("p c m -> p (c m)"), True, True, ["ones40", "RH"], ["pa"])
            self.cp(ATB[:].rearrange("p c m -> p (c m)"), pa[:], ["pa"], ["ATB"])
            self.act(EAT[:], ATB[:], AF.Exp, ["ATB"], ["EAT"])
            pt = self.ps(st2, "ptm", [128, 2, 40])
            for c in range(NCH):
                cs = c * 128
                self.tr(pt[:, 0, :], DT[0:40, cs:cs + 128], self.ident[0:40, 0:40], ["DT", "ident"], ["ptm"])
                self.tr(pt[:, 1, :], AC[0:40, cs:cs + 128], self.ident[0:40, 0:40], ["AC", "ident"], ["ptm"])
                self.cp(TM[:, c], pt[:], ["ptm"], ["TM"])
            P.barrier()
        XD = sb("XD", [128, NCH, 2, 512], BF16)
        CS = sb("CS", [128, NCH, 2, 512], BF16)
        c0 = 0
        with ExitStack() as st2:
            pxs = [self.ps(st2, "pxs", [128, 512]) for _ in range(2)]
            pbc = self.ps(st2, "pbc", [128, 2, 128], BF16)
            pcs = [self.ps(st2, "pcs", [128, 512]) for _ in range(2)]
            bct = self.sb(st2, "bct", [128, 2, 128], BF16)
            de = self.sb(st2, "de", [128, 16]); cf = self.sb(st2, "cf", [128, 16])
            xdd = self.sb(st2, "xdd", [128, 2, 512], BF16)
            for c in range(NCH):
                cs = c * 128
                px = pxs[c % 2]; pxk = "pxs%d" % (c % 2)
                for j in range(4):
                    self.tr(px[:, j * 128:(j + 1) * 128], XS[:, j, cs:cs + 128], self.ident[:], ["XS", "ident"], [pxk])
                for g in range(2):
                    self.tr(pbc[:, g, :], BM[:, g, cs:cs + 128], identb[:], ["BM", "identb"], ["pbc"])
                self.cp(bct[:], pbc[:], ["pbc"], ["bct"])
                dt16 = TM[:, c, 0, :].rearrange("p (a b) -> p a b", b=8)[:, 0:5:4, :]
                ac16 = TM[:, c, 1, :].rearrange("p (a b) -> p a b", b=8)[:, 0:5:4, :]
                self.tt(de[:].rearrange("p (a b) -> p a b", b=8), ATB[:, c, :].rearrange("p (a b) -> p a b", b=8), ac16,
                        ALU.subtract, ["ATB", "TM"], ["de"])
                self.act(de[:], de[:], AF.Exp, ["de"], ["de"])
                self.tt(cf[:].rearrange("p (a b) -> p a b", b=8), de[:].rearrange("p (a b) -> p a b", b=8), dt16, ALU.mult,
                        ["de", "TM"], ["cf"])
                pxv = px[:].rearrange("p (h e) -> p h e", e=64)
                for d in range(2):
                    self.tt(XD[:, c, d, :].rearrange("p (h e) -> p h e", e=64), pxv,
                            dt16[:, d, :].unsqueeze(2).to_broadcast([128, 8, 64]), ALU.mult, [pxk, "TM"], ["XD"])
                    self.tt(xdd[:, d, :].rearrange("p (h e) -> p h e", e=64), pxv,
                            cf[:, d * 8:(d + 1) * 8].unsqueeze(2).to_broadcast([128, 8, 64]), ALU.mult, [pxk, "cf"], ["xdd"])
                for d in range(2):
                    pc = pcs[d]; pck = "pcs%d" % d
                    for g in range(2):
                        self.mm(pc[:, g * 256:(g + 1) * 256], bct[:, g, :], xdd[:, d, g * 256:(g + 1) * 256], True, True,
                                ["bct", "xdd"], [pck])
                    self.cp(CS[:, c, d, :], pc[:], [pck], ["CS"], eng="act")
            P.barrier()
        S = sb("Sst", [128, 2, 512]); HCs = sb("HCs", [128, 2, 512])
        stg = sb("dstg", [128, 1040])

        def chain(dst, d, order, init_key):
            for c in order:
                v = dst[:, d, :].rearrange("p (h e) -> p h e", e=64)
                self.tt(v, v, EAT[:, c, d * 8:(d + 1) * 8].unsqueeze(2).to_broadcast([128, 8, 64]), ALU.mult,
                        [init_key, "EAT"], [init_key])
                self.tt(dst[:, d, :], dst[:, d, :], CS[:, c, d, :], ALU.add, [init_key, "CS"], [init_key])

        self.vop("memset", [], ["Sst"], S[:], 0.0)
        self.vop("memset", [], ["HCs"], HCs[:], 0.0)
        chain(S, 0, range(2, 10), "Sst")
        chain(S, 1, range(9, 1, -1), "Sst")
        chain(HCs, 0, (0, 1), "HCs")
        chain(HCs, 1, (1, 0), "HCs")
        self.cp(stg[:, 0:1024], S[:].rearrange("p d n -> p (d n)"), ["Sst"], ["dstg"])
        self.vop("tensor_reduce", ["ATB"], ["dstg"], out=stg[:, 1024:1040], in_=ATB[:, 2:10, :].rearrange("p c m -> p m c"),
                 axis=mybir.AxisListType.X, op=ALU.add)
        self.ld(self.cinD, stg[:], ["dstg"], ["cinD"])
        self.gather(self.cinD, self.coutD, "cinD", "coutD")
        H = sb("Hin", [128, 2, 512])
        with ExitStack() as st2:
            cod = self.coutD.rearrange("(r p) c -> p r c", p=128)
            GT = self.sb(st2, "GT", [128, 8, 16])
            self.ld(GT[:], cod[:, :, 1024:1040], ["coutD"], ["GT"])
            EXG = self.sb(st2, "EXG", [128, 8, 16])
            self.act(EXG[:], GT[:], AF.Exp, ["GT"], ["EXG"])
            N = self.sb(st2, "Nn", [128, 512])
            Gj = [self.sb(st2, "Gj", [128, 512]) for _ in range(2)]
            self.cp(H[:], HCs[:], ["HCs"], ["Hin"])
            nn = 0
            for d in range(2):
                order = range(8) if d == 0 else range(7, -1, -1)
                mo = 16 if d == 0 else 24
                for j in order:
                    gj = Gj[nn % 2]; gk = "Gj%d" % (nn % 2)
                    nn += 1
                    self.ld(gj[:], cod[:, j, d * 512:(d + 1) * 512], ["coutD"], [gk])
                    self.tt(N[:].rearrange("p (h e) -> p h e", e=64), H[:, d, :].rearrange("p (h e) -> p h e", e=64),
                            EXG[:, j, d * 8:(d + 1) * 8].unsqueeze(2).to_broadcast([128, 8, 64]), ALU.mult, ["Hin", "EXG"], ["Nn"])
                    self.tt(N[:], N[:], gj[:], ALU.add, ["Nn", gk], ["Nn"])
                    self.tt(N[:], N[:], H[:, d, :], ALU.subtract, ["Nn", "Hin"], ["Nn"])
                    self.stt(H[:, d, :], N[:], fl[:, mo + j:mo + j + 1], H[:, d, :], ALU.mult, ALU.add, ["Nn", "flags", "Hin"], ["Hin"])
            P.barrier()
        SP = CS
        Tt = sb("Ttmp", [128, 512])

        def chain2(init_ap, d, order, ik):
            self.cp(S[:, d, :], init_ap, [ik, "Sst"], ["Sst"])
            for c in order:
                self.tt(Tt[:].rearrange("p (h e) -> p h e", e=64), S[:, d, :].rearrange("p (h e) -> p h e", e=64),
                        EAT[:, c, d * 8:(d + 1) * 8].unsqueeze(2).to_broadcast([128, 8, 64]), ALU.mult, ["Sst", "EAT"], ["Tt"])
                self.tt(Tt[:], Tt[:], CS[:, c, d, :], ALU.add, ["Tt", "CS"], ["Tt"])
                self.cp(CS[:, c, d, :], S[:, d, :], ["Sst", "CS"], ["CS"])
                self.cp(S[:, d, :], Tt[:], ["Tt"], ["Sst"])

        chain2(H[:, 0, :], 0, range(2, 10), "Hin")
        chain2(H[:, 1, :], 1, range(9, 1, -1), "Hin")
        if need_ctx:
            zz = sb("zz", [128, 512])
            self.vop("memset", [], ["zz"], zz[:], 0.0)
            chain2(zz[:], 0, (0, 1), "zz")
            chain2(zz[:], 1, (1, 0), "zz")
        P.barrier()
        mk = sb("mk", [128, 2, 128])
        self.ld(mk[:, 0, :], I["maskF"], [], ["mk"]); self.ld(mk[:, 1, :], I["maskB"], [], ["mk"])
        md = sb("m2d", [128, 4]); nw = sb("m2nw", [128, 4])
        self.ld(md[:], I["m2d"][l], [], ["md"]); self.ld(nw[:], I["m2nw"][l], [], ["nw"])
        onesn = sb("onesn", [128, 128])
        self.vop("memset", [], ["onesn"], onesn[:], 1.0 / 256.0)
        pg = self.ps(st, "pg", [128, 2, 128])
        prs = [self.ps(st, "pr", [128, 128]) for _ in range(2)]
        pys = [self.ps(st, "py", [128, 128]) for _ in range(2)]
        pms = self.ps(st, "pms", [128, 128])
        Gm = sb("Gm", [128, 2, 2, 128])
        E1 = sb("E1", [128, 128]); STb = sb("STb", [128, 128], BF16); ER = sb("ER", [128, 128]); ccs = sb("ccs", [128, 128], BF16)
        Y = sb("Yt", [128, 4, 128]); Z = sb("Zt", [128, 4, 128]); gsq = sb("gsq", [128, 4, 128]); rsd = sb("rsd", [128, 128])
        n = 0
        for c in range(0 if need_ctx else 2, NCH):
            cs = c * 128
            pcol = cs if c < 2 else C_LAT[0] + (c - 2) * 128
            for g in range(2):
                self.mm(pg[:, g, :], BM[:, g, cs:cs + 128], CMb[:, g, cs:cs + 128], True, True, ["BM", "CMb"], ["pg"])
            for d in range(2):
                for g in range(2):
                    self.tt(Gm[:, d, g, :], pg[:, g, :], mk[:, d, :], ALU.mult, ["pg", "mk"], ["Gm"])
            ac16 = TM[:, c, 1, :].rearrange("p (a b) -> p a b", b=8)[:, 0:5:4, :]
            for hp in range(4):
                py = pys[hp % 2]; pyk = "py%d" % (hp % 2)
                for hh in range(2):
                    h = hp * 2 + hh
                    g = h // 4
                    for d in range(2):
                        pr = prs[n % 2]; prk = "pr%d" % (n % 2)
                        n += 1
                        self.mm(pr[:], selh[:, d * 8 + h, :], AC[0:40, cs:cs + 128], True, True, ["selh", "AC"], [prk])
                        self.ts(E1[:], pr[:], ac16[:, d, h:h + 1], 0.0, ALU.subtract, ALU.min, [prk, "TM"], ["E1"])
                        self.act(E1[:], E1[:], AF.Exp, ["E1"], ["E1"])
                        self.tt(STb[:], E1[:], Gm[:, d, g, :], ALU.mult, ["E1", "Gm"], ["STb"])
                        self.act(ER[:], pr[:], AF.Exp, [prk], ["ER"])
                        self.tt(ccs[:], CMb[:, g, cs:cs + 128], ER[:], ALU.mult, ["CMb", "ER"], ["ccs"])
                        self.mm(py[hh * 64:(hh + 1) * 64, :], XD[:, c, d, h * 64:(h + 1) * 64], STb[:], d == 0, False, ["XD", "STb"], [pyk])
                        self.mm(py[hh * 64:(hh + 1) * 64, :], SP[:, c, d, h * 64:(h + 1) * 64], ccs[:], False, d == 1, ["CS", "ccs"], [pyk])
                self.stt(Y[:, hp, :], XS[:, hp, cs:cs + 128], md[:, hp:hp + 1], py[:], ALU.mult, ALU.add, ["XS", "md", pyk], ["Yt"])
            self.ld(Z[:], self.proj[2048:2560, pcol:pcol + 128].rearrange("(j p) n -> p j n", p=128), ["proj"], ["Zt"])
            self.act(Z[:], Z[:], AF.Silu, ["Zt"], ["Zt"])
            self.tt(Y[:], Y[:], Z[:], ALU.mult, ["Yt", "Zt"], ["Yt"])
            self.tt(gsq[:], Y[:], Y[:], ALU.mult, ["Yt"], ["gsq"])
            for gg in range(2):
                self.mm(pms[:], onesn[:], gsq[:, 2 * gg, :], True, False, ["onesn", "gsq"], ["pms"])
                self.mm(pms[:], onesn[:], gsq[:, 2 * gg + 1, :], False, True, ["onesn", "gsq"], ["pms"])
                self.act(rsd[:], pms[:], AF.Sqrt, ["pms"], ["rsd"], bias=self.epsc[:, 0:1])
                self.vop("reciprocal", ["rsd"], ["rsd"], out=rsd[:], in_=rsd[:])
                for jj in range(2):
                    j = 2 * gg + jj
                    self.stt(self.mix[:, 12 + j, cs:cs + 128], Y[:, j, :], nw[:, j:j + 1], rsd[:], ALU.mult, ALU.mult,
                             ["Yt", "nw", "rsd"], ["mix"])
        P.barrier()


KB.mix_ssd = mix_ssd
def ln_apply(self, st, R, rkey, blocks, lnp, gi, bi, writer):
    onesD = self.sb(st, "onesD", [128, 128], BF16)
    self.vop("memset", [], ["onesD"], onesD[:], 1.0 / D)
    pm = self.ps(st, "lnpm", [128, 512]); pq = self.ps(st, "lnpq", [128, 512])
    sq = [self.sb(st, "lnsq", [128, 512], BF16) for _ in range(2)]
    rb = [self.sb(st, "lnrb", [128, 512], BF16) for _ in range(2)]
    mean = self.sb(st, "lnmean", [128, 512]); rstd = self.sb(st, "lnrstd", [128, 512]); tmp = self.sb(st, "lntmp", [128, 512])
    outs = [self.sb(st, "lnout", [128, 512]) for _ in range(2)]
    for (a, b) in blocks:
        n = b - a
        for k in range(16):
            r_ = rb[k % 2]; rk_ = "lnrb%d" % (k % 2)
            self.cp(r_[:, 0:n], R[:, k, a:b], [rkey], [rk_], eng="act")
            self.mm(pm[:, 0:n], onesD[:], r_[:, 0:n], k == 0, k == 15, ["onesD", rk_], ["lnpm"])
        for k in range(16):
            s_ = sq[k % 2]; sk = "lnsq%d" % (k % 2)
            self.act(s_[:, 0:n], R[:, k, a:b], AF.Square, [rkey], [sk])
            self.mm(pq[:, 0:n], onesD[:], s_[:, 0:n], k == 0, k == 15, ["onesD", sk], ["lnpq"])
        self.cp(mean[:, 0:n], pm[:, 0:n], ["lnpm"], ["lnmean"])
        self.tt(tmp[:, 0:n], mean[:, 0:n], mean[:, 0:n], ALU.mult, ["lnmean"], ["lntmp"])
        self.tt(tmp[:, 0:n], pq[:, 0:n], tmp[:, 0:n], ALU.subtract, ["lnpq", "lntmp"], ["lntmp"])
        self.act(rstd[:, 0:n], tmp[:, 0:n], AF.Sqrt, ["lntmp"], ["lnrstd"], bias=self.epsc[:, 0:1])
        self.vop("reciprocal", ["lnrstd"], ["lnrstd"], out=rstd[:, 0:n], in_=rstd[:, 0:n])
        for k in range(16):
            o = outs[k % 2]; ok = "lnout%d" % (k % 2)
            self.tt(o[:, 0:n], R[:, k, a:b], mean[:, 0:n], ALU.subtract, [rkey, "lnmean"], [ok])
            self.tt(o[:, 0:n], o[:, 0:n], rstd[:, 0:n], ALU.mult, [ok, "lnrstd"], [ok])
            self.ts(o[:, 0:n], o[:, 0:n], lnp[:, gi, k:k + 1], lnp[:, bi, k:k + 1], ALU.mult, ALU.add, [ok, "lnp"], [ok])
            writer(k, a, b, o[:, 0:n], ok)


def phase_outproj(self, l):
    P = self.P
    I = self.I
    need_ctx = l < NL - 1
    blocks = RB if need_ctx else RB[1:]
    with ExitStack() as st:
        lnp = self.sb(st, "lnp", [128, 4, 16])
        self.ld(lnp[:], I["ln_p"][l], [], ["lnp"])
        R = self.sb(st, "Rres", [128, 16, NR])
        with ExitStack() as st2:
            wv = self.W["w_out"].rearrange("(k p) n -> p k n", p=128)
            slabs = [self.sb(st2, "ow", [128, 16, 512], BF16) for _ in range(2)]
            hst = [self.sb(st2, "ohst", [128, NR]) for _ in range(2)]
            pss = [self.ps(st2, "ops", [128, 512]) for _ in range(4)]
            pi = 0
            for cb in range(4):
                sl = slabs[cb % 2]; sk = "ow%d" % (cb % 2)
                self.ld(sl[:], wv[:, :, cb * 512:(cb + 1) * 512], ["wf_w_out"], [sk])
                for mi in range(4):
                    m = cb * 4 + mi
                    hs = hst[m % 2]; hk = "ohst%d" % (m % 2)
                    if need_ctx:
                        self.ld(hs[:, 0:CTX], self.hbuf[:, m, 0:CTX], ["hbuf"], [hk])
                    self.ld(hs[:, CTX:NR], self.hbuf[:, m, C_LAT[0]:C_LAT[1]], ["hbuf"], [hk])
                    a0 = blocks[0][0]
                    self.act(hs[:, a0:NR], hs[:, a0:NR], AF.Copy, [hk], [hk], scale=ALPHA)
                    for (a, b) in blocks:
                        pt = pss[pi % 4]; pk = "ops%d" % (pi % 4)
                        pi += 1
                        for k in range(16):
                            self.mm(pt[:, 0:b - a], sl[:, k, mi * 128:(mi + 1) * 128], self.mix[:, k, a:b], k == 0, k == 15,
                                    [sk, "mix"], [pk])
                        col = 1 if a < CTX else 0
                        self.stt(R[:, m, a:b], pt[:, 0:b - a], self.mod[:, 32 + m, col:col + 1], hs[:, a:b], ALU.mult, ALU.add,
                                 [pk, "mod", hk], ["Rres"])
            P.barrier()
        with ExitStack() as st2:
            def writer(k, a, b, tile, key):
                self.ld(self.h1buf[:, k, a:b], tile, [key], ["h1buf"])
            self.ln_apply(st2, R, "Rres", blocks, lnp, 0, 1, writer)
            P.barrier()


def phase_ffn(self, l):
    P = self.P
    I = self.I
    fl = self.flags
    need_ctx = l < NL - 1
    h1 = self.h1buf
    with ExitStack() as st:
        self.exchange(st, h1[:, :, CTX:CTX + 64], h1[:, :, NR - 64:NR], h1[:, :, NR:NR + 64], h1[:, :, NR + 64:NR + 128],
                      64, self.cin2, self.cout2, ["h1buf"], ["h1buf"], "e2")
    NE = NR + 128
    LT0 = CTX + 64
    with ExitStack() as st:
        hm2 = self.sb(st, "hm2", [128, 16, NE], BF16)
        fcw = self.sb(st, "fcw", [128, NFT, 9]); fcb = self.sb(st, "fcb", [128, NFT])
        self.ld(fcw[:], I["fcw"][l], [], ["fcw"]); self.ld(fcb[:], I["fcb"][l], [], ["fcb"])
        with ExitStack() as st2:
            hst = [self.sb(st2, "fhst", [128, NE]) for _ in range(2)]
            for k in range(16):
                hs = hst[k % 2]; hk = "fhst%d" % (k % 2)
                if need_ctx:
                    self.ld(hs[:, 0:CTX], h1[:, k, 0:CTX], ["h1buf"], [hk])
                self.ld(hs[:, CTX:LT0], h1[:, k, NR:NR + 64], ["h1buf"], [hk])
                self.ld(hs[:, LT0:LT0 + LAT], h1[:, k, CTX:NR], ["h1buf"], [hk])
                self.ld(hs[:, LT0 + LAT:NE], h1[:, k, NR + 64:NR + 128], ["h1buf"], [hk])
                if need_ctx:
                    self.ts(hm2[:, k, 0:CTX], hs[:, 0:CTX], self.mod1p[:, 64 + k, 1:2], self.mod[:, 48 + k, 1:2], ALU.mult, ALU.add,
                            [hk, "mod", "mod1p"], ["hm2"])
                self.ts(hm2[:, k, CTX:NE], hs[:, CTX:NE], self.mod1p[:, 64 + k, 0:1], self.mod[:, 48 + k, 0:1], ALU.mult, ALU.add,
                        [hk, "mod", "mod1p"], ["hm2"])
            P.barrier()
        wv = self.W["ffn_w_up"].rearrange("(k p) n -> p k n", p=128)
        slabs = [self.sb(st, "uw", [128, 16, 256], BF16) for _ in range(2)]
        if need_ctx:
            ggrp = [(0, 512), (512, 1024), (1024, NE)]
            vgrp = [(0, 512), (512, 1024), (1024, LT0 + LAT)]
        else:
            ggrp = [(CTX, CTX + 384), (CTX + 384, CTX + 768), (CTX + 768, NE)]
            vgrp = [(LT0, LT0 + 512), (LT0 + 512, LT0 + LAT)]
        g0 = ggrp[0][0]
        v0 = vgrp[0][0]
        Gs = [self.ps(st, "fG", [128, 512]) for _ in ggrp]
        Vs = [self.ps(st, "fV", [128, 512]) for _ in vgrp]
        gsb = self.sb(st, "gsb", [128, NE]); cv = self.sb(st, "fcv", [128, NR]); gt = self.sb(st, "fgt", [128, NR])
        acts = [self.sb(st, "fact", [128, NR], BF16) for _ in range(2)]
        for f in range(NFT):
            sl = slabs[f % 2]; sk = "uw%d" % (f % 2)
            self.ld(sl[:, :, 0:128], wv[:, :, f * 128:(f + 1) * 128], ["wf_ffn_w_up"], [sk])
            self.ld(sl[:, :, 128:256], wv[:, :, FH + f * 128:FH + (f + 1) * 128], ["wf_ffn_w_up"], [sk])
            ab = acts[f % 2]; ak = "fact%d" % (f % 2)
            for gi, (a, b) in enumerate(ggrp):
                for k in range(16):
                    self.mm(Gs[gi][:, 0:b - a], sl[:, k, 0:128], hm2[:, k, a:b], k == 0, k == 15, [sk, "hm2"], ["fG%d" % gi])
            for vi, (a, b) in enumerate(vgrp):
                for k in range(16):
                    self.mm(Vs[vi][:, 0:b - a], sl[:, k, 128:256], hm2[:, k, a:b], k == 0, k == 15, [sk, "hm2"], ["fV%d" % vi])
            for gi, (a, b) in enumerate(ggrp):
                self.cp(gsb[:, a:b], Gs[gi][:, 0:b - a], ["fG%d" % gi], ["gsb"], eng="act")
            self.ts(gsb[:, CTX:LT0], gsb[:, CTX:LT0], fl[:, 32:33], None, ALU.mult, None, ["gsb", "flags"], ["gsb"])
            self.ts(gsb[:, LT0 + LAT:NE], gsb[:, LT0 + LAT:NE], fl[:, 33:34], None, ALU.mult, None, ["gsb", "flags"], ["gsb"])
            g3 = gsb[:, CTX:NE].rearrange("p (r c) -> p r c", c=64)
            o3 = cv[:, CTX:NR].rearrange("p (r c) -> p r c", c=64)
            self.ts(o3, g3[:, 1:17, :], fcw[:, f, 4:5], None, ALU.mult, None, ["gsb", "fcw"], ["fcv"])
            for dy in range(3):
                for dx in range(3):
                    if dy == 1 and dx == 1:
                        continue
                    if dx == 0:
                        src, dst = g3[:, dy:dy + 16, 0:63], o3[:, :, 1:64]
                    elif dx == 1:
                        src, dst = g3[:, dy:dy + 16, :], o3
                    else:
                        src, dst = g3[:, dy:dy + 16, 1:64], o3[:, :, 0:63]
                    w = fcw[:, f, dy * 3 + dx:dy * 3 + dx + 1]
                    self.stt(dst, src, w, dst, ALU.mult, ALU.add, ["gsb", "fcw", "fcv"], ["fcv"])
            c0 = 0 if need_ctx else CTX
            if need_ctx:
                self.ts(cv[:, 0:CTX], gsb[:, 0:CTX], fcw[:, f, 4:5], None, ALU.mult, None, ["gsb", "fcw"], ["fcv"])
                self.stt(cv[:, 1:CTX], gsb[:, 0:CTX - 1], fcw[:, f, 3:4], cv[:, 1:CTX], ALU.mult, ALU.add, ["gsb", "fcw", "fcv"], ["fcv"])
                self.stt(cv[:, 0:CTX - 1], gsb[:, 1:CTX], fcw[:, f, 5:6], cv[:, 0:CTX - 1], ALU.mult, ALU.add, ["gsb", "fcw", "fcv"], ["fcv"])
            self.ts(cv[:, c0:NR], cv[:, c0:NR], fcb[:, f:f + 1], None, ALU.add, None, ["fcv", "fcb"], ["fcv"])
            self.gelu(cv[:, c0:NR], cv[:, c0:NR], gt[:, c0:NR], ["fcv"], ["fcv"], "fgt")
            for vi, (a, b) in enumerate(vgrp):
                pieces = []
                if a < CTX:
                    pieces.append((a, min(b, CTX), a))
                la, lb = max(a, LT0), b
                if lb > la:
                    pieces.append((la, lb, la - 64))
                for (pa, pb, ra) in pieces:
                    self.tt(ab[:, ra:ra + pb - pa], cv[:, ra:ra + pb - pa], Vs[vi][:, pa - a:pb - a], ALU.mult, ["fcv", "fV%d" % vi], [ak])
            self.ld(self.actbuf[f, :, c0:NR], ab[:, c0:NR], [ak], ["actbuf"])
        P.barrier()
    self.prep_weight("ffn_w_up", l + 1)
    blocks = RB if need_ctx else RB[1:]
    a0 = blocks[0][0]
    with ExitStack() as st:
        A = self.sb(st, "Aact", [128, NFT, NR], BF16)
        for f in range(NFT):
            self.ld(A[:, f, a0:NR], self.actbuf[f, :, a0:NR], ["actbuf"], ["Aact"], q=("sp" if f % 2 == 0 else "act"))
        wv = self.W["ffn_w_down"].rearrange("(f p) n -> p f n", p=128)
        slabs = [self.sb(st, "dw", [128, NFT, 128], BF16) for _ in range(2)]
        hst = [self.sb(st, "dhst", [128, NR]) for _ in range(2)]
        pss = [self.ps(st, "dps", [128, 512]) for _ in range(4)]
        pi = 0
        for m in range(16):
            sl = slabs[m % 2]; sk = "dw%d" % (m % 2)
            self.ld(sl[:], wv[:, :, m * 128:(m + 1) * 128], ["wf_ffn_w_down"], [sk])
            hs = hst[m % 2]; hk = "dhst%d" % (m % 2)
            self.ld(hs[:, a0:NR], h1buf_real(self, m, a0), ["h1buf"], [hk])
            self.act(hs[:, a0:NR], hs[:, a0:NR], AF.Copy, [hk], [hk], scale=ALPHA)
            for (a, b) in blocks:
                pt = pss[pi % 4]; pk = "dps%d" % (pi % 4)
                pi += 1
                for f in range(NFT):
                    self.mm(pt[:, 0:b - a], sl[:, f, :], A[:, f, a:b], f == 0, f == NFT - 1, [sk, "Aact"], [pk])
                col = 1 if a < CTX else 0
                self.stt(hs[:, a:b], pt[:, 0:b - a], self.mod[:, 80 + m, col:col + 1], hs[:, a:b], ALU.mult, ALU.add,
                         [pk, "mod", hk], [hk])
            self.ld(self.r2buf[:, m, a0:NR], hs[:, a0:NR], [hk], ["r2buf"])
        P.barrier()
    self.prep_weight("ffn_w_down", l + 1)
    with ExitStack() as st:
        lnp = self.sb(st, "lnp2", [128, 4, 16])
        self.ld(lnp[:], I["ln_p"][l], [], ["lnp"])
        R = self.sb(st, "R2", [128, 16, NR])
        for k in range(16):
            self.ld(R[:, k, a0:NR], self.r2buf[:, k, a0:NR], ["r2buf"], ["R2"], q=("sp" if k % 2 == 0 else "act"))

        def writer(k, a, b, tile, key):
            if a < CTX:
                self.ld(self.hbuf[:, k, a:b], tile, [key], ["hbuf"])
            else:
                self.ld(self.hbuf[:, k, C_LAT[0] + a - CTX:C_LAT[0] + b - CTX], tile, [key], ["hbuf"])
        self.ln_apply(st, R, "R2", blocks, lnp, 2, 3, writer)
        P.barrier()


def h1buf_real(self, m, a0):
    return self.h1buf[:, m, a0:NR]


KB.ln_apply = ln_apply
KB.phase_outproj = phase_outproj
KB.phase_ffn = phase_ffn
def build(com, per0, nlayers=NL, dbg=(), stop=None):
    stop2 = None
    if stop and stop.startswith("L2"):
        stop2 = stop[2:]
        stop = None
    kb = KB(com, per0, nlayers, dbg)
    P = kb.P
    with ExitStack() as st:
        kb.phase_setup(st)
        for l in range(nlayers):
            if stop == "setup":
                kb.dump("hbuf", kb.hbuf, [128, 16, NT], "hbuf")
                break
            def s2(name):
                return l == 1 and stop2 == name
            if l == 0:
                kb.prep_weight("w_in", 0)
            kb.phase_e0()
            if s2("e0"):
                break
            if stop == "e0":
                kb.dump("hbuf", kb.hbuf, [128, 16, NT], "hbuf")
                break
            kb.phase_ada(l)
            if stop == "ada":
                break
            if l == 0:
                kb.prep_weight("w_out", 0)
            kb.phase_inproj(l)
            kb.prep_weight("w_in", l + 1)
            if l == 0:
                kb.prep_weight("ffn_w_up", 0)
                kb.prep_weight("ffn_w_down", 0)
            if s2("inproj"):
                break
            if l == 0:
                kb.dump("proj0", kb.proj[0:INW], [INW, NT], "proj")
            if stop == "inproj":
                break
            with ExitStack() as stl:
                kb.mix = kb.sb(stl, "mix", [128, 16, NR], BF16)
                kb.phase_mixers(l)
                if stop == "mix":
                    break
                if s2("mix"):
                    break
                kb.phase_outproj(l)
                kb.prep_weight("w_out", l + 1)
                if s2("ln1"):
                    break
            if stop == "ln1":
                kb.dump("h1", kb.h1buf[:, :, 0:NR], [128, 16, NR], "h1buf")
                break
            kb.phase_ffn(l)
            if stop == "ffn":
                kb.dump("h2", kb.hbuf, [128, 16, NT], "hbuf")
                break
        if stop is None:
            kb.ld(kb.out, kb.hbuf[:, :, C_LAT[0]:C_LAT[1]], ["hbuf"], ["out"])
        P.emit()
    return kb


_CACHE = {}


def run(inputs, nlayers=NL, dbg=(), stop=None, trace=False):
    com, per = host_prep({k: np.asarray(v) for k, v in inputs.items()})
    kb = build(com, per[0], nlayers, dbg, stop)
    print("ninstr", kb.P.ninstr(), flush=True)
    in_maps = [dict(com, **per[c]) for c in range(8)]
    if trace:
        res = run_bass_kernel_spmd(kb.nc, in_maps, core_ids=list(range(8)), trace=True)
        print("EXEC_NS", res.exec_time_ns, flush=True)
    else:
        res = run_bass_kernel_spmd(kb.nc, in_maps, core_ids=list(range(8)))
    return res.results


def kernel(**inputs):
    r = run(inputs)
    out = np.zeros((2, 4096, D), np.float32)
    for c in range(8):
        b, k = divmod(c, 4)
        o = r[c]["out"]
        out[b, k * LAT:(k + 1) * LAT, :] = o.transpose(2, 1, 0).reshape(LAT, D)
    return out
```
